# Optimizing a Trainium2 kernel written in Bass

```python
import math
import jax, jax.numpy as jnp
from jax import lax
import numpy as np

D_MODEL = 2048
BATCH = 32
SEQ = 256
DEPTH = 4
DEC_BATCH = 2
DEC_SEQ = 1024
PAST_LEN = 256

GRID_W = 64
QBLK = 128
ROPE_BASE = 10000.0
LN_EPS = 1e-5
RMS_EPS = 1e-6
ALPHA = (2 * DEPTH) ** 0.25
BETA = (8 * DEPTH) ** -0.25

MLA_HEADS = 4
MLA_NOPE = 128
MLA_ROPE = 64
MLA_V = 128
MLA_Q_LORA = 384
MLA_KV_LORA = 256
GQA_HEADS = 8
GQA_KV_HEADS = 2
GQA_GROUP = GQA_HEADS // GQA_KV_HEADS
GQA_HEAD_DIM = 64
WINDOW = 128
WIN_BLK = 128
POOL_WINDOWS = (2, 4, 8, 16)
POOL_GROUP_CH = 128
NA_HEADS = 8
NA_HEAD_DIM = 64
NA_KH = 8
NA_KW = 16
A_IN = MLA_Q_LORA + MLA_KV_LORA + MLA_ROPE
B_IN = (GQA_HEADS + 2 * GQA_KV_HEADS) * GQA_HEAD_DIM
C_IN = len(POOL_WINDOWS) * POOL_GROUP_CH
D_IN = 3 * NA_HEADS * NA_HEAD_DIM
IN_COLS = A_IN + B_IN + C_IN + D_IN
MIX_WIDTH = MLA_HEADS * MLA_V + GQA_HEADS * GQA_HEAD_DIM + C_IN + NA_HEADS * NA_HEAD_DIM
PEER_HEADS = 8
PEER_NKEYS = 128
PEER_EXPERTS = PEER_NKEYS * PEER_NKEYS
PEER_TOPK = 16
PEER_DKEY = 256
PEER_BLK = 128

kernel_name = 'hybrid_diffusion_prefix_trunk_step'


def rms_norm(x, g):
    xf = x.astype(jnp.float32)
    y = xf * lax.rsqrt(jnp.mean(xf * xf, -1, keepdims=True) + RMS_EPS)
    return (y * g).astype(x.dtype)


def layer_norm(x, g, b):
    xf = x.astype(jnp.float32)
    mu = jnp.mean(xf, -1, keepdims=True)
    var = jnp.mean(jnp.square(xf - mu), -1, keepdims=True)
    return ((xf - mu) * lax.rsqrt(var + LN_EPS) * g + b).astype(x.dtype)


def adaln(cond, w_ada, b_ada):
    m = jax.nn.silu(cond) @ w_ada + b_ada
    return jnp.split(m[:, None, :], 6, axis=-1)


def _rotate(xh, ang):
    qd = ang.shape[-1]
    cos = jnp.cos(ang).astype(xh.dtype)
    sin = jnp.sin(ang).astype(xh.dtype)
    x1, x2 = xh[..., :qd], xh[..., qd:]
    return jnp.concatenate([x1 * cos - x2 * sin, x2 * cos + x1 * sin], -1)


def axial_rope(x):
    n, r = x.shape[1], x.shape[-1]
    qd = r // 4
    t = jnp.arange(n)
    inv = ROPE_BASE ** (-jnp.arange(qd, dtype=jnp.float32) / qd)
    ang_row = (t // GRID_W).astype(jnp.float32)[:, None, None] * inv
    ang_col = (t % GRID_W).astype(jnp.float32)[:, None, None] * inv
    half = r // 2
    return jnp.concatenate([_rotate(x[..., :half], ang_row), _rotate(x[..., half:], ang_col)], -1)


def _softmax(s, sink=None):
    if sink is None:
        return jax.nn.softmax(s, axis=-1)
    m = jnp.maximum(jnp.max(s, -1, keepdims=True), sink)
    e = jnp.exp(s - m)
    return e / (jnp.sum(e, -1, keepdims=True) + jnp.exp(sink - m))


def dense_attention(q, k, v, scale, sink=None):
    B, S, Hk, G, dk = q.shape
    nb = S // QBLK
    qb = jnp.moveaxis(q.reshape(B, nb, QBLK, Hk, G, dk), 1, 0)
    sk = None if sink is None else sink.astype(jnp.float32)[None, :, :, None, None]

    def block(qblk):
        s = jnp.einsum('bqhgd,bkhd->bhgqk', qblk, k).astype(jnp.float32) * scale
        pr = _softmax(s, sk)
        return jnp.einsum('bhgqk,bkhd->bqhgd', pr.astype(v.dtype), v)

    o = lax.map(block, qb)
    return jnp.moveaxis(o, 0, 1).reshape(B, S, Hk, G, v.shape[-1])


def mla_attend(q, ckv, krope, w_uk, w_uv):
    B, K, _ = ckv.shape
    k_nope = (ckv @ w_uk).reshape(B, K, MLA_HEADS, MLA_NOPE)
    v = (ckv @ w_uv).reshape(B, K, MLA_HEADS, MLA_V)
    k = jnp.concatenate([k_nope, jnp.broadcast_to(krope[:, :, None, :], (B, K, MLA_HEADS, MLA_ROPE))], -1)
    o = dense_attention(q[:, :, :, None, :], k, v, (MLA_NOPE + MLA_ROPE) ** -0.5)
    return o.reshape(B, q.shape[1], MLA_HEADS * MLA_V)


def window_attention(q, k, v, ck, cv, sink, scale):
    B, N = q.shape[:2]
    nb = N // WIN_BLK
    qb = q.reshape(B, nb, WIN_BLK, *q.shape[2:])

    def band(t):
        tp = jnp.pad(t, ((0, 0), (WIN_BLK, WIN_BLK), (0, 0), (0, 0)))
        tp = tp.reshape(B, nb + 2, WIN_BLK, *t.shape[2:])
        return jnp.concatenate([tp[:, :nb], tp[:, 1:nb + 1], tp[:, 2:]], axis=2)

    kb, vb = band(k), band(v)
    s_loc = jnp.einsum('bnqhgd,bnkhd->bnhgqk', qb, kb).astype(jnp.float32) * scale
    s_ctx = jnp.einsum('bnqhgd,bchd->bnhgqc', qb, ck).astype(jnp.float32) * scale
    blk = jnp.arange(nb)[:, None]
    qpos = blk * WIN_BLK + jnp.arange(WIN_BLK)
    kpos = (blk - 1) * WIN_BLK + jnp.arange(3 * WIN_BLK)
    rel = kpos[:, None, :] - qpos[:, :, None]
    valid = (jnp.abs(rel) <= WINDOW) & (kpos[:, None, :] >= 0) & (kpos[:, None, :] < N)
    s_loc = jnp.where(valid[None, :, None, None], s_loc, -jnp.inf)
    sk = sink.astype(jnp.float32)[None, None, :, :, None, None]
    p = _softmax(jnp.concatenate([s_ctx, s_loc], -1), sk)
    L = ck.shape[1]
    o = (jnp.einsum('bnhgqc,bchd->bnqhgd', p[..., :L].astype(cv.dtype), cv)
         + jnp.einsum('bnhgqk,bnkhd->bnqhgd', p[..., L:].astype(vb.dtype), vb))
    return o.reshape(B, N, -1)


def neighborhood_attention(q, k, v, ck, cv, rpb, scale):
    B, N, H, d = q.shape
    R = N // GRID_W
    kh = min(NA_KH, R)
    rows = jnp.arange(R)
    rstart = jnp.clip(rows - kh // 2, 0, R - kh)
    ridx = rstart[:, None] + jnp.arange(kh)
    cols = jnp.arange(GRID_W)
    cstart = jnp.clip(cols - NA_KW // 2, 0, GRID_W - NA_KW)
    col_ok = (cols[None, :] >= cstart[:, None]) & (cols[None, :] < cstart[:, None] + NA_KW)
    qg = q.reshape(B, R, GRID_W, H, d)
    kg = k.reshape(B, R, GRID_W, H, d)[:, ridx]
    vg = v.reshape(B, R, GRID_W, H, d)[:, ridx]
    s_loc = jnp.einsum('brqhd,brkwhd->bhrqkw', qg, kg).astype(jnp.float32) * scale
    dr = ridx - rows[:, None] + NA_KH - 1
    dc = jnp.clip(cols[None, :] - cols[:, None] + NA_KW - 1, 0, 2 * NA_KW - 2)
    bias = rpb[:, dr[:, None, :, None], dc[None, :, None, :]].astype(jnp.float32)
    s_loc = jnp.where(col_ok[None, None, None, :, None, :], s_loc + bias[None], -jnp.inf)
    s_loc = s_loc.reshape(B, H, R, GRID_W, kh * GRID_W)
    s_ctx = jnp.einsum('brqhd,bchd->bhrqc', qg, ck).astype(jnp.float32) * scale
    p = _softmax(jnp.concatenate([s_ctx, s_loc], -1))
    L = ck.shape[1]
    p_loc = p[..., L:].reshape(B, H, R, GRID_W, kh, GRID_W)
    o = (jnp.einsum('bhrqc,bchd->brqhd', p[..., :L].astype(cv.dtype), cv)
         + jnp.einsum('bhrqkw,brkwhd->brqhd', p_loc.astype(vg.dtype), vg))
    return o.reshape(B, N, H * d)


def multiscale_pool(h, w_pool, pool_scale):
    B, S, _ = h.shape
    ng = len(POOL_WINDOWS)
    hg = h.reshape(B, S, ng, POOL_GROUP_CH).astype(jnp.float32)
    cs = jnp.concatenate([jnp.zeros((B, 1, ng, POOL_GROUP_CH), jnp.float32), jnp.cumsum(hg, axis=1)], axis=1)
    win = jnp.array(POOL_WINDOWS, jnp.int32)
    t = jnp.arange(S)[:, None]
    lo = jnp.clip(t - win // 2, 0, S)
    hi = jnp.clip(t + win // 2, 0, S)
    g = jnp.arange(ng)[None, :]
    mean = (cs[:, hi, g] - cs[:, lo, g]) / (hi - lo).astype(jnp.float32)[..., None]
    dlt = (mean - hg).astype(h.dtype)
    y = jnp.einsum('bsgc,gce->bsge', dlt, w_pool).reshape(B, S, C_IN)
    return y * pool_scale


def peer(h, w_pq, sub_keys, u_tab, v_tab):
    B, S, D = h.shape
    T = B * S
    xt = h.reshape(T, D)
    q = (xt @ w_pq).reshape(T, PEER_HEADS, 2, PEER_DKEY // 2)
    s = jnp.einsum('thpd,hpnd->thpn', q, sub_keys).astype(jnp.float32)
    v_top, i_top = lax.top_k(s, PEER_TOPK)
    cand = v_top[:, :, 0, :, None] + v_top[:, :, 1, None, :]
    sc, ci = lax.top_k(cand.reshape(T, PEER_HEADS, PEER_TOPK * PEER_TOPK), PEER_TOPK)
    e = (jnp.take_along_axis(i_top[:, :, 0], ci // PEER_TOPK, -1) * PEER_NKEYS
         + jnp.take_along_axis(i_top[:, :, 1], ci % PEER_TOPK, -1))
    gw = jax.nn.softmax(sc, axis=-1)
    nblk = T // PEER_BLK
    e = e.reshape(nblk, PEER_BLK, PEER_HEADS * PEER_TOPK)
    gw = gw.reshape(nblk, PEER_BLK, PEER_HEADS * PEER_TOPK)

    def block(args):
        xb, eb, gb = args
        a = jax.nn.gelu(jnp.einsum('td,tkd->tk', xb, u_tab[eb]).astype(jnp.float32))
        return jnp.einsum('tk,tkd->td', (gb * a).astype(xb.dtype), v_tab[eb])

    out = lax.map(block, (xt.reshape(nblk, PEER_BLK, D), e, gw))
    return out.reshape(B, S, D)


def mixer_inputs(h, p):
    B, S, _ = h.shape
    z = h @ p['w_in']
    za, zb, zc, zd = jnp.split(z, [A_IN, A_IN + B_IN, A_IN + B_IN + C_IN], axis=-1)
    cq = rms_norm(za[..., :MLA_Q_LORA], p['g_q'])
    q_mla = (cq @ p['w_uq']).reshape(B, S, MLA_HEADS, MLA_NOPE + MLA_ROPE)
    ckv = rms_norm(za[..., MLA_Q_LORA:MLA_Q_LORA + MLA_KV_LORA], p['g_kv'])
    krope = za[..., MLA_Q_LORA + MLA_KV_LORA:]
    nqc = GQA_HEADS * GQA_HEAD_DIM
    nkc = GQA_KV_HEADS * GQA_HEAD_DIM
    gq = zb[..., :nqc].reshape(B, S, GQA_HEADS, GQA_HEAD_DIM)
    gk = zb[..., nqc:nqc + nkc].reshape(B, S, GQA_KV_HEADS, GQA_HEAD_DIM)
    gv = zb[..., nqc + nkc:].reshape(B, S, GQA_KV_HEADS, GQA_HEAD_DIM)
    nq, nk, nv = [t.reshape(B, S, NA_HEADS, NA_HEAD_DIM) for t in jnp.split(zd, 3, axis=-1)]
    return q_mla, ckv, krope, gq, gk, gv, zc, nq, nk, nv


def context_mix(h, p):
    B, S, _ = h.shape
    q_mla, ckv, krope, gq, gk, gv, zc, nq, nk, nv = mixer_inputs(h, p)
    o_a = mla_attend(q_mla, ckv, krope, p['w_uk'], p['w_uv'])
    o_b = dense_attention(gq.reshape(B, S, GQA_KV_HEADS, GQA_GROUP, GQA_HEAD_DIM), gk, gv,
                          GQA_HEAD_DIM ** -0.5,
                          p['gqa_sink'].reshape(GQA_KV_HEADS, GQA_GROUP)).reshape(B, S, -1)
    o_c = multiscale_pool(zc, p['w_pool'], p['pool_scale'])
    o_d = dense_attention(nq[:, :, :, None], nk, nv, NA_HEAD_DIM ** -0.5).reshape(B, S, -1)
    o = jnp.concatenate([o_a, o_b, o_c, o_d], -1) @ p['w_out']
    return o, (ckv, krope, gk, gv, nk, nv)


def latent_mix(h, p, c_ckv, c_krope, c_gk, c_gv, c_nk, c_nv):
    B, N, _ = h.shape
    q_mla, ckv, krope, gq, gk, gv, zc, nq, nk, nv = mixer_inputs(h, p)
    q_mla = jnp.concatenate([q_mla[..., :MLA_NOPE], axial_rope(q_mla[..., MLA_NOPE:])], -1)
    krope = axial_rope(krope[:, :, None, :])[:, :, 0]
    o_a = mla_attend(q_mla, jnp.concatenate([c_ckv, ckv], 1), jnp.concatenate([c_krope, krope], 1),
                     p['w_uk'], p['w_uv'])
    gq = axial_rope(gq).reshape(B, N, GQA_KV_HEADS, GQA_GROUP, GQA_HEAD_DIM)
    o_b = window_attention(gq, axial_rope(gk), gv, c_gk, c_gv,
                           p['gqa_sink'].reshape(GQA_KV_HEADS, GQA_GROUP), GQA_HEAD_DIM ** -0.5)
    o_c = multiscale_pool(zc, p['w_pool'], p['pool_scale'])
    o_d = neighborhood_attention(nq, nk, nv, c_nk, c_nv, p['na_rpb'], NA_HEAD_DIM ** -0.5)
    return jnp.concatenate([o_a, o_b, o_c, o_d], -1) @ p['w_out']


def channel_sublayer(x, sh, sc, gate, p):
    y = peer(x * (1 + sc) + sh, p['peer_wq'], p['peer_keys'], p['peer_u'], p['peer_v'])
    return layer_norm(ALPHA * x + gate * y, p['ln2_g'], p['ln2_b'])


def setup_inputs(seed: int = 0) -> dict:
    key = jax.random.key(seed)
    ks = iter(jax.random.split(key, 48))

    def nrm(shape, scale=1.0):
        return scale * jax.random.normal(next(ks), shape, jnp.float32)

    D = D_MODEL
    return {
        'x_prompt': nrm((BATCH, SEQ, D)),
        'x_sample': nrm((DEC_BATCH, DEC_SEQ, D)),
        'cache_mla_ckv': nrm((DEC_BATCH, DEPTH, PAST_LEN, MLA_KV_LORA)),
        'cache_mla_krope': nrm((DEC_BATCH, DEPTH, PAST_LEN, MLA_ROPE)),
        'cache_gqa_k': nrm((DEC_BATCH, DEPTH, PAST_LEN, GQA_KV_HEADS, GQA_HEAD_DIM)),
        'cache_gqa_v': nrm((DEC_BATCH, DEPTH, PAST_LEN, GQA_KV_HEADS, GQA_HEAD_DIM)),
        'cache_na_k': nrm((DEC_BATCH, DEPTH, PAST_LEN, NA_HEADS, NA_HEAD_DIM)),
        'cache_na_v': nrm((DEC_BATCH, DEPTH, PAST_LEN, NA_HEADS, NA_HEAD_DIM)),
        'c': nrm((DEC_BATCH, D)),
        'c_ctx': nrm((D,)),
        'w_ada': nrm((DEPTH, D, 6 * D), D ** -0.5),
        'b_ada': nrm((DEPTH, 6 * D), 0.02),
        'w_in': nrm((DEPTH, D, IN_COLS), D ** -0.5),
        'g_q': 1.0 + nrm((DEPTH, MLA_Q_LORA), 0.02),
        'g_kv': 1.0 + nrm((DEPTH, MLA_KV_LORA), 0.02),
        'w_uq': nrm((DEPTH, MLA_Q_LORA, MLA_HEADS * (MLA_NOPE + MLA_ROPE)), MLA_Q_LORA ** -0.5),
        'w_uk': nrm((DEPTH, MLA_KV_LORA, MLA_HEADS * MLA_NOPE), MLA_KV_LORA ** -0.5),
        'w_uv': nrm((DEPTH, MLA_KV_LORA, MLA_HEADS * MLA_V), MLA_KV_LORA ** -0.5),
        'gqa_sink': nrm((DEPTH, GQA_HEADS)),
        'w_pool': nrm((DEPTH, len(POOL_WINDOWS), POOL_GROUP_CH, POOL_GROUP_CH), POOL_GROUP_CH ** -0.5),
        'pool_scale': 1.0 + nrm((DEPTH, C_IN), 0.1),
        'na_rpb': nrm((DEPTH, NA_HEADS, 2 * NA_KH - 1, 2 * NA_KW - 1), 0.1),
        'w_out': nrm((DEPTH, MIX_WIDTH, D), BETA * MIX_WIDTH ** -0.5),
        'ln1_g': 1.0 + nrm((DEPTH, D), 0.02),
        'ln1_b': nrm((DEPTH, D), 0.02),
        'peer_wq': nrm((DEPTH, D, PEER_HEADS * PEER_DKEY), D ** -0.5),
        'peer_keys': nrm((DEPTH, PEER_HEADS, 2, PEER_NKEYS, PEER_DKEY // 2), (PEER_DKEY // 2) ** -0.5),
        'peer_u': nrm((DEPTH, PEER_EXPERTS, D), D ** -0.5),
        'peer_v': nrm((DEPTH, PEER_EXPERTS, D), BETA * PEER_HEADS ** -0.5),
        'ln2_g': 1.0 + nrm((DEPTH, D), 0.02),
        'ln2_b': nrm((DEPTH, D), 0.02),
    }


def reference(x_prompt, x_sample, cache_mla_ckv, cache_mla_krope, cache_gqa_k, cache_gqa_v,
              cache_na_k, cache_na_v, c, c_ctx, w_ada, b_ada, w_in, g_q, g_kv, w_uq, w_uk, w_uv,
              gqa_sink, w_pool, pool_scale, na_rpb, w_out, ln1_g, ln1_b, peer_wq, peer_keys,
              peer_u, peer_v, ln2_g, ln2_b):
    xp, xs = x_prompt, x_sample
    ckv_l, krope_l, gk_l, gv_l, nk_l, nv_l = [], [], [], [], [], []
    for l in range(DEPTH):
        p = {
            'w_in': w_in[l], 'g_q': g_q[l], 'g_kv': g_kv[l], 'w_uq': w_uq[l], 'w_uk': w_uk[l],
            'w_uv': w_uv[l], 'gqa_sink': gqa_sink[l], 'w_pool': w_pool[l], 'pool_scale': pool_scale[l],
            'na_rpb': na_rpb[l], 'w_out': w_out[l], 'peer_wq': peer_wq[l], 'peer_keys': peer_keys[l],
            'peer_u': peer_u[l], 'peer_v': peer_v[l], 'ln2_g': ln2_g[l], 'ln2_b': ln2_b[l],
        }
        sh1, sc1, g1, sh2, sc2, g2 = adaln(c_ctx[None, :], w_ada[l], b_ada[l])
        o, (ckv, krope, gk, gv, nk, nv) = context_mix(xp * (1 + sc1) + sh1, p)
        xp = layer_norm(ALPHA * xp + g1 * o, ln1_g[l], ln1_b[l])
        xp = channel_sublayer(xp, sh2, sc2, g2, p)
        ckv_l.append(ckv); krope_l.append(krope); gk_l.append(gk)
        gv_l.append(gv); nk_l.append(nk); nv_l.append(nv)
        sh1, sc1, g1, sh2, sc2, g2 = adaln(c, w_ada[l], b_ada[l])
        o = latent_mix(xs * (1 + sc1) + sh1, p, cache_mla_ckv[:, l], cache_mla_krope[:, l],
                       cache_gqa_k[:, l], cache_gqa_v[:, l], cache_na_k[:, l], cache_na_v[:, l])
        xs = layer_norm(ALPHA * xs + g1 * o, ln1_g[l], ln1_b[l])
        xs = channel_sublayer(xs, sh2, sc2, g2, p)
    new_mla_ckv = jnp.stack(ckv_l, axis=1)
    new_mla_krope = jnp.stack(krope_l, axis=1)
    new_gqa_k = jnp.stack(gk_l, axis=1)
    new_gqa_v = jnp.stack(gv_l, axis=1)
    new_na_k = jnp.stack(nk_l, axis=1)
    new_na_v = jnp.stack(nv_l, axis=1)
    return (xp, xs, new_mla_ckv, new_mla_krope, new_gqa_k, new_gqa_v, new_na_k, new_na_v)
```

```python
import contextlib
import math

import numpy as np
import concourse.bass as bass
import concourse.mybir as mybir
from concourse.bass_utils import run_bass_kernel_spmd

F32 = mybir.dt.float32
BF = mybir.dt.bfloat16
I32 = mybir.dt.int32
U32 = mybir.dt.uint32
AF = mybir.ActivationFunctionType
ALU = mybir.AluOpType

D = 2048
DEPTH = 4
NCORES = 8
SEQ = 256
NSEQ = 4
DSEQ = 1024
PAST = 256
GRID_W = 64
ALPHA = (2 * DEPTH) ** 0.25
LN_EPS = 1e-5
RMS_EPS = 1e-6
IN_COLS = 3520
NEG = -1.0e30

SEM_LIMIT = 60000
N_DMA_SEMS = 24


class _Sem:
    def __init__(self, tr, name):
        self.tr = tr
        self.name = name
        self.gen = 0
        self.count = 0
        self.handle = tr._new_sem(f"{name}_0")

    def bump(self, inc):
        if self.count + inc > SEM_LIMIT:
            self.gen += 1
            self.count = 0
            self.handle = self.tr._new_sem(f"{self.name}_{self.gen}")
        self.count += inc
        return (self.handle, self.count)


class TR:
    ENG = ("pe", "dve", "act", "pool", "sp")

    def __init__(self, nc, es):
        self.nc = nc
        self.es = es
        self.nsem = 0
        self.eng = {"pe": nc.tensor, "dve": nc.vector, "act": nc.scalar, "pool": nc.gpsimd, "sp": nc.sync}
        self.esem = {e: _Sem(self, e) for e in self.ENG}
        self.dsem = [_Sem(self, f"d{i}") for i in range(N_DMA_SEMS)]
        self.dsem_sw = [_Sem(self, f"w{i}") for i in range(N_DMA_SEMS)]
        self.dnext = 0
        self.dnext_sw = 0
        self.seen = {e: {} for e in self.ENG}
        self.handles = {}
        self.W = {}
        self.R = {}
        self.excl = set()
        self.part = {}
        self.ninstr = 0

    def _new_sem(self, name):
        self.nsem += 1
        return self.es.enter_context(self.nc.semaphore(name))

    def keys(self, x):
        if isinstance(x, (str, tuple)):
            return [x]
        if not hasattr(x, "tensor"):
            name = x.name
            pt = self.part.get(name)
            if pt is None:
                return [name]
            return [(name, i) for i in range((pt[0] + pt[1] - 1) // pt[1])]
        name = x.tensor.name
        pt = self.part.get(name)
        if pt is None:
            return [name]
        period, block = pt
        off = int(x.offset) % period
        n = int(x.ap[-1][1]) if int(x.ap[-1][0]) == 1 else 1
        return [(name, i) for i in range(off // block, (off + n - 1) // block + 1)]

    def _deps(self, reads, writes):
        deps = {}

        def add(d):
            for hid, (h, v) in d.items():
                if hid not in deps or deps[hid][1] < v:
                    deps[hid] = (h, v)

        for r in reads:
            for k in self.keys(r):
                add(self.W.get(k, {}))
                if k[0] in self.excl if isinstance(k, tuple) else k in self.excl:
                    add(self.R.get(k, {}))
        for w in writes:
            for k in self.keys(w):
                add(self.W.get(k, {}))
                add(self.R.get(k, {}))
        return deps

    def _wait(self, e, deps):
        eng = self.eng[e]
        own = id(self.esem[e].handle)
        for hid, (h, v) in deps.items():
            if e == "pe" and hid == own:
                continue
            if self.seen[e].get(hid, 0) >= v:
                continue
            eng.wait_ge(h, v)
            self.ninstr += 1
            self.seen[e][hid] = v

    def _record(self, reads, writes, h, v):
        hid = id(h)
        for r in reads:
            for k in self.keys(r):
                ex = (k[0] in self.excl) if isinstance(k, tuple) else (k in self.excl)
                if ex:
                    self.W.setdefault(k, {})[hid] = (h, v)
                else:
                    self.R.setdefault(k, {})[hid] = (h, v)
        for w in writes:
            for k in self.keys(w):
                self.W.setdefault(k, {})[hid] = (h, v)

    def op(self, e, fn, reads=(), writes=()):
        self._wait(e, self._deps(reads, writes))
        ins = fn(self.eng[e])
        h, v = self.esem[e].bump(1)
        ins.then_inc(h, 1)
        self.ninstr += 1
        self._record(reads, writes, h, v)
        return ins

    def dma(self, q, fn, reads=(), writes=()):
        deps = self._deps(reads, writes)
        if q == "pool":
            s = self.dsem_sw[self.dnext_sw % N_DMA_SEMS]
            self.dnext_sw += 1
        else:
            s = self.dsem[self.dnext % N_DMA_SEMS]
            self.dnext += 1
        if s.count > 0:
            deps.setdefault(id(s.handle), (s.handle, s.count))
        self._wait(q, deps)
        ins = fn(self.eng[q])
        h, v = s.bump(16)
        ins.then_inc(h, 16)
        self.ninstr += 1
        self._record(reads, writes, h, v)
        return ins

    def barrier(self):
        deps = {}
        for s in list(self.esem.values()) + self.dsem + self.dsem_sw:
            if s.count > 0:
                deps[id(s.handle)] = (s.handle, s.count)
        for e in self.ENG:
            self._wait(e, dict(deps))

    def final_wait(self, e="sp"):
        deps = {}
        for s in list(self.esem.values()) + self.dsem + self.dsem_sw:
            if s.count > 0:
                deps[id(s.handle)] = (s.handle, s.count)
        self._wait(e, deps)


class Rot:
    def __init__(self, items):
        self.items = list(items)
        self.i = 0

    def next(self):
        x = self.items[self.i % len(self.items)]
        self.i += 1
        return x


A_IN = 704
CH = [("A1", 0, 384), ("A2", 384, 320), ("B1", 704, 512), ("B2", 1216, 256),
      ("C", 1472, 512), ("D1", 1984, 512), ("D2", 2496, 512), ("D3", 3008, 512)]


def host_constants():
    c = {}
    n = np.arange(DSEQ)
    inv = (10000.0 ** (-np.arange(16, dtype=np.float32) / 16.0)).astype(np.float32)
    ar = (n // GRID_W).astype(np.float32)[:, None] * inv[None, :]
    ac = (n % GRID_W).astype(np.float32)[:, None] * inv[None, :]
    cosr, sinr = np.cos(ar).astype(np.float32), np.sin(ar).astype(np.float32)
    cosc, sinc = np.cos(ac).astype(np.float32), np.sin(ac).astype(np.float32)
    c["rope_cos"] = np.concatenate([cosr, cosr, cosc, cosc], 1).astype(np.float32)
    c["rope_sin"] = np.concatenate([-sinr, sinr, -sinc, sinc], 1).astype(np.float32)
    def band(S):
        out = np.zeros((4, S, S), np.float32)
        for g, w in enumerate((2, 4, 8, 16)):
            for t in range(S):
                lo = min(max(t - w // 2, 0), S)
                hi = min(max(t + w // 2, 0), S)
                out[g, t, lo:hi] = 1.0 / float(hi - lo)
                out[g, t, t] -= 1.0
        return out
    def blocks(S):
        A = band(S)
        nt = S // 128
        o = np.zeros((4, 5, 128, 128), np.float32)
        for g in range(4):
            At = A[g].T
            o[g, 0] = At[0:128, 128:256]
            o[g, 1] = At[0:128, 0:128]
            mid = 1 if nt > 2 else 0
            o[g, 2] = At[mid * 128:(mid + 1) * 128, mid * 128:(mid + 1) * 128]
            o[g, 3] = At[(nt - 1) * 128:, (nt - 1) * 128:]
            o[g, 4] = At[128:256, 0:128]
        return o
    c["pool_at_ctx"] = np.ascontiguousarray(blocks(SEQ).transpose(2, 0, 1, 3).reshape(128, 20, 128))
    c["pool_at_lat"] = np.ascontiguousarray(blocks(DSEQ).transpose(2, 0, 1, 3).reshape(128, 20, 128))
    kk = np.arange(128)[:, None]
    qq = np.arange(128)[None, :]
    c["win_mask"] = np.stack([(kk >= qq), (kk <= qq)], 1).astype(np.float32)
    cols = np.arange(GRID_W)
    cstart = np.clip(cols - 8, 0, GRID_W - 16)
    ok = (cols[None, :] >= cstart[:, None]) & (cols[None, :] < cstart[:, None] + 16)
    okT = ok.T.astype(np.float32)
    c["na_colok"] = np.concatenate([okT, okT], 0)
    return c


CONST_SHAPES = {"rope_cos": [DSEQ, 64], "rope_sin": [DSEQ, 64], "pool_at_ctx": [128, 20, 128],
                "pool_at_lat": [128, 20, 128], "win_mask": [128, 2, 128], "na_colok": [128, 64]}

IN_SHAPES = {
    "xp": [NSEQ * SEQ, D], "xs": [DSEQ, D],
    "c_ckv": [DEPTH, PAST, 256], "c_kr": [DEPTH, PAST, 64], "c_gk": [DEPTH, PAST, 128],
    "c_gv": [DEPTH, PAST, 128], "c_nk": [DEPTH, PAST, 512], "c_nv": [DEPTH, PAST, 512],
    "cond": [2, D],
    "w_ada": [DEPTH, D, 6 * D], "b_ada": [DEPTH, 6 * D], "w_in": [DEPTH, D, IN_COLS],
    "g_q": [DEPTH, 384], "g_kv": [DEPTH, 256], "w_uq": [DEPTH, 384, 768], "w_uk": [DEPTH, 256, 512],
    "w_uv": [DEPTH, 256, 512], "gqa_sink": [DEPTH, 8], "w_pool": [DEPTH, 4, 128, 128],
    "pool_scale": [DEPTH, 512], "na_rpb": [DEPTH, 8, 15, 31], "w_out": [DEPTH, D, D],
    "ln1_g": [DEPTH, D], "ln1_b": [DEPTH, D], "peer_wq": [DEPTH, D, D],
    "peer_keys": [DEPTH, 16, 128, 128], "peer_u": [DEPTH, 16384, D], "peer_v": [DEPTH, 16384, D],
    "ln2_g": [DEPTH, D], "ln2_b": [DEPTH, D],
}
OUT_SHAPES = {
    "yp": [NSEQ * SEQ, D], "ys": [DSEQ, D],
    "o_ckv": [NSEQ, DEPTH, SEQ, 256], "o_kr": [NSEQ, DEPTH, SEQ, 64], "o_gk": [NSEQ, DEPTH, SEQ, 128],
    "o_gv": [NSEQ, DEPTH, SEQ, 128], "o_nk": [NSEQ, DEPTH, SEQ, 512], "o_nv": [NSEQ, DEPTH, SEQ, 512],
}


def build_program(opts=None):
    opts = opts or {}
    n_layers = opts.get("layers", DEPTH)
    do_latent = opts.get("latent", True)
    do_peer = opts.get("peer", True)
    dbg = opts.get("dbg", False)

    nc = bass.Bass("TRN2", target_bir_lowering=False)
    I = {k: nc.dram_tensor(k, s, F32, kind="ExternalInput").ap() for k, s in IN_SHAPES.items()}
    C = {k: nc.dram_tensor(k, s, F32, kind="ExternalInput").ap() for k, s in CONST_SHAPES.items()}
    O = {k: nc.dram_tensor(k, s, F32, kind="ExternalOutput").ap() for k, s in OUT_SHAPES.items()}
    NTOK = NSEQ * SEQ + DSEQ
    xres = nc.dram_tensor("xres", [NTOK, D], F32, kind="Internal").ap()
    hqres = nc.dram_tensor("hqres", [NTOK, D], F32, kind="Internal").ap()
    modd = nc.dram_tensor("modd", [DEPTH, 2, 6 * D], F32, kind="Internal").ap()
    DBG = {}
    if dbg:
        DBG["mod"] = nc.dram_tensor("dbg_mod", [DEPTH, 2, 6 * D], F32, kind="ExternalOutput").ap()
        DBG["x1"] = nc.dram_tensor("dbg_x1", [NTOK, D], F32, kind="ExternalOutput").ap()
        DBG["o"] = nc.dram_tensor("dbg_o", [NTOK, D], F32, kind="ExternalOutput").ap()

    es = contextlib.ExitStack()
    with es:
        tr = TR(nc, es)

        def sb(name, shape, dt, stack=es):
            return stack.enter_context(nc.sbuf_tensor(name, shape, dt))

        PS = [es.enter_context(nc.psum_tensor(f"ps{i}", [128, 512], F32)) for i in range(8)]
        for p in PS:
            tr.excl.add(p[:].tensor.name)
        psg = Rot(PS[0:3])
        pss = Rot(PS[3:6])
        pso = Rot(PS[6:8])

        def mm(out, lhsT, rhs, start, stop):
            tr.op("pe", lambda e: e.matmul(out, lhsT, rhs, start=start, stop=stop), reads=[lhsT, rhs], writes=[out])

        def tp(out, in_, idn):
            tr.op("pe", lambda e: e.transpose(out, in_, idn), reads=[in_, idn], writes=[out])

        def dma(q, out, in_, **kw):
            tr.dma(q, lambda e: e.dma_start(out=out, in_=in_, **kw), reads=[in_], writes=[out])

        def vcopy(eng, out, in_):
            if eng == "act":
                tr.op("act", lambda e: e.copy(out, in_), reads=[in_], writes=[out])
            else:
                tr.op(eng, lambda e: e.tensor_copy(out, in_), reads=[in_], writes=[out])

        def tt(eng, out, a, b, op):
            tr.op(eng, lambda e: e.tensor_tensor(out, a, b, op=op), reads=[a, b], writes=[out])

        def ts(eng, out, a, s1, s2, op0, op1=None, extra_reads=()):
            rd = [a] + [s for s in (s1, s2) if not isinstance(s, (int, float, type(None)))] + list(extra_reads)
            if op1 is None:
                tr.op(eng, lambda e: e.tensor_scalar(out, a, s1, None, op0=op0), reads=rd, writes=[out])
            else:
                tr.op(eng, lambda e: e.tensor_scalar(out, a, s1, s2, op0=op0, op1=op1), reads=rd, writes=[out])

        def stt(out, a, s, b, op0, op1):
            rd = [a, b] + ([] if isinstance(s, (int, float)) else [s])
            tr.op("dve", lambda e: e.scalar_tensor_tensor(out, a, s, b, op0=op0, op1=op1), reads=rd, writes=[out])

        def actf(out, in_, func, scale=1.0, bias=0.0, accum=None):
            rd = [in_] + [s for s in (scale, bias) if not isinstance(s, (int, float))]
            wr = [out] + ([accum] if accum is not None else [])
            kw = {}
            if accum is not None:
                kw["accum_out"] = accum
            tr.op("act", lambda e: e.activation(out=out, in_=in_, func=func, bias=bias, scale=scale, **kw),
                  reads=rd, writes=wr)

        evq = Rot(["dve", "act"])

        def evac(out, in_):
            vcopy(evq.next(), out, in_)

        ident = sb("ident", [128, 128], F32)
        tr.op("pool", lambda e: e.memset(ident[:], 0.0), writes=[ident])
        tr.op("pool", lambda e: e.affine_select(out=ident[:], in_=ident[:], pattern=[[-1, 128]],
                                                 compare_op=ALU.not_equal, fill=1.0, base=0,
                                                 channel_multiplier=1), reads=[ident], writes=[ident])
        iota_i = sb("iota_i", [128, 256], I32)
        iota256 = sb("iota256", [128, 256], F32)
        tr.op("pool", lambda e: e.iota(iota_i[:], pattern=[[1, 256]], base=0, channel_multiplier=0), writes=[iota_i])
        tr.op("pool", lambda e: e.tensor_copy(iota256[:], iota_i[:]), reads=[iota_i], writes=[iota256])
        epsc = sb("epsc", [128, 2], F32)
        tr.op("pool", lambda e: e.memset(epsc[:, 0:1], RMS_EPS), writes=[epsc])
        tr.op("pool", lambda e: e.memset(epsc[:, 1:2], LN_EPS), writes=[epsc])

        dma("sp", xres[0:NSEQ * SEQ, :], I["xp"])
        dma("sp", xres[NSEQ * SEQ:NTOK, :], I["xs"])

        with contextlib.ExitStack() as ph:
            cond_sb = sb("cond_sb", [2, D], F32, ph)
            sil = sb("sil", [2, D], F32, ph)
            sT = sb("sT", [128, 16, 2], BF, ph)
            wa = [sb(f"wa{i}", [128, 16, 512], BF, ph) for i in range(2)]
            msb = [sb(f"msb{i}", [2, 2048], F32, ph) for i in range(2)]
            bsb = [sb(f"bsb{i}", [2, 2048], F32, ph) for i in range(2)]
            dma("sp", cond_sb[:], I["cond"])
            actf(sil[:], cond_sb[:], AF.Silu)
            ps = psg.next()
            for kc in range(16):
                tp(ps[:, kc * 2:(kc + 1) * 2], sil[0:2, kc * 128:(kc + 1) * 128], ident[0:2, 0:2])
            vcopy("dve", sT[:].rearrange("p a b -> p (a b)"), ps[:, 0:32])
            ci = 0
            for l in range(n_layers):
                for g in range(6):
                    bs, ms = bsb[g % 2], msb[g % 2]
                    dma("sp", bs[:], I["b_ada"][l, g * 2048:(g + 1) * 2048].partition_broadcast(2))
                    for j in range(4):
                        c0 = (g * 4 + j) * 512
                        w = wa[ci % 2]
                        ci += 1
                        dma("pool", w[:], I["w_ada"][l, :, c0:c0 + 512].rearrange("(kc p) c -> p kc c", p=128))
                        ps = psg.next()
                        for kc in range(16):
                            mm(ps[0:2, :], sT[:, kc, :], w[:, kc, :], kc == 0, kc == 15)
                        tt("dve", ms[:, j * 512:(j + 1) * 512], ps[0:2, :], bs[:, j * 512:(j + 1) * 512], ALU.add)
                    dma("sp", modd[l, :, g * 2048:(g + 1) * 2048], ms[:])
                    if dbg:
                        dma("sp", DBG["mod"][l, :, g * 2048:(g + 1) * 2048], ms[:])
        tr.barrier()

        uid = [0]

        def sbp(name, shape, dt, stack, part=None):
            uid[0] += 1
            t = stack.enter_context(nc.sbuf_tensor(f"{name}_{uid[0]}", shape, dt))
            if part is not None:
                tr.part[t[:].tensor.name] = part
            return t

        modcol = sb("modcol", [128, 2, 2, 16], F32)
        gqcol = sb("gqcol", [128, 3], F32)
        pscol = sb("pscol", [128, 4], F32)
        gkvb = sb("gkvb", [128, 256], F32)
        esink = sb("esink", [128, 8], F32)
        wuqn = sb("wuqn", [128, 3, 512], BF)
        wuqr = sb("wuqr", [128, 3, 256], BF)
        wuk = sb("wuk", [128, 2, 512], BF)
        wuv = sb("wuv", [128, 2, 512], BF)
        wpool = sb("wpool", [128, 4, 128], BF)
        ropec = sb("ropec", [128, 8, 64], F32)
        ropes = sb("ropes", [128, 8, 64], F32)
        winm = sb("winm", [128, 2, 128], BF)
        Ebuf = [sb(f"Ebuf{i}", [128, 512], BF) for i in range(3)]
        Erot = Rot(Ebuf)
        small = sb("small", [128, 64], F32)
        if do_latent:
            dma("sp", ropec[:], C["rope_cos"].rearrange("(t p) c -> p t c", p=128))
            dma("sp", ropes[:], C["rope_sin"].rearrange("(t p) c -> p t c", p=128))
            dma("pool", winm[:], C["win_mask"])

        def load_layer_consts(l):
            with nc.allow_non_contiguous_dma(reason="tiny per-layer vectors in column layout"):
                for path in range(2):
                    for j, k in enumerate((0, 1)):
                        dma("sp", modcol[:, path, j, :], modd[l, path, k * D:(k + 1) * D].rearrange("(c p) -> p c", p=128))
                dma("sp", gqcol[:], I["g_q"][l].rearrange("(c p) -> p c", p=128))
                dma("sp", pscol[:], I["pool_scale"][l].rearrange("(c p) -> p c", p=128))
            ts("dve", modcol[:, :, 1, :], modcol[:, :, 1, :], 1.0, None, ALU.add)
            dma("sp", gkvb[:], I["g_kv"][l].partition_broadcast(128))
            dma("sp", esink[:], I["gqa_sink"][l].partition_broadcast(128))
            actf(esink[:], esink[:], AF.Exp)
            for c3 in range(3):
                srcq = I["w_uq"][l, c3 * 128:(c3 + 1) * 128, :].rearrange("p (h d) -> p h d", h=4)
                dma("pool", wuqn[:, c3, :].rearrange("p (h d) -> p h d", h=4), srcq[:, :, 0:128])
                dma("pool", wuqr[:, c3, :].rearrange("p (h d) -> p h d", h=4), srcq[:, :, 128:192])
            dma("pool", wuk[:], I["w_uk"][l].rearrange("(c p) n -> p c n", p=128))
            dma("pool", wuv[:], I["w_uv"][l].rearrange("(c p) n -> p c n", p=128))
            dma("pool", wpool[:], I["w_pool"][l].rearrange("g c e -> c g e"))

        ropetmp = sb("ropetmp", [128, 8, 2, 16], F32)

        def rope(x, H, ti):
            xv = x.rearrange("p (h a b c) -> p h a b c", h=H, a=2, b=2)
            for a in range(2):
                xa = xv[:, :, a, :, :]
                xsw = xa[:, :, ::-1, :]
                cs = ropec[:, ti, a * 32:(a + 1) * 32].rearrange("p (b c) -> p b c", b=2).unsqueeze(1).to_broadcast([128, H, 2, 16])
                sn = ropes[:, ti, a * 32:(a + 1) * 32].rearrange("p (b c) -> p b c", b=2).unsqueeze(1).to_broadcast([128, H, 2, 16])
                tmp = ropetmp[:, 0:H, :, :]
                tt("dve", tmp, xsw, sn, ALU.mult)
                tt("dve", xa, xa, cs, ALU.mult)
                tt("dve", xa, xa, tmp, ALU.add)

        def attend(q_ops, k_ops, v_op, kblocks, dv, scale, out_ap, sink_ap=None, mask_fn=None):
            Ob = pso.next()
            Oap = Ob[:, 0:dv + 1]
            nkb = len(kblocks)
            done = 0
            for c0 in range(0, nkb, 4):
                chunk = kblocks[c0:c0 + 4]
                Sb = pss.next()
                for si, kb in enumerate(chunk):
                    ko = k_ops(kb)
                    for oi, (kq, qq) in enumerate(zip(ko, q_ops)):
                        mm(Sb[:, si * 128:(si + 1) * 128], kq, qq, oi == 0, oi == len(ko) - 1)
                E = Erot.next()
                n = len(chunk) * 128
                actf(E[:, 0:n], Sb[:, 0:n], AF.Exp, scale=scale)
                for si, kb in enumerate(chunk):
                    Es = E[:, si * 128:(si + 1) * 128]
                    if mask_fn is not None:
                        mask_fn(kb, Es)
                    mm(Oap, Es, v_op(kb), done == 0, done == nkb - 1)
                    done += 1
            den = small[:, 0:1]
            if sink_ap is not None:
                tt("dve", den, Ob[:, dv:dv + 1], sink_ap, ALU.add)
            else:
                vcopy("dve", den, Ob[:, dv:dv + 1])
            tr.op("dve", lambda e: e.reciprocal(small[:, 1:2], den), reads=[small], writes=[small])
            ts("dve", out_ap, Ob[:, 0:dv], small[:, 1:2], None, ALU.mult)

        def mixer(l, path, tok0, NT, seq, oT, ph):
            S = NT * 128
            L = PAST if path == 1 else 0
            LB = L // 128
            NK = L + S
            KB = NK // 128
            hT = sbp("hT", [128, 16, S], BF, ph, (S, 128))
            wb = [sbp(f"wb{i}", [128, 16, 512], BF, ph) for i in range(2)]
            wrot = Rot(wb)
            s1 = contextlib.ExitStack()
            xt = [sbp(f"xt{i}", [128, D], F32, s1) for i in range(2)]
            for ti in range(NT):
                x = xt[ti % 2]
                dma("sp", x[:], xres[tok0 + ti * 128: tok0 + (ti + 1) * 128, :])
                for q4 in range(4):
                    ps = psg.next()
                    for j in range(4):
                        kc = q4 * 4 + j
                        tp(ps[:, j * 128:(j + 1) * 128], x[:, kc * 128:(kc + 1) * 128], ident[:])
                    for j in range(4):
                        kc = q4 * 4 + j
                        dst = hT[:, kc, ti * 128:(ti + 1) * 128]
                        src = ps[:, j * 128:(j + 1) * 128]
                        if (j % 2) == 0:
                            ts("dve", dst, src, modcol[:, path, 1, kc:kc + 1], modcol[:, path, 0, kc:kc + 1], ALU.mult, ALU.add)
                        else:
                            actf(dst, src, AF.Identity, scale=modcol[:, path, 1, kc:kc + 1], bias=modcol[:, path, 0, kc:kc + 1])
            tr.barrier()
            s1.close()

            def in_proj(chunk_idx, handler):
                _, c0, w = CH[chunk_idx]
                wt = wrot.next()
                dma("pool", wt[:, :, 0:w], I["w_in"][l, :, c0:c0 + w].rearrange("(kc p) c -> p kc c", p=128))
                for ti in range(NT):
                    ps = psg.next()
                    for kc in range(16):
                        mm(ps[:, 0:w], hT[:, kc, ti * 128:(ti + 1) * 128], wt[:, kc, 0:w], kc == 0, kc == 15)
                    handler(ti, ps)

            def rms_rstd(ps_ap, n, col):
                junk = Erot.next()
                actf(junk[:, 0:n], ps_ap, AF.Square, accum=small[:, col:col + 1])
                ts("dve", small[:, col:col + 1], small[:, col:col + 1], 1.0 / n, None, ALU.mult)
                actf(small[:, col:col + 1], small[:, col:col + 1], AF.Sqrt, bias=epsc[:, 0:1])
                tr.op("dve", lambda e: e.reciprocal(small[:, col:col + 1], small[:, col:col + 1]), reads=[small], writes=[small])

            def transpose_into(dst_fn, src, nchunks, scale_col=None):
                for c0 in range(0, nchunks, 4):
                    ps = psg.next()
                    n = min(4, nchunks - c0)
                    for j in range(n):
                        tp(ps[:, j * 128:(j + 1) * 128], src[:, (c0 + j) * 128:(c0 + j + 1) * 128], ident[:])
                    for j in range(n):
                        if scale_col is not None:
                            ts("dve", dst_fn(c0 + j), ps[:, j * 128:(j + 1) * 128], scale_col[:, c0 + j:c0 + j + 1], None, ALU.mult)
                        else:
                            evac(dst_fn(c0 + j), ps[:, j * 128:(j + 1) * 128])

            with contextlib.ExitStack() as ga:
                cqT = sbp("cqT", [128, 3, S], BF, ga, (S, 128))
                ckvT = sbp("ckvT", [128, 2, NK], BF, ga, (NK, 128))
                krT = sbp("krT", [128, NK], BF, ga, (NK, 128))
                QnT = sbp("QnT", [128, 4, S], BF, ga, (S, 128))
                QrT = sbp("QrT", [128, 2, S], BF, ga, (S, 128))
                KnT = sbp("KnT", [128, 4, NK], BF, ga, (NK, 128))
                VA = sbp("VA", [128, KB, 4, 130], BF, ga, (KB * 520, 520))
                og = sbp("ogA", [128, NT, 512], F32, ga, (NT * 512, 512))
                tmpA = [sbp(f"tmpA{i}", [128, 384], F32, ga) for i in range(2)]
                ckvs = [sbp(f"ckvs{i}", [128, 256], F32, ga) for i in range(2)]
                kr2 = [sbp(f"kr2{i}", [128, 128], F32, ga) for i in range(2)]
                qrs = [sbp(f"qrs{i}", [128, 256], F32, ga) for i in range(2)]
                tr.op("pool", lambda e: e.memset(VA[:, :, :, 128:129], 1.0), writes=["VAones"])

                def hA1(ti, ps):
                    rms_rstd(ps[:, 0:384], 384, 2)
                    t = tmpA[ti % 2]
                    ts("dve", t[:], ps[:, 0:384], small[:, 2:3], None, ALU.mult)
                    transpose_into(lambda c: cqT[:, c, ti * 128:(ti + 1) * 128], t, 3, scale_col=gqcol)

                def ckv_finish(cs, k2, col0, ti_rope):
                    if ti_rope is not None:
                        rope(k2[:, 0:64], 1, ti_rope)
                    vcopy("pool", k2[:, 64:128], k2[:, 0:64])
                    transpose_into(lambda c: ckvT[:, c, col0:col0 + 128], cs, 2)
                    transpose_into(lambda c: krT[:, col0:col0 + 128], k2, 1)

                def hA2(ti, ps):
                    rms_rstd(ps[:, 0:256], 256, 3)
                    cs, k2 = ckvs[ti % 2], kr2[ti % 2]
                    stt(cs[:], ps[:, 0:256], small[:, 3:4], gkvb[:], ALU.mult, ALU.mult)
                    vcopy("act", k2[:, 0:64], ps[:, 256:320])
                    if path == 0:
                        dma("sp", O["o_ckv"][seq, l, ti * 128:(ti + 1) * 128, :], cs[:])
                        dma("sp", O["o_kr"][seq, l, ti * 128:(ti + 1) * 128, :], k2[:, 0:64])
                    ckv_finish(cs, k2, L + ti * 128, ti if path == 1 else None)

                in_proj(0, hA1)
                in_proj(1, hA2)
                for b in range(LB):
                    cs, k2 = ckvs[b % 2], kr2[b % 2]
                    dma("sp", cs[:], I["c_ckv"][l, b * 128:(b + 1) * 128, :])
                    dma("sp", k2[:, 0:64], I["c_kr"][l, b * 128:(b + 1) * 128, :])
                    ckv_finish(cs, k2, b * 128, None)
                for h in range(4):
                    for k0 in range(0, NK, 512):
                        n = min(512, NK - k0)
                        ps = psg.next()
                        for c in range(2):
                            mm(ps[:, 0:n], wuk[:, c, h * 128:(h + 1) * 128], ckvT[:, c, k0:k0 + n], c == 0, c == 1)
                        evac(KnT[:, h, k0:k0 + n], ps[:, 0:n])
                for kb in range(KB):
                    ps = psg.next()
                    for c in range(2):
                        mm(ps[:, 0:512], ckvT[:, c, kb * 128:(kb + 1) * 128], wuv[:, c, :], c == 0, c == 1)
                    tr.op(evq.next(), (lambda kb, ps: (lambda e: (e.copy if e is nc.scalar else e.tensor_copy)(
                        VA[:, kb, :, 0:128], ps[:, 0:512].rearrange("p (h d) -> p h d", h=4))))(kb, ps),
                        reads=[ps, "VAones"], writes=[VA[:, kb, 0, :]])
                for h in range(4):
                    for t0 in range(0, S, 512):
                        n = min(512, S - t0)
                        ps = psg.next()
                        for c in range(3):
                            mm(ps[:, 0:n], wuqn[:, c, h * 128:(h + 1) * 128], cqT[:, c, t0:t0 + n], c == 0, c == 2)
                        evac(QnT[:, h, t0:t0 + n], ps[:, 0:n])
                for ti in range(NT):
                    ps = psg.next()
                    for c in range(3):
                        mm(ps[:, 0:256], cqT[:, c, ti * 128:(ti + 1) * 128], wuqr[:, c, :], c == 0, c == 2)
                    qr = qrs[ti % 2]
                    evac(qr[:], ps[:, 0:256])
                    if path == 1:
                        rope(qr[:], 4, ti)
                    transpose_into(lambda c: QrT[:, c, ti * 128:(ti + 1) * 128], qr, 2)
                scl = 192.0 ** -0.5
                for qt in range(NT):
                    for h in range(4):
                        j, e2 = h // 2, h % 2
                        qs = slice(qt * 128, (qt + 1) * 128)
                        attend([QnT[:, h, qs], QrT[e2 * 64:(e2 + 1) * 64, j, qs]],
                               lambda kb, h=h, e2=e2: [KnT[:, h, kb * 128:(kb + 1) * 128], krT[e2 * 64:(e2 + 1) * 64, kb * 128:(kb + 1) * 128]],
                               lambda kb, h=h: VA[:, kb, h, 0:129],
                               list(range(KB)), 128, scl, og[:, qt, h * 128:(h + 1) * 128])
                for qt in range(NT):
                    transpose_into(lambda c: oT[:, c, qt * 128:(qt + 1) * 128], og[:, qt, :], 4)
                    if dbg:
                        dma("sp", DBG["o"][tok0 + qt * 128: tok0 + (qt + 1) * 128, 0:512], og[:, qt, :])
                tr.barrier()

            with contextlib.ExitStack() as gb:
                QTB = sbp("QTB", [128, 4, S], BF, gb, (S, 128))
                KTB = sbp("KTB", [128, 2, NK], BF, gb, (NK, 128))
                VB = sbp("VB", [128, KB, 2, 66], BF, gb, (KB * 132, 132))
                og = sbp("ogB", [128, NT, 512], F32, gb, (NT * 512, 512))
                gqs = [sbp(f"gqs{i}", [128, 512], F32, gb) for i in range(2)]
                gks = [sbp(f"gks{i}", [128, 256], F32, gb) for i in range(2)]
                gvs = [sbp(f"gvs{i}", [128, 128], F32, gb) for i in range(2)]
                tr.op("pool", lambda e: e.memset(VB[:, :, :, 64:65], 1.0), writes=["VBones"])

                def hB1(ti, ps):
                    gq = gqs[ti % 2]
                    evac(gq[:], ps[:, 0:512])
                    if path == 1:
                        rope(gq[:], 8, ti)
                    transpose_into(lambda c: QTB[:, c, ti * 128:(ti + 1) * 128], gq, 4)

                def kvB(gk, gv, col0, kb):
                    vcopy("pool", gk[:, 128:192], gk[:, 64:128])
                    vcopy("pool", gk[:, 192:256], gk[:, 0:64])
                    transpose_into(lambda c: KTB[:, c, col0:col0 + 128], gk, 2)
                    tr.op("dve", lambda e: e.tensor_copy(VB[:, kb, :, 0:64], gv[:].rearrange("p (h d) -> p h d", h=2)),
                          reads=[gv, "VBones"], writes=[VB[:, kb, 0, :]])

                def hB2(ti, ps):
                    gk, gv = gks[ti % 2], gvs[ti % 2]
                    evac(gk[:, 0:128], ps[:, 0:128])
                    evac(gv[:], ps[:, 128:256])
                    if path == 0:
                        dma("sp", O["o_gk"][seq, l, ti * 128:(ti + 1) * 128, :], gk[:, 0:128])
                        dma("sp", O["o_gv"][seq, l, ti * 128:(ti + 1) * 128, :], gv[:])
                    else:
                        rope(gk[:, 0:128], 2, ti)
                    kvB(gk, gv, L + ti * 128, LB + ti)

                in_proj(2, hB1)
                in_proj(3, hB2)
                for b in range(LB):
                    gk, gv = gks[b % 2], gvs[b % 2]
                    dma("sp", gk[:, 0:128], I["c_gk"][l, b * 128:(b + 1) * 128, :])
                    dma("sp", gv[:], I["c_gv"][l, b * 128:(b + 1) * 128, :])
                    kvB(gk, gv, b * 128, b)
                for qt in range(NT):
                    if path == 0:
                        kbl = list(range(KB))
                    else:
                        kbl = list(range(LB)) + [LB + t for t in (qt - 1, qt, qt + 1) if 0 <= t < NT]

                    def mfn(kb, Es, qt=qt):
                        if path == 0 or kb < LB or kb == LB + qt:
                            return
                        i = 0 if kb == LB + qt - 1 else 1
                        tt("pool", Es, Es, winm[:, i, :], ALU.mult)

                    for h in range(8):
                        j, e2, kv = h // 2, h % 2, h // 4
                        a = 0 if e2 == kv else 1
                        qs = slice(qt * 128, (qt + 1) * 128)
                        attend([QTB[e2 * 64:(e2 + 1) * 64, j, qs]],
                               lambda kb, a=a, e2=e2: [KTB[e2 * 64:(e2 + 1) * 64, a, kb * 128:(kb + 1) * 128]],
                               lambda kb, kv=kv: VB[:, kb, kv, 0:65],
                               kbl, 64, 0.125, og[:, qt, h * 64:(h + 1) * 64],
                               sink_ap=esink[:, h:h + 1], mask_fn=mfn)
                for qt in range(NT):
                    transpose_into(lambda c: oT[:, 4 + c, qt * 128:(qt + 1) * 128], og[:, qt, :], 4)
                    if dbg:
                        dma("sp", DBG["o"][tok0 + qt * 128: tok0 + (qt + 1) * 128, 512:1024], og[:, qt, :])
                tr.barrier()

            with contextlib.ExitStack() as gc:
                zc = sbp("zc", [128, NT, 512], F32, gc, (NT * 512, 512))
                AT = sbp("AT", [128, 20, 128], F32, gc)
                dlt = [sbp(f"dlt{i}", [128, 128], BF, gc) for i in range(2)]
                dma("sp", AT[:], C["pool_at_lat" if path == 1 else "pool_at_ctx"])
                in_proj(4, lambda ti, ps: evac(zc[:, ti, :], ps[:, 0:512]))
                di = 0
                for ti in range(NT):
                    for g in range(4):
                        nb = []
                        if ti > 0:
                            nb.append((ti - 1, 0))
                        nb.append((ti, 1 if ti == 0 else (3 if ti == NT - 1 else 2)))
                        if ti < NT - 1:
                            nb.append((ti + 1, 4))
                        ps = psg.next()
                        for i, (st, kind) in enumerate(nb):
                            mm(ps[:, 0:128], zc[:, st, g * 128:(g + 1) * 128], AT[:, g * 5 + kind, :], i == 0, i == len(nb) - 1)
                        d = dlt[di % 2]
                        di += 1
                        evac(d[:], ps[:, 0:128])
                        mm(ps[:, 128:256], wpool[:, g, :], d[:], True, True)
                        ts("dve", oT[:, 8 + g, ti * 128:(ti + 1) * 128], ps[:, 128:256], pscol[:, g:g + 1], None, ALU.mult)
                tr.barrier()

            with contextlib.ExitStack() as gd:
                Tm = build_tm(l, gd) if path == 1 else None
                QTD = sbp("QTD", [128, 4, S], BF, gd, (S, 128))
                KTD = sbp("KTD", [128, 4, NK], BF, gd, (NK, 128))
                VD = sbp("VD", [128, KB, 8, 66], BF, gd, (KB * 528, 528))
                og = sbp("ogD", [128, NT, 512], F32, gd, (NT * 512, 512))
                nqs = [sbp(f"nqs{i}", [128, 512], F32, gd) for i in range(2)]
                nks = [sbp(f"nks{i}", [128, 512], F32, gd) for i in range(2)]
                nvs = [sbp(f"nvs{i}", [128, 512], F32, gd) for i in range(2)]
                tr.op("pool", lambda e: e.memset(VD[:, :, :, 64:65], 1.0), writes=["VDones"])

                def hD1(ti, ps):
                    nq = nqs[ti % 2]
                    evac(nq[:], ps[:, 0:512])
                    transpose_into(lambda c: QTD[:, c, ti * 128:(ti + 1) * 128], nq, 4)

                def kD(nk, col0):
                    transpose_into(lambda c: KTD[:, c, col0:col0 + 128], nk, 4)

                def vD(nv, kb):
                    tr.op("dve", lambda e: e.tensor_copy(VD[:, kb, :, 0:64], nv[:].rearrange("p (h d) -> p h d", h=8)),
                          reads=[nv, "VDones"], writes=[VD[:, kb, 0, :]])

                def hD2(ti, ps):
                    nk = nks[ti % 2]
                    evac(nk[:], ps[:, 0:512])
                    if path == 0:
                        dma("sp", O["o_nk"][seq, l, ti * 128:(ti + 1) * 128, :], nk[:])
                    kD(nk, L + ti * 128)

                def hD3(ti, ps):
                    nv = nvs[ti % 2]
                    evac(nv[:], ps[:, 0:512])
                    if path == 0:
                        dma("sp", O["o_nv"][seq, l, ti * 128:(ti + 1) * 128, :], nv[:])
                    vD(nv, LB + ti)

                in_proj(5, hD1)
                in_proj(6, hD2)
                in_proj(7, hD3)
                for b in range(LB):
                    nk, nv = nks[b % 2], nvs[b % 2]
                    dma("sp", nk[:], I["c_nk"][l, b * 128:(b + 1) * 128, :])
                    dma("sp", nv[:], I["c_nv"][l, b * 128:(b + 1) * 128, :])
                    kD(nk, b * 128)
                    vD(nv, b)
                rst = lambda r: min(max(r - 4, 0), 8)
                for qt in range(NT):
                    if path == 0:
                        kbl = list(range(KB))
                    else:
                        k_lo = rst(2 * qt) // 2
                        k_hi = (rst(2 * qt + 1) + 7) // 2
                        kbl = list(range(LB)) + [LB + t for t in range(k_lo, k_hi + 1)]
                    for h in range(8):
                        j, e2 = h // 2, h % 2
                        qs = slice(qt * 128, (qt + 1) * 128)

                        def mfn(kb, Es, qt=qt, h=h):
                            if path == 0 or kb < LB:
                                return
                            kt = kb - LB
                            for kr in range(2):
                                for qr in range(2):
                                    rk, rq = 2 * kt + kr, 2 * qt + qr
                                    quad = Es[kr * 64:(kr + 1) * 64, qr * 64:(qr + 1) * 64]
                                    if rst(rq) <= rk <= rst(rq) + 7:
                                        dr = rk - rq + 7
                                        tt("pool" if kr == 0 else "dve", quad, quad, Tm[kr * 64:(kr + 1) * 64, h * 15 + dr, :], ALU.mult)
                                    else:
                                        tr.op("pool", lambda e, quad=quad: e.memset(quad, 0.0), writes=[quad])

                        attend([QTD[e2 * 64:(e2 + 1) * 64, j, qs]],
                               lambda kb, j=j, e2=e2: [KTD[e2 * 64:(e2 + 1) * 64, j, kb * 128:(kb + 1) * 128]],
                               lambda kb, h=h: VD[:, kb, h, 0:65],
                               kbl, 64, 0.125, og[:, qt, h * 64:(h + 1) * 64], mask_fn=mfn)
                for qt in range(NT):
                    transpose_into(lambda c: oT[:, 12 + c, qt * 128:(qt + 1) * 128], og[:, qt, :], 4)
                    if dbg:
                        dma("sp", DBG["o"][tok0 + qt * 128: tok0 + (qt + 1) * 128, 1536:2048], og[:, qt, :])
                tr.barrier()

        colok = sb("colok", [128, 64], F32)
        YW = 64 * 97
        ysc = nc.dram_tensor("ysc", [120, YW], F32, kind="Internal").ap()
        if do_latent:
            dma("sp", colok[:], C["na_colok"])

        def build_tm(l, stack):
            Tm = sbp("Tm", [128, 120, 64], BF, stack)
            with contextlib.ExitStack() as ph:
                rows = sbp("rpbrows", [120, 31], F32, ph)
                rr = sbp("rpbrev", [120, 31], F32, ph)
                zt = sbp("zt", [120, YW], F32, ph)
                traw = sbp("traw", [128, 120, 64], F32, ph)
                dma("sp", rows[:], I["na_rpb"][l].rearrange("h a b -> (h a) b"))
                vcopy("dve", rr[:], rows[:, ::-1])
                tr.op("pool", lambda e: e.memset(zt[:], 0.0), writes=[zt])
                dma("sp", ysc, zt[:])
                with nc.allow_non_contiguous_dma(reason="toeplitz skew"):
                    dma("sp", ysc.rearrange("a (k c) -> a k c", c=97)[:, :, 0:31],
                        rr[:].unsqueeze(1).to_broadcast([120, 64, 31]))
                    src = ysc[:, 0:64 * 96].rearrange("a (k c) -> k a c", c=96)[:, :, 15:79]
                    dma("sp", traw[0:64, :, :], src)
                    dma("sp", traw[64:128, :, :], src)
                actf(traw[:], traw[:], AF.Exp)
                tt("dve", Tm[:], traw[:], colok[:].unsqueeze(1).to_broadcast([128, 120, 64]), ALU.mult)
                tr.barrier()
            return Tm

        def layer_norm_tile(x, out, lng, lnb, scratch):
            st = scratch["st"]
            for c in range(4):
                tr.op("dve", lambda e, c=c: e.bn_stats(st[:, c, :], x[:, c * 512:(c + 1) * 512]), reads=[x], writes=[st])
            mv = scratch["mv"]
            tr.op("dve", lambda e: e.bn_aggr(mv[:, 0:2], st[:].rearrange("p a b -> p (a b)")), reads=[st], writes=[mv])
            actf(mv[:, 2:3], mv[:, 1:2], AF.Sqrt, bias=epsc[:, 1:2])
            tr.op("dve", lambda e: e.reciprocal(mv[:, 3:4], mv[:, 2:3]), reads=[mv], writes=[mv])
            ts("dve", out[:], x[:], mv[:, 0:1], mv[:, 3:4], ALU.subtract, ALU.mult)
            tt("pool", out[:], out[:], lng[:], ALU.mult)
            tt("dve", out[:], out[:], lnb[:], ALU.add)

        def wpass1(l, path, tok0, NT, oT):
            with contextlib.ExitStack() as ph:
                wo = sbp("wo", [128, 16, D], BF, ph, (16 * D, D * 4))
                for c in range(4):
                    dma("pool", wo[:, 4 * c:4 * c + 4, :], I["w_out"][l, c * 512:(c + 1) * 512, :].rearrange("(kc p) n -> p kc n", p=128))
                g1b = sbp("g1b", [128, D], F32, ph)
                dma("sp", g1b[:], modd[l, path, 2 * D:3 * D].partition_broadcast(128))
                xt = [sbp(f"w1x{i}", [128, D], F32, ph) for i in range(2)]
                x1p = [sbp(f"w1p{i}", [128, D], F32, ph) for i in range(2)]
                tmp = sbp("w1t", [128, 512], F32, ph)
                for ti in range(NT):
                    x, xp_ = xt[ti % 2], x1p[ti % 2]
                    rows = slice(tok0 + ti * 128, tok0 + (ti + 1) * 128)
                    dma("sp", x[:], xres[rows, :])
                    for cc in range(4):
                        cs = slice(cc * 512, (cc + 1) * 512)
                        ps = psg.next()
                        for mc in range(16):
                            mm(ps[:], oT[:, mc, ti * 128:(ti + 1) * 128], wo[:, mc, cs], mc == 0, mc == 15)
                        tt("dve", tmp[:], ps[:], g1b[:, cs], ALU.mult)
                        stt(xp_[:, cs], x[:, cs], ALPHA, tmp[:], ALU.mult, ALU.add)
                    dma("sp", xres[rows, :], xp_[:])

        def wpass2(l, path, tok0, NT):
            with contextlib.ExitStack() as ph:
                lng = sbp("lng", [128, D], F32, ph)
                lnb = sbp("lnb", [128, D], F32, ph)
                sc2b = sbp("sc2b", [128, D], F32, ph)
                sh2b = sbp("sh2b", [128, D], F32, ph)
                dma("sp", lng[:], I["ln1_g"][l].partition_broadcast(128))
                dma("sp", lnb[:], I["ln1_b"][l].partition_broadcast(128))
                dma("sp", sh2b[:], modd[l, path, 3 * D:4 * D].partition_broadcast(128))
                dma("sp", sc2b[:], modd[l, path, 4 * D:5 * D].partition_broadcast(128))
                ts("pool", sc2b[:], sc2b[:], 1.0, None, ALU.add)
                scr = {"st": sbp("lnst", [128, 4, 6], F32, ph), "mv": sbp("lnmv", [128, 4], F32, ph)}
                xt = [sbp(f"w2x{i}", [128, D], F32, ph) for i in range(2)]
                hq = [sbp(f"w2h{i}", [128, D], F32, ph) for i in range(2)]
                for ti in range(NT):
                    x, h = xt[ti % 2], hq[ti % 2]
                    rows = slice(tok0 + ti * 128, tok0 + (ti + 1) * 128)
                    dma("sp", x[:], xres[rows, :])
                    layer_norm_tile(x, x, lng, lnb, scr)
                    dma("sp", xres[rows, :], x[:])
                    if dbg:
                        dma("sp", DBG["x1"][rows, :], x[:])
                    tt("pool", h[:], x[:], sc2b[:], ALU.mult)
                    tt("dve", h[:], h[:], sh2b[:], ALU.add)
                    dma("sp", hqres[rows, :], h[:])

        def peer(l, path, tok0, G, last):
            GT = G * 128
            with contextlib.ExitStack() as ph:
                eidx = sbp("eidx", [128, G, 128], I32, ph)
                gwt = sbp("gwt", [128, G, 128], F32, ph)
                with contextlib.ExitStack() as p1:
                    hqT = sbp("hqT", [128, 16, GT], F32, p1, (GT, 128))
                    qT = sbp("qT", [128, 16, GT], F32, p1)
                    wq = [sbp(f"wq{i}", [128, 16, 128], F32, p1) for i in range(2)]
                    hq = [sbp(f"phq{i}", [128, D], F32, p1) for i in range(2)]
                    ssb = sbp("ssb", [128, 16, 128], F32, p1, (2048, 128))
                    v16 = sbp("v16", [128, 16, 16], F32, p1)
                    i16 = sbp("i16", [128, 16, 16], U32, p1)
                    i16f = sbp("i16f", [128, 16, 16], F32, p1)
                    cand = sbp("cand", [128, 8, 256], F32, p1)
                    eix = sbp("eix", [128, 8, 256], F32, p1)
                    scr = sbp("pscr", [128, 256], F32, p1)
                    sc16 = sbp("sc16", [128, 8, 16], F32, p1)
                    ci16 = sbp("ci16", [128, 8, 16], U32, p1)
                    cif = sbp("cif", [128, 8, 16], F32, p1)
                    ef = sbp("ef", [128, 128], F32, p1)
                    gs = sbp("gs", [128, 24], F32, p1)
                    keysT = sbp("keysT", [128, 16, 128], F32, p1)
                    dma("sp", ssb[:], I["peer_keys"][l].rearrange("c n d -> n c d"))
                    for q4 in range(4):
                        ps = psg.next()
                        for j in range(4):
                            tp(ps[:, j * 128:(j + 1) * 128], ssb[:, q4 * 4 + j, :], ident[:])
                        evac(keysT[:, q4 * 4:(q4 + 1) * 4, :].rearrange("p a b -> p (a b)"), ps[:])
                    for t in range(G):
                        h = hq[t % 2]
                        dma("sp", h[:], hqres[tok0 + t * 128: tok0 + (t + 1) * 128, :])
                        for q4 in range(4):
                            ps = psg.next()
                            for j in range(4):
                                tp(ps[:, j * 128:(j + 1) * 128], h[:, (q4 * 4 + j) * 128:(q4 * 4 + j + 1) * 128], ident[:])
                            for j in range(4):
                                evac(hqT[:, q4 * 4 + j, t * 128:(t + 1) * 128], ps[:, j * 128:(j + 1) * 128])
                    for c in range(16):
                        w = wq[c % 2]
                        dma("sp", w[:], I["peer_wq"][l, :, c * 128:(c + 1) * 128].rearrange("(kc p) n -> p kc n", p=128))
                        ps = psg.next()
                        for kc in range(16):
                            mm(ps[:, 0:GT], w[:, kc, :], hqT[:, kc, :], kc == 0, kc == 15)
                        evac(qT[:, c, :], ps[:, 0:GT])
                    v16v = v16[:].rearrange("p (h two) k -> p h two k", two=2)
                    i16fv = i16f[:].rearrange("p (h two) k -> p h two k", two=2)
                    for t in range(G):
                        for q4 in range(4):
                            ps = psg.next()
                            for j in range(4):
                                c = q4 * 4 + j
                                mm(ps[:, j * 128:(j + 1) * 128], qT[:, c, t * 128:(t + 1) * 128], keysT[:, c, :], True, True)
                            evac(ssb[:, q4 * 4:(q4 + 1) * 4, :].rearrange("p a b -> p (a b)"), ps[:])
                        for c in range(16):
                            sv = ssb[:, c, :]
                            tr.op("dve", lambda e, c=c, sv=sv: e.max(out=v16[:, c, 0:8], in_=sv), reads=[sv], writes=[v16])
                            tr.op("dve", lambda e, c=c, sv=sv: e.max_index(out=i16[:, c, 0:8], in_max=v16[:, c, 0:8], in_values=sv), reads=[sv, v16], writes=[i16])
                            tr.op("dve", lambda e, c=c, sv=sv: e.match_replace(out=scr[:, 0:128], in_to_replace=v16[:, c, 0:8], in_values=sv, imm_value=NEG), reads=[sv, v16], writes=[scr])
                            tr.op("dve", lambda e, c=c: e.max(out=v16[:, c, 8:16], in_=scr[:, 0:128]), reads=[scr], writes=[v16])
                            tr.op("dve", lambda e, c=c: e.max_index(out=i16[:, c, 8:16], in_max=v16[:, c, 8:16], in_values=scr[:, 0:128]), reads=[scr, v16], writes=[i16])
                        vcopy("dve", i16f[:], i16[:])
                        ts("dve", i16fv[:, :, 0, :], i16fv[:, :, 0, :], 128.0, None, ALU.mult)
                        c4 = cand[:].rearrange("p h (a b) -> p h a b", a=16)
                        e4 = eix[:].rearrange("p h (a b) -> p h a b", a=16)
                        tt("dve", c4, v16v[:, :, 0, :].unsqueeze(3).to_broadcast([128, 8, 16, 16]),
                           v16v[:, :, 1, :].unsqueeze(2).to_broadcast([128, 8, 16, 16]), ALU.add)
                        tt("pool", e4, i16fv[:, :, 0, :].unsqueeze(3).to_broadcast([128, 8, 16, 16]),
                           i16fv[:, :, 1, :].unsqueeze(2).to_broadcast([128, 8, 16, 16]), ALU.add)
                        for hh in range(8):
                            cv = cand[:, hh, :]
                            tr.op("dve", lambda e, hh=hh, cv=cv: e.max(out=sc16[:, hh, 0:8], in_=cv), reads=[cv], writes=[sc16])
                            tr.op("dve", lambda e, hh=hh, cv=cv: e.max_index(out=ci16[:, hh, 0:8], in_max=sc16[:, hh, 0:8], in_values=cv), reads=[cv, sc16], writes=[ci16])
                            tr.op("dve", lambda e, hh=hh, cv=cv: e.match_replace(out=scr[:], in_to_replace=sc16[:, hh, 0:8], in_values=cv, imm_value=NEG), reads=[cv, sc16], writes=[scr])
                            tr.op("dve", lambda e, hh=hh: e.max(out=sc16[:, hh, 8:16], in_=scr[:]), reads=[scr], writes=[sc16])
                            tr.op("dve", lambda e, hh=hh: e.max_index(out=ci16[:, hh, 8:16], in_max=sc16[:, hh, 8:16], in_values=scr[:]), reads=[scr, sc16], writes=[ci16])
                        vcopy("dve", cif[:], ci16[:])
                        for hh in range(8):
                            for k in range(16):
                                tr.op("dve", lambda e, hh=hh, k=k: e.scalar_tensor_tensor(
                                    scr[:], iota256[:], cif[:, hh, k:k + 1], eix[:, hh, :],
                                    op0=ALU.is_equal, op1=ALU.mult, accum_out=ef[:, hh * 16 + k:hh * 16 + k + 1]),
                                    reads=[iota256, cif, eix], writes=[scr, ef])
                        ts("dve", ef[:], ef[:], float(l * 16384), None, ALU.add)
                        vcopy("dve", eidx[:, t, :], ef[:])
                        ts("dve", gs[:, 0:8], sc16[:, :, 0], -1.0, None, ALU.mult)
                        for hh in range(8):
                            actf(gwt[:, t, hh * 16:(hh + 1) * 16], sc16[:, hh, :], AF.Exp, bias=gs[:, hh:hh + 1], accum=gs[:, 8 + hh:9 + hh])
                        tr.op("dve", lambda e: e.reciprocal(gs[:, 16:24], gs[:, 8:16]), reads=[gs], writes=[gs])
                        gv_ = gwt[:, t, :].rearrange("p (h k) -> p h k", h=8)
                        tt("dve", gv_, gv_, gs[:, 16:24].unsqueeze(2).to_broadcast([128, 8, 16]), ALU.mult)
                tr.barrier()
                with contextlib.ExitStack() as p2:
                    NB = 4
                    ub = [sbp(f"ub{i}", [128, D], F32, p2) for i in range(NB)]
                    vb = [sbp(f"vb{i}", [128, D], F32, p2) for i in range(NB)]
                    hqt = sbp("hqt", [128, D], F32, p2)
                    yt = sbp("yt", [128, D], F32, p2)
                    x1 = sbp("x1t", [128, D], F32, p2)
                    junk = sbp("junk", [128, D], F32, p2)
                    av = sbp("av", [128, 128], F32, p2)
                    wg = sbp("wg", [128, 128], F32, p2)
                    g2b = sbp("g2b", [128, D], F32, p2)
                    lng = sbp("lng2", [128, D], F32, p2)
                    lnb = sbp("lnb2", [128, D], F32, p2)
                    scr2 = {"st": sbp("lnst2", [128, 4, 6], F32, p2), "mv": sbp("lnmv2", [128, 4], F32, p2)}
                    dma("sp", g2b[:], modd[l, path, 5 * D:6 * D].partition_broadcast(128))
                    dma("sp", lng[:], I["ln2_g"][l].partition_broadcast(128))
                    dma("sp", lnb[:], I["ln2_b"][l].partition_broadcast(128))
                    for t in range(G):
                        rows = slice(tok0 + t * 128, tok0 + (t + 1) * 128)
                        dma("sp", hqt[:], hqres[rows, :])
                        dma("sp", x1[:], xres[rows, :])
                        for p in range(128):
                            u = ub[p % NB]
                            tr.dma("pool", lambda e, u=u, p=p: e.indirect_dma_start(
                                out=u[:], out_offset=None, in_=I["peer_u"].rearrange("l e d -> (l e) d"),
                                in_offset=bass.IndirectOffsetOnAxis(ap=eidx[:, t, p:p + 1], axis=0)),
                                reads=[eidx], writes=[u])
                            tr.op("dve", lambda e, u=u, p=p: e.scalar_tensor_tensor(
                                junk[:], u[:], 1.0, hqt[:], op0=ALU.mult, op1=ALU.mult, accum_out=av[:, p:p + 1]),
                                reads=[u, hqt], writes=[junk, av])
                        actf(wg[:], av[:], AF.Gelu_apprx_tanh)
                        tt("dve", wg[:], wg[:], gwt[:, t, :], ALU.mult)
                        for p in range(128):
                            v = vb[p % NB]
                            tr.dma("pool", lambda e, v=v, p=p: e.indirect_dma_start(
                                out=v[:], out_offset=None, in_=I["peer_v"].rearrange("l e d -> (l e) d"),
                                in_offset=bass.IndirectOffsetOnAxis(ap=eidx[:, t, p:p + 1], axis=0)),
                                reads=[eidx], writes=[v])
                            if p == 0:
                                ts("dve", yt[:], v[:], wg[:, 0:1], None, ALU.mult)
                            else:
                                stt(yt[:], v[:], wg[:, p:p + 1], yt[:], ALU.mult, ALU.add)
                        tt("pool", yt[:], yt[:], g2b[:], ALU.mult)
                        stt(yt[:], x1[:], ALPHA, yt[:], ALU.mult, ALU.add)
                        layer_norm_tile(yt, yt, lng, lnb, scr2)
                        dma("sp", xres[rows, :], yt[:])
            tr.barrier()

        units = [(0, s * SEQ, 2, s) for s in range(NSEQ)]
        if do_latent:
            units.append((1, NSEQ * SEQ, 8, None))
        for l in range(n_layers):
            load_layer_consts(l)
            for (path, tok0, NT, seq) in units:
                with contextlib.ExitStack() as pu:
                    oT = sbp("oT", [128, 16, NT * 128], BF, pu, (NT * 128, 128))
                    with contextlib.ExitStack() as pm:
                        mixer(l, path, tok0, NT, seq, oT, pm)
                    tr.barrier()
                    wpass1(l, path, tok0, NT, oT)
                tr.barrier()
                wpass2(l, path, tok0, NT)
                tr.barrier()
                if do_peer:
                    for g0 in range(0, NT, 4):
                        peer(l, path, tok0 + g0 * 128, min(4, NT - g0), l == n_layers - 1)
        tr.barrier()
        dma("sp", O["yp"], xres[0:NSEQ * SEQ, :])
        dma("sp", O["ys"], xres[NSEQ * SEQ:NTOK, :])
        tr.final_wait("sp")
        stats = {"ninstr": tr.ninstr, "nsem": tr.nsem}
    return nc, stats


_CONSTS = None


def kernel(**inputs):
    global _CONSTS
    if _CONSTS is None:
        _CONSTS = host_constants()
    f = lambda a: np.ascontiguousarray(np.asarray(a, dtype=np.float32))
    x_prompt = f(inputs["x_prompt"])
    x_sample = f(inputs["x_sample"])
    shared = {
        "w_ada": f(inputs["w_ada"]), "b_ada": f(inputs["b_ada"]), "w_in": f(inputs["w_in"]),
        "g_q": f(inputs["g_q"]), "g_kv": f(inputs["g_kv"]), "w_uq": f(inputs["w_uq"]), "w_uk": f(inputs["w_uk"]),
        "w_uv": f(inputs["w_uv"]), "gqa_sink": f(inputs["gqa_sink"]), "w_pool": f(inputs["w_pool"]),
        "pool_scale": f(inputs["pool_scale"]), "na_rpb": f(inputs["na_rpb"]), "w_out": f(inputs["w_out"]),
        "ln1_g": f(inputs["ln1_g"]), "ln1_b": f(inputs["ln1_b"]), "peer_wq": f(inputs["peer_wq"]),
        "peer_keys": f(inputs["peer_keys"]).reshape(DEPTH, 16, 128, 128),
        "peer_u": f(inputs["peer_u"]), "peer_v": f(inputs["peer_v"]),
        "ln2_g": f(inputs["ln2_g"]), "ln2_b": f(inputs["ln2_b"]),
    }
    shared.update(_CONSTS)
    in_maps = []
    for c in range(NCORES):
        s = c // 4
        m = dict(shared)
        m["xp"] = np.ascontiguousarray(x_prompt[c * NSEQ:(c + 1) * NSEQ].reshape(NSEQ * SEQ, D))
        m["xs"] = np.ascontiguousarray(x_sample[s])
        m["c_ckv"] = f(inputs["cache_mla_ckv"][s])
        m["c_kr"] = f(inputs["cache_mla_krope"][s])
        m["c_gk"] = f(inputs["cache_gqa_k"][s]).reshape(DEPTH, PAST, 128)
        m["c_gv"] = f(inputs["cache_gqa_v"][s]).reshape(DEPTH, PAST, 128)
        m["c_nk"] = f(inputs["cache_na_k"][s]).reshape(DEPTH, PAST, 512)
        m["c_nv"] = f(inputs["cache_na_v"][s]).reshape(DEPTH, PAST, 512)
        m["cond"] = np.ascontiguousarray(np.stack([f(inputs["c_ctx"]), f(inputs["c"])[s]], 0))
        in_maps.append(m)
    nc, _ = build_program(KOPTS)
    res = run_bass_kernel_spmd(nc, in_maps, core_ids=list(range(NCORES)))
    R = res.results
    B = NCORES * NSEQ
    y_prompt = np.concatenate([R[c]["yp"].reshape(NSEQ, SEQ, D) for c in range(NCORES)], 0)
    y_sample = np.stack([R[0]["ys"], R[4]["ys"]], 0)
    cat = lambda k, shp: np.concatenate([R[c][k] for c in range(NCORES)], 0).reshape((B, DEPTH, SEQ) + shp)
    outs = (y_prompt.astype(np.float32), y_sample.astype(np.float32),
            cat("o_ckv", (256,)), cat("o_kr", (64,)), cat("o_gk", (2, 64)), cat("o_gv", (2, 64)),
            cat("o_nk", (8, 64)), cat("o_nv", (8, 64)))
    kernel.last_results = R
    return outs


KOPTS = {}
```

```python
import contextlib
import math

import numpy as np
import concourse.bass as bass
import concourse.mybir as mybir
from concourse.bass_utils import run_bass_kernel_spmd

F32 = mybir.dt.float32
BF = mybir.dt.bfloat16
I32 = mybir.dt.int32
U32 = mybir.dt.uint32
AF = mybir.ActivationFunctionType
ALU = mybir.AluOpType

D = 2048
DEPTH = 4
NCORES = 8
SEQ = 256
NSEQ = 4
DSEQ = 1024
PAST = 256
GRID_W = 64
ALPHA = (2 * DEPTH) ** 0.25
LN_EPS = 1e-5
RMS_EPS = 1e-6
IN_COLS = 3520
NEG = -1.0e30

SEM_LIMIT = 60000
N_DMA_SEMS = 24


class _Sem:
    def __init__(self, tr, name):
        self.tr = tr
        self.name = name
        self.gen = 0
        self.count = 0
        self.handle = tr._new_sem(f"{name}_0")

    def bump(self, inc):
        if self.count + inc > SEM_LIMIT:
            self.gen += 1
            self.count = 0
            self.handle = self.tr._new_sem(f"{self.name}_{self.gen}")
        self.count += inc
        return (self.handle, self.count)


class TR:
    ENG = ("pe", "dve", "act", "pool", "sp")

    def __init__(self, nc, es):
        self.nc = nc
        self.es = es
        self.nsem = 0
        self.eng = {"pe": nc.tensor, "dve": nc.vector, "act": nc.scalar, "pool": nc.gpsimd, "sp": nc.sync}
        self.esem = {e: _Sem(self, e) for e in self.ENG}
        self.dsem = [_Sem(self, f"d{i}") for i in range(N_DMA_SEMS)]
        self.dsem_sw = [_Sem(self, f"w{i}") for i in range(N_DMA_SEMS)]
        self.dnext = 0
        self.dnext_sw = 0
        self.seen = {e: {} for e in self.ENG}
        self.handles = {}
        self.W = {}
        self.R = {}
        self.excl = set()
        self.part = {}
        self.ninstr = 0

    def _new_sem(self, name):
        self.nsem += 1
        return self.es.enter_context(self.nc.semaphore(name))

    def keys(self, x):
        if isinstance(x, (str, tuple)):
            return [x]
        if not hasattr(x, "tensor"):
            name = x.name
            pt = self.part.get(name)
            if pt is None:
                return [name]
            return [(name, i) for i in range((pt[0] + pt[1] - 1) // pt[1])]
        name = x.tensor.name
        pt = self.part.get(name)
        if pt is None:
            return [name]
        period, block = pt
        off = int(x.offset) % period
        n = int(x.ap[-1][1]) if int(x.ap[-1][0]) == 1 else 1
        return [(name, i) for i in range(off // block, (off + n - 1) // block + 1)]

    def _deps(self, reads, writes):
        deps = {}

        def add(d):
            for hid, (h, v) in d.items():
                if hid not in deps or deps[hid][1] < v:
                    deps[hid] = (h, v)

        for r in reads:
            for k in self.keys(r):
                add(self.W.get(k, {}))
                if k[0] in self.excl if isinstance(k, tuple) else k in self.excl:
                    add(self.R.get(k, {}))
        for w in writes:
            for k in self.keys(w):
                add(self.W.get(k, {}))
                add(self.R.get(k, {}))
        return deps

    def _wait(self, e, deps):
        eng = self.eng[e]
        own = id(self.esem[e].handle)
        for hid, (h, v) in deps.items():
            if e == "pe" and hid == own:
                continue
            if self.seen[e].get(hid, 0) >= v:
                continue
            eng.wait_ge(h, v)
            self.ninstr += 1
            self.seen[e][hid] = v

    def _record(self, reads, writes, h, v):
        hid = id(h)
        for r in reads:
            for k in self.keys(r):
                ex = (k[0] in self.excl) if isinstance(k, tuple) else (k in self.excl)
                if ex:
                    self.W.setdefault(k, {})[hid] = (h, v)
                else:
                    self.R.setdefault(k, {})[hid] = (h, v)
        for w in writes:
            for k in self.keys(w):
                self.W.setdefault(k, {})[hid] = (h, v)

    def op(self, e, fn, reads=(), writes=()):
        self._wait(e, self._deps(reads, writes))
        ins = fn(self.eng[e])
        h, v = self.esem[e].bump(1)
        ins.then_inc(h, 1)
        self.ninstr += 1
        self._record(reads, writes, h, v)
        return ins

    def dma(self, q, fn, reads=(), writes=()):
        deps = self._deps(reads, writes)
        if q == "pool":
            s = self.dsem_sw[self.dnext_sw % N_DMA_SEMS]
            self.dnext_sw += 1
        else:
            s = self.dsem[self.dnext % N_DMA_SEMS]
            self.dnext += 1
        if s.count > 0:
            deps.setdefault(id(s.handle), (s.handle, s.count))
        self._wait(q, deps)
        ins = fn(self.eng[q])
        h, v = s.bump(16)
        ins.then_inc(h, 16)
        self.ninstr += 1
        self._record(reads, writes, h, v)
        return ins

    def barrier(self):
        deps = {}
        for s in list(self.esem.values()) + self.dsem + self.dsem_sw:
            if s.count > 0:
                deps[id(s.handle)] = (s.handle, s.count)
        for e in self.ENG:
            self._wait(e, dict(deps))

    def final_wait(self, e="sp"):
        deps = {}
        for s in list(self.esem.values()) + self.dsem + self.dsem_sw:
            if s.count > 0:
                deps[id(s.handle)] = (s.handle, s.count)
        self._wait(e, deps)


class Rot:
    def __init__(self, items):
        self.items = list(items)
        self.i = 0

    def next(self):
        x = self.items[self.i % len(self.items)]
        self.i += 1
        return x


A_IN = 704
CH = [("A1", 0, 384), ("A2", 384, 320), ("B1", 704, 512), ("B2", 1216, 256),
      ("C", 1472, 512), ("D1", 1984, 512), ("D2", 2496, 512), ("D3", 3008, 512)]


def host_constants():
    c = {}
    n = np.arange(DSEQ)
    inv = (10000.0 ** (-np.arange(16, dtype=np.float32) / 16.0)).astype(np.float32)
    ar = (n // GRID_W).astype(np.float32)[:, None] * inv[None, :]
    ac = (n % GRID_W).astype(np.float32)[:, None] * inv[None, :]
    cosr, sinr = np.cos(ar).astype(np.float32), np.sin(ar).astype(np.float32)
    cosc, sinc = np.cos(ac).astype(np.float32), np.sin(ac).astype(np.float32)
    c["rope_cos"] = np.concatenate([cosr, cosr, cosc, cosc], 1).astype(np.float32)
    c["rope_sin"] = np.concatenate([-sinr, sinr, -sinc, sinc], 1).astype(np.float32)
    def band(S):
        out = np.zeros((4, S, S), np.float32)
        for g, w in enumerate((2, 4, 8, 16)):
            for t in range(S):
                lo = min(max(t - w // 2, 0), S)
                hi = min(max(t + w // 2, 0), S)
                out[g, t, lo:hi] = 1.0 / float(hi - lo)
                out[g, t, t] -= 1.0
        return out
    def blocks(S):
        A = band(S)
        nt = S // 128
        o = np.zeros((4, 5, 128, 128), np.float32)
        for g in range(4):
            At = A[g].T
            o[g, 0] = At[0:128, 128:256]
            o[g, 1] = At[0:128, 0:128]
            mid = 1 if nt > 2 else 0
            o[g, 2] = At[mid * 128:(mid + 1) * 128, mid * 128:(mid + 1) * 128]
            o[g, 3] = At[(nt - 1) * 128:, (nt - 1) * 128:]
            o[g, 4] = At[128:256, 0:128]
        return o
    c["pool_at_ctx"] = np.ascontiguousarray(blocks(SEQ).transpose(2, 0, 1, 3).reshape(128, 20, 128))
    c["pool_at_lat"] = np.ascontiguousarray(blocks(DSEQ).transpose(2, 0, 1, 3).reshape(128, 20, 128))
    kk = np.arange(128)[:, None]
    qq = np.arange(128)[None, :]
    c["win_mask"] = np.stack([(kk >= qq), (kk <= qq)], 1).astype(np.float32)
    cols = np.arange(GRID_W)
    cstart = np.clip(cols - 8, 0, GRID_W - 16)
    ok = (cols[None, :] >= cstart[:, None]) & (cols[None, :] < cstart[:, None] + 16)
    okT = ok.T.astype(np.float32)
    c["na_colok"] = np.concatenate([okT, okT], 0)
    return c


CONST_SHAPES = {"rope_cos": [DSEQ, 64], "rope_sin": [DSEQ, 64], "pool_at_ctx": [128, 20, 128],
                "pool_at_lat": [128, 20, 128], "win_mask": [128, 2, 128], "na_colok": [128, 64]}

IN_SHAPES = {
    "xp": [NSEQ * SEQ, D], "xs": [DSEQ, D],
    "c_ckv": [DEPTH, PAST, 256], "c_kr": [DEPTH, PAST, 64], "c_gk": [DEPTH, PAST, 128],
    "c_gv": [DEPTH, PAST, 128], "c_nk": [DEPTH, PAST, 512], "c_nv": [DEPTH, PAST, 512],
    "cond": [2, D],
    "w_ada": [DEPTH, D, 6 * D], "b_ada": [DEPTH, 6 * D], "w_in": [DEPTH, D, IN_COLS],
    "g_q": [DEPTH, 384], "g_kv": [DEPTH, 256], "w_uq": [DEPTH, 384, 768], "w_uk": [DEPTH, 256, 512],
    "w_uv": [DEPTH, 256, 512], "gqa_sink": [DEPTH, 8], "w_pool": [DEPTH, 4, 128, 128],
    "pool_scale": [DEPTH, 512], "na_rpb": [DEPTH, 8, 15, 31], "w_out": [DEPTH, D, D],
    "ln1_g": [DEPTH, D], "ln1_b": [DEPTH, D], "peer_wq": [DEPTH, D, D],
    "peer_keys": [DEPTH, 16, 128, 128], "peer_u": [DEPTH, 16384, D], "peer_v": [DEPTH, 16384, D],
    "ln2_g": [DEPTH, D], "ln2_b": [DEPTH, D],
}
OUT_SHAPES = {
    "yp": [NSEQ * SEQ, D], "ys": [DSEQ, D],
    "o_ckv": [NSEQ, DEPTH, SEQ, 256], "o_kr": [NSEQ, DEPTH, SEQ, 64], "o_gk": [NSEQ, DEPTH, SEQ, 128],
    "o_gv": [NSEQ, DEPTH, SEQ, 128], "o_nk": [NSEQ, DEPTH, SEQ, 512], "o_nv": [NSEQ, DEPTH, SEQ, 512],
}


def build_program(opts=None):
    opts = opts or {}
    n_layers = opts.get("layers", DEPTH)
    do_latent = opts.get("latent", True)
    do_peer = opts.get("peer", True)
    dbg = opts.get("dbg", False)

    nc = bass.Bass("TRN2", target_bir_lowering=False)
    I = {k: nc.dram_tensor(k, s, F32, kind="ExternalInput").ap() for k, s in IN_SHAPES.items()}
    C = {k: nc.dram_tensor(k, s, F32, kind="ExternalInput").ap() for k, s in CONST_SHAPES.items()}
    O = {k: nc.dram_tensor(k, s, F32, kind="ExternalOutput").ap() for k, s in OUT_SHAPES.items()}
    NTOK = NSEQ * SEQ + DSEQ
    xres = nc.dram_tensor("xres", [NTOK, D], F32, kind="Internal").ap()
    hqres = nc.dram_tensor("hqres", [NTOK, D], F32, kind="Internal").ap()
    modd = nc.dram_tensor("modd", [DEPTH, 2, 6 * D], F32, kind="Internal").ap()
    DBG = {}
    if dbg:
        DBG["mod"] = nc.dram_tensor("dbg_mod", [DEPTH, 2, 6 * D], F32, kind="ExternalOutput").ap()
        DBG["x1"] = nc.dram_tensor("dbg_x1", [NTOK, D], F32, kind="ExternalOutput").ap()
        DBG["o"] = nc.dram_tensor("dbg_o", [NTOK, D], F32, kind="ExternalOutput").ap()

    es = contextlib.ExitStack()
    with es:
        tr = TR(nc, es)

        def sb(name, shape, dt, stack=es):
            return stack.enter_context(nc.sbuf_tensor(name, shape, dt))

        psall = es.enter_context(nc.psum_tensor("psall", [128, 4096], F32))
        tr.excl.add(psall[:].tensor.name)
        tr.part[psall[:].tensor.name] = (4096, 512)
        PS = [psall[:, i * 512:(i + 1) * 512] for i in range(8)]
        psg = Rot(PS[0:3])
        pss = Rot(PS[3:6])
        pso = Rot(PS[6:8])

        def mm(out, lhsT, rhs, start, stop):
            tr.op("pe", lambda e: e.matmul(out, lhsT, rhs, start=start, stop=stop), reads=[lhsT, rhs], writes=[out])

        def tp(out, in_, idn):
            tr.op("pe", lambda e: e.transpose(out, in_, idn), reads=[in_, idn], writes=[out])

        def dma(q, out, in_, **kw):
            tr.dma(q, lambda e: e.dma_start(out=out, in_=in_, **kw), reads=[in_], writes=[out])

        def vcopy(eng, out, in_):
            if eng == "act":
                tr.op("act", lambda e: e.copy(out, in_), reads=[in_], writes=[out])
            else:
                tr.op(eng, lambda e: e.tensor_copy(out, in_), reads=[in_], writes=[out])

        def tt(eng, out, a, b, op):
            tr.op(eng, lambda e: e.tensor_tensor(out, a, b, op=op), reads=[a, b], writes=[out])

        def ts(eng, out, a, s1, s2, op0, op1=None, extra_reads=()):
            rd = [a] + [s for s in (s1, s2) if not isinstance(s, (int, float, type(None)))] + list(extra_reads)
            if op1 is None:
                tr.op(eng, lambda e: e.tensor_scalar(out, a, s1, None, op0=op0), reads=rd, writes=[out])
            else:
                tr.op(eng, lambda e: e.tensor_scalar(out, a, s1, s2, op0=op0, op1=op1), reads=rd, writes=[out])

        def stt(out, a, s, b, op0, op1):
            rd = [a, b] + ([] if isinstance(s, (int, float)) else [s])
            tr.op("dve", lambda e: e.scalar_tensor_tensor(out, a, s, b, op0=op0, op1=op1), reads=rd, writes=[out])

        def actf(out, in_, func, scale=1.0, bias=0.0, accum=None):
            rd = [in_] + [s for s in (scale, bias) if not isinstance(s, (int, float))]
            wr = [out] + ([accum] if accum is not None else [])
            kw = {}
            if accum is not None:
                kw["accum_out"] = accum
            tr.op("act", lambda e: e.activation(out=out, in_=in_, func=func, bias=bias, scale=scale, **kw),
                  reads=rd, writes=wr)

        evq = Rot(["dve", "act"])

        def evac(out, in_):
            vcopy(evq.next(), out, in_)

        ident = sb("ident", [128, 128], F32)
        tr.op("pool", lambda e: e.memset(ident[:], 0.0), writes=[ident])
        tr.op("pool", lambda e: e.affine_select(out=ident[:], in_=ident[:], pattern=[[-1, 128]],
                                                 compare_op=ALU.not_equal, fill=1.0, base=0,
                                                 channel_multiplier=1), reads=[ident], writes=[ident])
        iota_i = sb("iota_i", [128, 256], I32)
        iota256 = sb("iota256", [128, 256], F32)
        tr.op("pool", lambda e: e.iota(iota_i[:], pattern=[[1, 256]], base=0, channel_multiplier=0), writes=[iota_i])
        tr.op("pool", lambda e: e.tensor_copy(iota256[:], iota_i[:]), reads=[iota_i], writes=[iota256])
        epsc = sb("epsc", [128, 2], F32)
        tr.op("pool", lambda e: e.memset(epsc[:, 0:1], RMS_EPS), writes=[epsc])
        tr.op("pool", lambda e: e.memset(epsc[:, 1:2], LN_EPS), writes=[epsc])

        ub16 = nc.dram_tensor("ub16", [DEPTH * 16384, D], BF, kind="Internal").ap()
        vb16 = nc.dram_tensor("vb16", [DEPTH * 16384, D], BF, kind="Internal").ap()
        if do_peer:
            for l_ in range(n_layers):
                for src_, dst_ in ((I["peer_u"], ub16), (I["peer_v"], vb16)):
                    for r0 in range(0, 16384, 2048):
                        dma("pool", dst_[l_ * 16384 + r0: l_ * 16384 + r0 + 2048, :], src_[l_, r0:r0 + 2048, :])
        dma("sp", xres[0:NSEQ * SEQ, :], I["xp"])
        dma("sp", xres[NSEQ * SEQ:NTOK, :], I["xs"])

        with contextlib.ExitStack() as ph:
            cond_sb = sb("cond_sb", [2, D], F32, ph)
            sil = sb("sil", [2, D], F32, ph)
            sT = sb("sT", [128, 16, 2], BF, ph)
            wa = [sb(f"wa{i}", [128, 16, 512], BF, ph) for i in range(2)]
            msb = [sb(f"msb{i}", [2, 2048], F32, ph) for i in range(2)]
            bsb = [sb(f"bsb{i}", [2, 2048], F32, ph) for i in range(2)]
            dma("sp", cond_sb[:], I["cond"])
            actf(sil[:], cond_sb[:], AF.Silu)
            ps = psg.next()
            for kc in range(16):
                tp(ps[:, kc * 2:(kc + 1) * 2], sil[0:2, kc * 128:(kc + 1) * 128], ident[0:2, 0:2])
            vcopy("dve", sT[:].rearrange("p a b -> p (a b)"), ps[:, 0:32])
            ci = 0
            for l in range(n_layers):
                for g in range(6):
                    bs, ms = bsb[g % 2], msb[g % 2]
                    dma("sp", bs[:], I["b_ada"][l, g * 2048:(g + 1) * 2048].partition_broadcast(2))
                    for j in range(4):
                        c0 = (g * 4 + j) * 512
                        w = wa[ci % 2]
                        ci += 1
                        dma("pool", w[:], I["w_ada"][l, :, c0:c0 + 512].rearrange("(kc p) c -> p kc c", p=128))
                        ps = psg.next()
                        for kc in range(16):
                            mm(ps[0:2, :], sT[:, kc, :], w[:, kc, :], kc == 0, kc == 15)
                        tt("dve", ms[:, j * 512:(j + 1) * 512], ps[0:2, :], bs[:, j * 512:(j + 1) * 512], ALU.add)
                    dma("sp", modd[l, :, g * 2048:(g + 1) * 2048], ms[:])
                    if dbg:
                        dma("sp", DBG["mod"][l, :, g * 2048:(g + 1) * 2048], ms[:])
        tr.barrier()

        uid = [0]

        def sbp(name, shape, dt, stack, part=None):
            uid[0] += 1
            t = stack.enter_context(nc.sbuf_tensor(f"{name}_{uid[0]}", shape, dt))
            if part is not None:
                tr.part[t[:].tensor.name] = part
            return t

        modcol = sb("modcol", [128, 2, 2, 16], F32)
        gqcol = sb("gqcol", [128, 3], F32)
        pscol = sb("pscol", [128, 4], F32)
        gkvb = sb("gkvb", [128, 256], F32)
        esink = sb("esink", [128, 8], F32)
        wuqn = sb("wuqn", [128, 3, 512], BF)
        wuqr = sb("wuqr", [128, 3, 256], BF)
        wuk = sb("wuk", [128, 2, 512], BF)
        wuv = sb("wuv", [128, 2, 512], BF)
        wpool = sb("wpool", [128, 4, 128], BF)
        ropec = sb("ropec", [128, 8, 64], F32)
        ropes = sb("ropes", [128, 8, 64], F32)
        winm = sb("winm", [128, 2, 128], BF)
        Ebuf = [sb(f"Ebuf{i}", [128, 512], BF) for i in range(3)]
        Erot = Rot(Ebuf)
        small = sb("small", [128, 64], F32)
        if do_latent:
            dma("sp", ropec[:], C["rope_cos"].rearrange("(t p) c -> p t c", p=128))
            dma("sp", ropes[:], C["rope_sin"].rearrange("(t p) c -> p t c", p=128))
            dma("pool", winm[:], C["win_mask"])

        def load_layer_consts(l):
            with nc.allow_non_contiguous_dma(reason="tiny per-layer vectors in column layout"):
                for path in range(2):
                    for j, k in enumerate((0, 1)):
                        dma("sp", modcol[:, path, j, :], modd[l, path, k * D:(k + 1) * D].rearrange("(c p) -> p c", p=128))
                dma("sp", gqcol[:], I["g_q"][l].rearrange("(c p) -> p c", p=128))
                dma("sp", pscol[:], I["pool_scale"][l].rearrange("(c p) -> p c", p=128))
            ts("dve", modcol[:, :, 1, :], modcol[:, :, 1, :], 1.0, None, ALU.add)
            dma("sp", gkvb[:], I["g_kv"][l].partition_broadcast(128))
            dma("sp", esink[:], I["gqa_sink"][l].partition_broadcast(128))
            actf(esink[:], esink[:], AF.Exp)
            for c3 in range(3):
                srcq = I["w_uq"][l, c3 * 128:(c3 + 1) * 128, :].rearrange("p (h d) -> p h d", h=4)
                dma("pool", wuqn[:, c3, :].rearrange("p (h d) -> p h d", h=4), srcq[:, :, 0:128])
                dma("pool", wuqr[:, c3, :].rearrange("p (h d) -> p h d", h=4), srcq[:, :, 128:192])
            dma("pool", wuk[:], I["w_uk"][l].rearrange("(c p) n -> p c n", p=128))
            dma("pool", wuv[:], I["w_uv"][l].rearrange("(c p) n -> p c n", p=128))
            dma("pool", wpool[:], I["w_pool"][l].rearrange("g c e -> c g e"))

        ropetmp = sb("ropetmp", [128, 8, 2, 16], F32)

        def rope(x, H, ti):
            xv = x.rearrange("p (h a b c) -> p h a b c", h=H, a=2, b=2)
            for a in range(2):
                xa = xv[:, :, a, :, :]
                xsw = xa[:, :, ::-1, :]
                cs = ropec[:, ti, a * 32:(a + 1) * 32].rearrange("p (b c) -> p b c", b=2).unsqueeze(1).to_broadcast([128, H, 2, 16])
                sn = ropes[:, ti, a * 32:(a + 1) * 32].rearrange("p (b c) -> p b c", b=2).unsqueeze(1).to_broadcast([128, H, 2, 16])
                tmp = ropetmp[:, 0:H, :, :]
                tt("dve", tmp, xsw, sn, ALU.mult)
                tt("dve", xa, xa, cs, ALU.mult)
                tt("dve", xa, xa, tmp, ALU.add)

        def attend(q_ops, k_ops, v_op, kblocks, dv, scale, out_ap, sink_ap=None, mask_fn=None):
            Ob = pso.next()
            Oap = Ob[:, 0:dv + 1]
            nkb = len(kblocks)
            done = 0
            for c0 in range(0, nkb, 4):
                chunk = kblocks[c0:c0 + 4]
                Sb = pss.next()
                for si, kb in enumerate(chunk):
                    ko = k_ops(kb)
                    for oi, (kq, qq) in enumerate(zip(ko, q_ops)):
                        mm(Sb[:, si * 128:(si + 1) * 128], kq, qq, oi == 0, oi == len(ko) - 1)
                E = Erot.next()
                n = len(chunk) * 128
                actf(E[:, 0:n], Sb[:, 0:n], AF.Exp, scale=scale)
                for si, kb in enumerate(chunk):
                    Es = E[:, si * 128:(si + 1) * 128]
                    if mask_fn is not None:
                        mask_fn(kb, Es)
                    mm(Oap, Es, v_op(kb), done == 0, done == nkb - 1)
                    done += 1
            den = small[:, 0:1]
            if sink_ap is not None:
                tt("dve", den, Ob[:, dv:dv + 1], sink_ap, ALU.add)
            else:
                vcopy("dve", den, Ob[:, dv:dv + 1])
            tr.op("dve", lambda e: e.reciprocal(small[:, 1:2], den), reads=[small], writes=[small])
            ts("dve", out_ap, Ob[:, 0:dv], small[:, 1:2], None, ALU.mult)

        def mixer(l, path, tok0, NT, seq, oT, ph):
            S = NT * 128
            L = PAST if path == 1 else 0
            LB = L // 128
            NK = L + S
            KB = NK // 128
            hT = sbp("hT", [128, 16, S], BF, ph, (S, 128))
            wb = [sbp(f"wb{i}", [128, 16, 512], BF, ph) for i in range(2)]
            wrot = Rot(wb)
            s1 = contextlib.ExitStack()
            xt = [sbp(f"xt{i}", [128, D], F32, s1) for i in range(2)]
            for ti in range(NT):
                x = xt[ti % 2]
                dma("sp", x[:], xres[tok0 + ti * 128: tok0 + (ti + 1) * 128, :])
                for q4 in range(4):
                    ps = psg.next()
                    for j in range(4):
                        kc = q4 * 4 + j
                        tp(ps[:, j * 128:(j + 1) * 128], x[:, kc * 128:(kc + 1) * 128], ident[:])
                    for j in range(4):
                        kc = q4 * 4 + j
                        dst = hT[:, kc, ti * 128:(ti + 1) * 128]
                        src = ps[:, j * 128:(j + 1) * 128]
                        if (j % 2) == 0:
                            ts("dve", dst, src, modcol[:, path, 1, kc:kc + 1], modcol[:, path, 0, kc:kc + 1], ALU.mult, ALU.add)
                        else:
                            actf(dst, src, AF.Identity, scale=modcol[:, path, 1, kc:kc + 1], bias=modcol[:, path, 0, kc:kc + 1])
            tr.barrier()
            s1.close()

            def in_proj(chunk_idx, handler):
                _, c0, w = CH[chunk_idx]
                wt = wrot.next()
                dma("pool", wt[:, :, 0:w], I["w_in"][l, :, c0:c0 + w].rearrange("(kc p) c -> p kc c", p=128))
                for ti in range(NT):
                    ps = psg.next()
                    for kc in range(16):
                        mm(ps[:, 0:w], hT[:, kc, ti * 128:(ti + 1) * 128], wt[:, kc, 0:w], kc == 0, kc == 15)
                    handler(ti, ps)

            def rms_rstd(ps_ap, n, col):
                junk = Erot.next()
                actf(junk[:, 0:n], ps_ap, AF.Square, accum=small[:, col:col + 1])
                ts("dve", small[:, col:col + 1], small[:, col:col + 1], 1.0 / n, None, ALU.mult)
                actf(small[:, col:col + 1], small[:, col:col + 1], AF.Sqrt, bias=epsc[:, 0:1])
                tr.op("dve", lambda e: e.reciprocal(small[:, col:col + 1], small[:, col:col + 1]), reads=[small], writes=[small])

            def transpose_into(dst_fn, src, nchunks, scale_col=None):
                for c0 in range(0, nchunks, 4):
                    ps = psg.next()
                    n = min(4, nchunks - c0)
                    for j in range(n):
                        tp(ps[:, j * 128:(j + 1) * 128], src[:, (c0 + j) * 128:(c0 + j + 1) * 128], ident[:])
                    for j in range(n):
                        if scale_col is not None:
                            ts("dve", dst_fn(c0 + j), ps[:, j * 128:(j + 1) * 128], scale_col[:, c0 + j:c0 + j + 1], None, ALU.mult)
                        else:
                            evac(dst_fn(c0 + j), ps[:, j * 128:(j + 1) * 128])

            with contextlib.ExitStack() as ga:
                cqT = sbp("cqT", [128, 3, S], BF, ga, (S, 128))
                ckvT = sbp("ckvT", [128, 2, NK], BF, ga, (NK, 128))
                krT = sbp("krT", [128, NK], BF, ga, (NK, 128))
                QnT = sbp("QnT", [128, 4, S], BF, ga, (S, 128))
                QrT = sbp("QrT", [128, 2, S], BF, ga, (S, 128))
                KnT = sbp("KnT", [128, 4, NK], BF, ga, (NK, 128))
                VA = sbp("VA", [128, KB, 4, 130], BF, ga, (KB * 520, 520))
                og = sbp("ogA", [128, NT, 512], F32, ga, (NT * 512, 512))
                tmpA = [sbp(f"tmpA{i}", [128, 384], F32, ga) for i in range(2)]
                ckvs = [sbp(f"ckvs{i}", [128, 256], F32, ga) for i in range(2)]
                kr2 = [sbp(f"kr2{i}", [128, 128], F32, ga) for i in range(2)]
                qrs = [sbp(f"qrs{i}", [128, 256], F32, ga) for i in range(2)]
                tr.op("pool", lambda e: e.memset(VA[:, :, :, 128:129], 1.0), writes=["VAones"])

                def hA1(ti, ps):
                    rms_rstd(ps[:, 0:384], 384, 2)
                    t = tmpA[ti % 2]
                    ts("dve", t[:], ps[:, 0:384], small[:, 2:3], None, ALU.mult)
                    transpose_into(lambda c: cqT[:, c, ti * 128:(ti + 1) * 128], t, 3, scale_col=gqcol)

                def ckv_finish(cs, k2, col0, ti_rope):
                    if ti_rope is not None:
                        rope(k2[:, 0:64], 1, ti_rope)
                    vcopy("pool", k2[:, 64:128], k2[:, 0:64])
                    transpose_into(lambda c: ckvT[:, c, col0:col0 + 128], cs, 2)
                    transpose_into(lambda c: krT[:, col0:col0 + 128], k2, 1)

                def hA2(ti, ps):
                    rms_rstd(ps[:, 0:256], 256, 3)
                    cs, k2 = ckvs[ti % 2], kr2[ti % 2]
                    stt(cs[:], ps[:, 0:256], small[:, 3:4], gkvb[:], ALU.mult, ALU.mult)
                    vcopy("act", k2[:, 0:64], ps[:, 256:320])
                    if path == 0:
                        dma("sp", O["o_ckv"][seq, l, ti * 128:(ti + 1) * 128, :], cs[:])
                        dma("sp", O["o_kr"][seq, l, ti * 128:(ti + 1) * 128, :], k2[:, 0:64])
                    ckv_finish(cs, k2, L + ti * 128, ti if path == 1 else None)

                in_proj(0, hA1)
                in_proj(1, hA2)
                for b in range(LB):
                    cs, k2 = ckvs[b % 2], kr2[b % 2]
                    dma("sp", cs[:], I["c_ckv"][l, b * 128:(b + 1) * 128, :])
                    dma("sp", k2[:, 0:64], I["c_kr"][l, b * 128:(b + 1) * 128, :])
                    ckv_finish(cs, k2, b * 128, None)
                for h in range(4):
                    for k0 in range(0, NK, 512):
                        n = min(512, NK - k0)
                        ps = psg.next()
                        for c in range(2):
                            mm(ps[:, 0:n], wuk[:, c, h * 128:(h + 1) * 128], ckvT[:, c, k0:k0 + n], c == 0, c == 1)
                        evac(KnT[:, h, k0:k0 + n], ps[:, 0:n])
                for kb in range(KB):
                    ps = psg.next()
                    for c in range(2):
                        mm(ps[:, 0:512], ckvT[:, c, kb * 128:(kb + 1) * 128], wuv[:, c, :], c == 0, c == 1)
                    tr.op(evq.next(), (lambda kb, ps: (lambda e: (e.copy if e is nc.scalar else e.tensor_copy)(
                        VA[:, kb, :, 0:128], ps[:, 0:512].rearrange("p (h d) -> p h d", h=4))))(kb, ps),
                        reads=[ps, "VAones"], writes=[VA[:, kb, 0, :]])
                for h in range(4):
                    for t0 in range(0, S, 512):
                        n = min(512, S - t0)
                        ps = psg.next()
                        for c in range(3):
                            mm(ps[:, 0:n], wuqn[:, c, h * 128:(h + 1) * 128], cqT[:, c, t0:t0 + n], c == 0, c == 2)
                        evac(QnT[:, h, t0:t0 + n], ps[:, 0:n])
                for ti in range(NT):
                    ps = psg.next()
                    for c in range(3):
                        mm(ps[:, 0:256], cqT[:, c, ti * 128:(ti + 1) * 128], wuqr[:, c, :], c == 0, c == 2)
                    qr = qrs[ti % 2]
                    evac(qr[:], ps[:, 0:256])
                    if path == 1:
                        rope(qr[:], 4, ti)
                    transpose_into(lambda c: QrT[:, c, ti * 128:(ti + 1) * 128], qr, 2)
                scl = 192.0 ** -0.5
                for qt in range(NT):
                    for h in range(4):
                        j, e2 = h // 2, h % 2
                        qs = slice(qt * 128, (qt + 1) * 128)
                        attend([QnT[:, h, qs], QrT[e2 * 64:(e2 + 1) * 64, j, qs]],
                               lambda kb, h=h, e2=e2: [KnT[:, h, kb * 128:(kb + 1) * 128], krT[e2 * 64:(e2 + 1) * 64, kb * 128:(kb + 1) * 128]],
                               lambda kb, h=h: VA[:, kb, h, 0:129],
                               list(range(KB)), 128, scl, og[:, qt, h * 128:(h + 1) * 128])
                for qt in range(NT):
                    transpose_into(lambda c: oT[:, c, qt * 128:(qt + 1) * 128], og[:, qt, :], 4)
                    if dbg:
                        dma("sp", DBG["o"][tok0 + qt * 128: tok0 + (qt + 1) * 128, 0:512], og[:, qt, :])
                tr.barrier()

            with contextlib.ExitStack() as gb:
                QTB = sbp("QTB", [128, 4, S], BF, gb, (S, 128))
                KTB = sbp("KTB", [128, 2, NK], BF, gb, (NK, 128))
                VB = sbp("VB", [128, KB, 2, 66], BF, gb, (KB * 132, 132))
                og = sbp("ogB", [128, NT, 512], F32, gb, (NT * 512, 512))
                gqs = [sbp(f"gqs{i}", [128, 512], F32, gb) for i in range(2)]
                gks = [sbp(f"gks{i}", [128, 256], F32, gb) for i in range(2)]
                gvs = [sbp(f"gvs{i}", [128, 128], F32, gb) for i in range(2)]
                tr.op("pool", lambda e: e.memset(VB[:, :, :, 64:65], 1.0), writes=["VBones"])

                def hB1(ti, ps):
                    gq = gqs[ti % 2]
                    evac(gq[:], ps[:, 0:512])
                    if path == 1:
                        rope(gq[:], 8, ti)
                    transpose_into(lambda c: QTB[:, c, ti * 128:(ti + 1) * 128], gq, 4)

                def kvB(gk, gv, col0, kb):
                    vcopy("pool", gk[:, 128:192], gk[:, 64:128])
                    vcopy("pool", gk[:, 192:256], gk[:, 0:64])
                    transpose_into(lambda c: KTB[:, c, col0:col0 + 128], gk, 2)
                    tr.op("dve", lambda e: e.tensor_copy(VB[:, kb, :, 0:64], gv[:].rearrange("p (h d) -> p h d", h=2)),
                          reads=[gv, "VBones"], writes=[VB[:, kb, 0, :]])

                def hB2(ti, ps):
                    gk, gv = gks[ti % 2], gvs[ti % 2]
                    evac(gk[:, 0:128], ps[:, 0:128])
                    evac(gv[:], ps[:, 128:256])
                    if path == 0:
                        dma("sp", O["o_gk"][seq, l, ti * 128:(ti + 1) * 128, :], gk[:, 0:128])
                        dma("sp", O["o_gv"][seq, l, ti * 128:(ti + 1) * 128, :], gv[:])
                    else:
                        rope(gk[:, 0:128], 2, ti)
                    kvB(gk, gv, L + ti * 128, LB + ti)

                in_proj(2, hB1)
                in_proj(3, hB2)
                for b in range(LB):
                    gk, gv = gks[b % 2], gvs[b % 2]
                    dma("sp", gk[:, 0:128], I["c_gk"][l, b * 128:(b + 1) * 128, :])
                    dma("sp", gv[:], I["c_gv"][l, b * 128:(b + 1) * 128, :])
                    kvB(gk, gv, b * 128, b)
                for qt in range(NT):
                    if path == 0:
                        kbl = list(range(KB))
                    else:
                        kbl = list(range(LB)) + [LB + t for t in (qt - 1, qt, qt + 1) if 0 <= t < NT]

                    def mfn(kb, Es, qt=qt):
                        if path == 0 or kb < LB or kb == LB + qt:
                            return
                        i = 0 if kb == LB + qt - 1 else 1
                        tt("pool", Es, Es, winm[:, i, :], ALU.mult)

                    for h in range(8):
                        j, e2, kv = h // 2, h % 2, h // 4
                        a = 0 if e2 == kv else 1
                        qs = slice(qt * 128, (qt + 1) * 128)
                        attend([QTB[e2 * 64:(e2 + 1) * 64, j, qs]],
                               lambda kb, a=a, e2=e2: [KTB[e2 * 64:(e2 + 1) * 64, a, kb * 128:(kb + 1) * 128]],
                               lambda kb, kv=kv: VB[:, kb, kv, 0:65],
                               kbl, 64, 0.125, og[:, qt, h * 64:(h + 1) * 64],
                               sink_ap=esink[:, h:h + 1], mask_fn=mfn)
                for qt in range(NT):
                    transpose_into(lambda c: oT[:, 4 + c, qt * 128:(qt + 1) * 128], og[:, qt, :], 4)
                    if dbg:
                        dma("sp", DBG["o"][tok0 + qt * 128: tok0 + (qt + 1) * 128, 512:1024], og[:, qt, :])
                tr.barrier()

            with contextlib.ExitStack() as gc:
                zc = sbp("zc", [128, NT, 512], F32, gc, (NT * 512, 512))
                AT = sbp("AT", [128, 20, 128], F32, gc)
                dlt = [sbp(f"dlt{i}", [128, 128], BF, gc) for i in range(2)]
                dma("sp", AT[:], C["pool_at_lat" if path == 1 else "pool_at_ctx"])
                in_proj(4, lambda ti, ps: evac(zc[:, ti, :], ps[:, 0:512]))
                di = 0
                for ti in range(NT):
                    for g in range(4):
                        nb = []
                        if ti > 0:
                            nb.append((ti - 1, 0))
                        nb.append((ti, 1 if ti == 0 else (3 if ti == NT - 1 else 2)))
                        if ti < NT - 1:
                            nb.append((ti + 1, 4))
                        ps = psg.next()
                        for i, (st, kind) in enumerate(nb):
                            mm(ps[:, 0:128], zc[:, st, g * 128:(g + 1) * 128], AT[:, g * 5 + kind, :], i == 0, i == len(nb) - 1)
                        d = dlt[di % 2]
                        di += 1
                        evac(d[:], ps[:, 0:128])
                        mm(ps[:, 128:256], wpool[:, g, :], d[:], True, True)
                        ts("dve", oT[:, 8 + g, ti * 128:(ti + 1) * 128], ps[:, 128:256], pscol[:, g:g + 1], None, ALU.mult)
                tr.barrier()

            with contextlib.ExitStack() as gd:
                Tm = build_tm(l, gd) if path == 1 else None
                QTD = sbp("QTD", [128, 4, S], BF, gd, (S, 128))
                KTD = sbp("KTD", [128, 4, NK], BF, gd, (NK, 128))
                VD = sbp("VD", [128, KB, 8, 66], BF, gd, (KB * 528, 528))
                og = sbp("ogD", [128, NT, 512], F32, gd, (NT * 512, 512))
                nqs = [sbp(f"nqs{i}", [128, 512], F32, gd) for i in range(2)]
                nks = [sbp(f"nks{i}", [128, 512], F32, gd) for i in range(2)]
                nvs = [sbp(f"nvs{i}", [128, 512], F32, gd) for i in range(2)]
                tr.op("pool", lambda e: e.memset(VD[:, :, :, 64:65], 1.0), writes=["VDones"])

                def hD1(ti, ps):
                    nq = nqs[ti % 2]
                    evac(nq[:], ps[:, 0:512])
                    transpose_into(lambda c: QTD[:, c, ti * 128:(ti + 1) * 128], nq, 4)

                def kD(nk, col0):
                    transpose_into(lambda c: KTD[:, c, col0:col0 + 128], nk, 4)

                def vD(nv, kb):
                    tr.op("dve", lambda e: e.tensor_copy(VD[:, kb, :, 0:64], nv[:].rearrange("p (h d) -> p h d", h=8)),
                          reads=[nv, "VDones"], writes=[VD[:, kb, 0, :]])

                def hD2(ti, ps):
                    nk = nks[ti % 2]
                    evac(nk[:], ps[:, 0:512])
                    if path == 0:
                        dma("sp", O["o_nk"][seq, l, ti * 128:(ti + 1) * 128, :], nk[:])
                    kD(nk, L + ti * 128)

                def hD3(ti, ps):
                    nv = nvs[ti % 2]
                    evac(nv[:], ps[:, 0:512])
                    if path == 0:
                        dma("sp", O["o_nv"][seq, l, ti * 128:(ti + 1) * 128, :], nv[:])
                    vD(nv, LB + ti)

                in_proj(5, hD1)
                in_proj(6, hD2)
                in_proj(7, hD3)
                for b in range(LB):
                    nk, nv = nks[b % 2], nvs[b % 2]
                    dma("sp", nk[:], I["c_nk"][l, b * 128:(b + 1) * 128, :])
                    dma("sp", nv[:], I["c_nv"][l, b * 128:(b + 1) * 128, :])
                    kD(nk, b * 128)
                    vD(nv, b)
                rst = lambda r: min(max(r - 4, 0), 8)
                for qt in range(NT):
                    if path == 0:
                        kbl = list(range(KB))
                    else:
                        k_lo = rst(2 * qt) // 2
                        k_hi = (rst(2 * qt + 1) + 7) // 2
                        kbl = list(range(LB)) + [LB + t for t in range(k_lo, k_hi + 1)]
                    for h in range(8):
                        j, e2 = h // 2, h % 2
                        qs = slice(qt * 128, (qt + 1) * 128)

                        def mfn(kb, Es, qt=qt, h=h):
                            if path == 0 or kb < LB:
                                return
                            kt = kb - LB
                            for kr in range(2):
                                for qr in range(2):
                                    rk, rq = 2 * kt + kr, 2 * qt + qr
                                    quad = Es[kr * 64:(kr + 1) * 64, qr * 64:(qr + 1) * 64]
                                    if rst(rq) <= rk <= rst(rq) + 7:
                                        dr = rk - rq + 7
                                        tt("pool" if kr == 0 else "dve", quad, quad, Tm[kr * 64:(kr + 1) * 64, h * 15 + dr, :], ALU.mult)
                                    else:
                                        tr.op("pool", lambda e, quad=quad: e.memset(quad, 0.0), writes=[quad])

                        attend([QTD[e2 * 64:(e2 + 1) * 64, j, qs]],
                               lambda kb, j=j, e2=e2: [KTD[e2 * 64:(e2 + 1) * 64, j, kb * 128:(kb + 1) * 128]],
                               lambda kb, h=h: VD[:, kb, h, 0:65],
                               kbl, 64, 0.125, og[:, qt, h * 64:(h + 1) * 64], mask_fn=mfn)
                for qt in range(NT):
                    transpose_into(lambda c: oT[:, 12 + c, qt * 128:(qt + 1) * 128], og[:, qt, :], 4)
                    if dbg:
                        dma("sp", DBG["o"][tok0 + qt * 128: tok0 + (qt + 1) * 128, 1536:2048], og[:, qt, :])
                tr.barrier()

        colok = sb("colok", [128, 64], F32)
        YW = 64 * 97
        ysc = nc.dram_tensor("ysc", [120, YW], F32, kind="Internal").ap()
        if do_latent:
            dma("sp", colok[:], C["na_colok"])

        def build_tm(l, stack):
            Tm = sbp("Tm", [128, 120, 64], BF, stack)
            with contextlib.ExitStack() as ph:
                rows = sbp("rpbrows", [120, 31], F32, ph)
                rr = sbp("rpbrev", [120, 31], F32, ph)
                zt = sbp("zt", [120, YW], F32, ph)
                traw = sbp("traw", [128, 120, 64], F32, ph)
                dma("sp", rows[:], I["na_rpb"][l].rearrange("h a b -> (h a) b"))
                vcopy("dve", rr[:], rows[:, ::-1])
                tr.op("pool", lambda e: e.memset(zt[:], 0.0), writes=[zt])
                dma("sp", ysc, zt[:])
                with nc.allow_non_contiguous_dma(reason="toeplitz skew"):
                    dma("sp", ysc.rearrange("a (k c) -> a k c", c=97)[:, :, 0:31],
                        rr[:].unsqueeze(1).to_broadcast([120, 64, 31]))
                    src = ysc[:, 0:64 * 96].rearrange("a (k c) -> k a c", c=96)[:, :, 15:79]
                    dma("sp", traw[0:64, :, :], src)
                    dma("sp", traw[64:128, :, :], src)
                actf(traw[:], traw[:], AF.Exp)
                tt("dve", Tm[:], traw[:], colok[:].unsqueeze(1).to_broadcast([128, 120, 64]), ALU.mult)
                tr.barrier()
            return Tm

        def layer_norm_tile(x, out, lng, lnb, scratch):
            st = scratch["st"]
            for c in range(4):
                tr.op("dve", lambda e, c=c: e.bn_stats(st[:, c, :], x[:, c * 512:(c + 1) * 512]), reads=[x], writes=[st])
            mv = scratch["mv"]
            tr.op("dve", lambda e: e.bn_aggr(mv[:, 0:2], st[:].rearrange("p a b -> p (a b)")), reads=[st], writes=[mv])
            actf(mv[:, 2:3], mv[:, 1:2], AF.Sqrt, bias=epsc[:, 1:2])
            tr.op("dve", lambda e: e.reciprocal(mv[:, 3:4], mv[:, 2:3]), reads=[mv], writes=[mv])
            ts("dve", out[:], x[:], mv[:, 0:1], mv[:, 3:4], ALU.subtract, ALU.mult)
            tt("pool", out[:], out[:], lng[:], ALU.mult)
            tt("dve", out[:], out[:], lnb[:], ALU.add)

        def wpass1(l, path, tok0, NT, oT):
            with contextlib.ExitStack() as ph:
                wo = sbp("wo", [128, 16, D], BF, ph, (16 * D, D * 4))
                for c in range(4):
                    dma("pool", wo[:, 4 * c:4 * c + 4, :], I["w_out"][l, c * 512:(c + 1) * 512, :].rearrange("(kc p) n -> p kc n", p=128))
                g1b = sbp("g1b", [128, D], F32, ph)
                dma("sp", g1b[:], modd[l, path, 2 * D:3 * D].partition_broadcast(128))
                xt = [sbp(f"w1x{i}", [128, D], F32, ph) for i in range(2)]
                x1p = [sbp(f"w1p{i}", [128, D], F32, ph) for i in range(2)]
                tmp = sbp("w1t", [128, 512], F32, ph)
                for ti in range(NT):
                    x, xp_ = xt[ti % 2], x1p[ti % 2]
                    rows = slice(tok0 + ti * 128, tok0 + (ti + 1) * 128)
                    dma("sp", x[:], xres[rows, :])
                    for cc in range(4):
                        cs = slice(cc * 512, (cc + 1) * 512)
                        ps = psg.next()
                        for mc in range(16):
                            mm(ps[:], oT[:, mc, ti * 128:(ti + 1) * 128], wo[:, mc, cs], mc == 0, mc == 15)
                        tt("dve", tmp[:], ps[:], g1b[:, cs], ALU.mult)
                        stt(xp_[:, cs], x[:, cs], ALPHA, tmp[:], ALU.mult, ALU.add)
                    dma("sp", xres[rows, :], xp_[:])

        def wpass2(l, path, tok0, NT):
            with contextlib.ExitStack() as ph:
                lng = sbp("lng", [128, D], F32, ph)
                lnb = sbp("lnb", [128, D], F32, ph)
                sc2b = sbp("sc2b", [128, D], F32, ph)
                sh2b = sbp("sh2b", [128, D], F32, ph)
                dma("sp", lng[:], I["ln1_g"][l].partition_broadcast(128))
                dma("sp", lnb[:], I["ln1_b"][l].partition_broadcast(128))
                dma("sp", sh2b[:], modd[l, path, 3 * D:4 * D].partition_broadcast(128))
                dma("sp", sc2b[:], modd[l, path, 4 * D:5 * D].partition_broadcast(128))
                ts("pool", sc2b[:], sc2b[:], 1.0, None, ALU.add)
                scr = {"st": sbp("lnst", [128, 4, 6], F32, ph), "mv": sbp("lnmv", [128, 4], F32, ph)}
                xt = [sbp(f"w2x{i}", [128, D], F32, ph) for i in range(2)]
                hq = [sbp(f"w2h{i}", [128, D], F32, ph) for i in range(2)]
                for ti in range(NT):
                    x, h = xt[ti % 2], hq[ti % 2]
                    rows = slice(tok0 + ti * 128, tok0 + (ti + 1) * 128)
                    dma("sp", x[:], xres[rows, :])
                    layer_norm_tile(x, x, lng, lnb, scr)
                    dma("sp", xres[rows, :], x[:])
                    if dbg:
                        dma("sp", DBG["x1"][rows, :], x[:])
                    tt("pool", h[:], x[:], sc2b[:], ALU.mult)
                    tt("dve", h[:], h[:], sh2b[:], ALU.add)
                    dma("sp", hqres[rows, :], h[:])

        def peer(l, path, tok0, G, last):
            GT = G * 128
            with contextlib.ExitStack() as ph:
                eidx = sbp("eidx", [128, G, 128], I32, ph)
                gwt = sbp("gwt", [128, G, 128], F32, ph)
                with contextlib.ExitStack() as p1:
                    hqT = sbp("hqT", [128, 16, GT], F32, p1, (GT, 128))
                    qT = sbp("qT", [128, 16, GT], F32, p1)
                    wq = [sbp(f"wq{i}", [128, 16, 128], F32, p1) for i in range(2)]
                    hq = [sbp(f"phq{i}", [128, D], F32, p1) for i in range(2)]
                    ssb = sbp("ssb", [128, 16, 128], F32, p1, (2048, 128))
                    v16 = sbp("v16", [128, 16, 16], F32, p1)
                    i16 = sbp("i16", [128, 16, 16], U32, p1)
                    i16f = sbp("i16f", [128, 16, 16], F32, p1)
                    cand = sbp("cand", [128, 8, 256], F32, p1)
                    eix = sbp("eix", [128, 8, 256], F32, p1)
                    scr = sbp("pscr", [128, 256], F32, p1)
                    sc16 = sbp("sc16", [128, 8, 16], F32, p1)
                    ci16 = sbp("ci16", [128, 8, 16], U32, p1)
                    cif = sbp("cif", [128, 8, 16], F32, p1)
                    ef = sbp("ef", [128, 128], F32, p1)
                    gs = sbp("gs", [128, 24], F32, p1)
                    keysT = sbp("keysT", [128, 16, 128], F32, p1)
                    dma("sp", ssb[:], I["peer_keys"][l].rearrange("c n d -> n c d"))
                    for q4 in range(4):
                        ps = psg.next()
                        for j in range(4):
                            tp(ps[:, j * 128:(j + 1) * 128], ssb[:, q4 * 4 + j, :], ident[:])
                        evac(keysT[:, q4 * 4:(q4 + 1) * 4, :].rearrange("p a b -> p (a b)"), ps[:])
                    for t in range(G):
                        h = hq[t % 2]
                        dma("sp", h[:], hqres[tok0 + t * 128: tok0 + (t + 1) * 128, :])
                        for q4 in range(4):
                            ps = psg.next()
                            for j in range(4):
                                tp(ps[:, j * 128:(j + 1) * 128], h[:, (q4 * 4 + j) * 128:(q4 * 4 + j + 1) * 128], ident[:])
                            for j in range(4):
                                evac(hqT[:, q4 * 4 + j, t * 128:(t + 1) * 128], ps[:, j * 128:(j + 1) * 128])
                    for c in range(16):
                        w = wq[c % 2]
                        dma("sp", w[:], I["peer_wq"][l, :, c * 128:(c + 1) * 128].rearrange("(kc p) n -> p kc n", p=128))
                        ps = psg.next()
                        for kc in range(16):
                            mm(ps[:, 0:GT], w[:, kc, :], hqT[:, kc, :], kc == 0, kc == 15)
                        evac(qT[:, c, :], ps[:, 0:GT])
                    v16v = v16[:].rearrange("p (h two) k -> p h two k", two=2)
                    i16fv = i16f[:].rearrange("p (h two) k -> p h two k", two=2)
                    for t in range(G):
                        for q4 in range(4):
                            ps = psg.next()
                            for j in range(4):
                                c = q4 * 4 + j
                                mm(ps[:, j * 128:(j + 1) * 128], qT[:, c, t * 128:(t + 1) * 128], keysT[:, c, :], True, True)
                            evac(ssb[:, q4 * 4:(q4 + 1) * 4, :].rearrange("p a b -> p (a b)"), ps[:])
                        for c in range(16):
                            sv = ssb[:, c, :]
                            tr.op("dve", lambda e, c=c, sv=sv: e.max(out=v16[:, c, 0:8], in_=sv), reads=[sv], writes=[v16])
                            tr.op("dve", lambda e, c=c, sv=sv: e.max_index(out=i16[:, c, 0:8], in_max=v16[:, c, 0:8], in_values=sv), reads=[sv, v16], writes=[i16])
                            tr.op("dve", lambda e, c=c, sv=sv: e.match_replace(out=scr[:, 0:128], in_to_replace=v16[:, c, 0:8], in_values=sv, imm_value=NEG), reads=[sv, v16], writes=[scr])
                            tr.op("dve", lambda e, c=c: e.max(out=v16[:, c, 8:16], in_=scr[:, 0:128]), reads=[scr], writes=[v16])
                            tr.op("dve", lambda e, c=c: e.max_index(out=i16[:, c, 8:16], in_max=v16[:, c, 8:16], in_values=scr[:, 0:128]), reads=[scr, v16], writes=[i16])
                        vcopy("dve", i16f[:], i16[:])
                        ts("dve", i16fv[:, :, 0, :], i16fv[:, :, 0, :], 128.0, None, ALU.mult)
                        c4 = cand[:].rearrange("p h (a b) -> p h a b", a=16)
                        e4 = eix[:].rearrange("p h (a b) -> p h a b", a=16)
                        tt("dve", c4, v16v[:, :, 0, :].unsqueeze(3).to_broadcast([128, 8, 16, 16]),
                           v16v[:, :, 1, :].unsqueeze(2).to_broadcast([128, 8, 16, 16]), ALU.add)
                        tt("pool", e4, i16fv[:, :, 0, :].unsqueeze(3).to_broadcast([128, 8, 16, 16]),
                           i16fv[:, :, 1, :].unsqueeze(2).to_broadcast([128, 8, 16, 16]), ALU.add)
                        for hh in range(8):
                            cv = cand[:, hh, :]
                            tr.op("dve", lambda e, hh=hh, cv=cv: e.max(out=sc16[:, hh, 0:8], in_=cv), reads=[cv], writes=[sc16])
                            tr.op("dve", lambda e, hh=hh, cv=cv: e.max_index(out=ci16[:, hh, 0:8], in_max=sc16[:, hh, 0:8], in_values=cv), reads=[cv, sc16], writes=[ci16])
                            tr.op("dve", lambda e, hh=hh, cv=cv: e.match_replace(out=scr[:], in_to_replace=sc16[:, hh, 0:8], in_values=cv, imm_value=NEG), reads=[cv, sc16], writes=[scr])
                            tr.op("dve", lambda e, hh=hh: e.max(out=sc16[:, hh, 8:16], in_=scr[:]), reads=[scr], writes=[sc16])
                            tr.op("dve", lambda e, hh=hh: e.max_index(out=ci16[:, hh, 8:16], in_max=sc16[:, hh, 8:16], in_values=scr[:]), reads=[scr, sc16], writes=[ci16])
                        vcopy("dve", cif[:], ci16[:])
                        for hh in range(8):
                            for k in range(16):
                                tr.op("dve", lambda e, hh=hh, k=k: e.scalar_tensor_tensor(
                                    scr[:], iota256[:], cif[:, hh, k:k + 1], eix[:, hh, :],
                                    op0=ALU.is_equal, op1=ALU.mult, accum_out=ef[:, hh * 16 + k:hh * 16 + k + 1]),
                                    reads=[iota256, cif, eix], writes=[scr, ef])
                        ts("dve", ef[:], ef[:], float(l * 16384), None, ALU.add)
                        vcopy("dve", eidx[:, t, :], ef[:])
                        ts("dve", gs[:, 0:8], sc16[:, :, 0], -1.0, None, ALU.mult)
                        for hh in range(8):
                            actf(gwt[:, t, hh * 16:(hh + 1) * 16], sc16[:, hh, :], AF.Exp, bias=gs[:, hh:hh + 1], accum=gs[:, 8 + hh:9 + hh])
                        tr.op("dve", lambda e: e.reciprocal(gs[:, 16:24], gs[:, 8:16]), reads=[gs], writes=[gs])
                        gv_ = gwt[:, t, :].rearrange("p (h k) -> p h k", h=8)
                        tt("dve", gv_, gv_, gs[:, 16:24].unsqueeze(2).to_broadcast([128, 8, 16]), ALU.mult)
                tr.barrier()
                with contextlib.ExitStack() as p2:
                    NB = 8
                    ub = [sbp(f"ub{i}", [128, D], BF, p2) for i in range(NB)]
                    vb = [sbp(f"vb{i}", [128, D], BF, p2) for i in range(NB)]
                    hqt = sbp("hqt", [128, D], F32, p2)
                    yt = sbp("yt", [128, D], F32, p2)
                    x1 = sbp("x1t", [128, D], F32, p2)
                    junk = sbp("junk", [128, D], BF, p2)
                    av = sbp("av", [128, 128], F32, p2)
                    wg = sbp("wg", [128, 128], F32, p2)
                    g2b = sbp("g2b", [128, D], F32, p2)
                    lng = sbp("lng2", [128, D], F32, p2)
                    lnb = sbp("lnb2", [128, D], F32, p2)
                    scr2 = {"st": sbp("lnst2", [128, 4, 6], F32, p2), "mv": sbp("lnmv2", [128, 4], F32, p2)}
                    hqP = psall[:, 0:2048]
                    yP = psall[:, 2048:4096]
                    dma("sp", g2b[:], modd[l, path, 5 * D:6 * D].partition_broadcast(128))
                    dma("sp", lng[:], I["ln2_g"][l].partition_broadcast(128))
                    dma("sp", lnb[:], I["ln2_b"][l].partition_broadcast(128))
                    for t in range(G):
                        rows = slice(tok0 + t * 128, tok0 + (t + 1) * 128)
                        dma("sp", hqt[:], hqres[rows, :])
                        dma("sp", x1[:], xres[rows, :])
                        vcopy("act", hqP, hqt[:])
                        for p in range(128):
                            u = ub[p % NB]
                            tr.dma("pool", lambda e, u=u, p=p: e.indirect_dma_start(
                                out=u[:], out_offset=None, in_=ub16,
                                in_offset=bass.IndirectOffsetOnAxis(ap=eidx[:, t, p:p + 1], axis=0)),
                                reads=[eidx, ub16], writes=[u])
                            tr.op("dve", lambda e, u=u, p=p: e.scalar_tensor_tensor(
                                junk[:], u[:], 1.0, hqP, op0=ALU.mult, op1=ALU.mult, accum_out=av[:, p:p + 1]),
                                reads=[u, hqP], writes=[junk, av])
                        actf(wg[:], av[:], AF.Gelu_apprx_tanh)
                        tt("dve", wg[:], wg[:], gwt[:, t, :], ALU.mult)
                        for p in range(128):
                            v = vb[p % NB]
                            tr.dma("pool", lambda e, v=v, p=p: e.indirect_dma_start(
                                out=v[:], out_offset=None, in_=vb16,
                                in_offset=bass.IndirectOffsetOnAxis(ap=eidx[:, t, p:p + 1], axis=0)),
                                reads=[eidx, vb16], writes=[v])
                            if p == 0:
                                ts("dve", yP, v[:], wg[:, 0:1], None, ALU.mult)
                            else:
                                stt(yP, v[:], wg[:, p:p + 1], yP, ALU.mult, ALU.add)
                        tt("dve", yt[:], yP, g2b[:], ALU.mult)
                        stt(yt[:], x1[:], ALPHA, yt[:], ALU.mult, ALU.add)
                        layer_norm_tile(yt, yt, lng, lnb, scr2)
                        dma("sp", xres[rows, :], yt[:])
            tr.barrier()

        units = [(0, s * SEQ, 2, s) for s in range(NSEQ)]
        if do_latent:
            units.append((1, NSEQ * SEQ, 8, None))
        for l in range(n_layers):
            load_layer_consts(l)
            for (path, tok0, NT, seq) in units:
                with contextlib.ExitStack() as pu:
                    oT = sbp("oT", [128, 16, NT * 128], BF, pu, (NT * 128, 128))
                    with contextlib.ExitStack() as pm:
                        mixer(l, path, tok0, NT, seq, oT, pm)
                    tr.barrier()
                    wpass1(l, path, tok0, NT, oT)
                tr.barrier()
                wpass2(l, path, tok0, NT)
                tr.barrier()
                if do_peer:
                    for g0 in range(0, NT, 4):
                        peer(l, path, tok0 + g0 * 128, min(4, NT - g0), l == n_layers - 1)
        tr.barrier()
        dma("sp", O["yp"], xres[0:NSEQ * SEQ, :])
        dma("sp", O["ys"], xres[NSEQ * SEQ:NTOK, :])
        tr.final_wait("sp")
        stats = {"ninstr": tr.ninstr, "nsem": tr.nsem}
    return nc, stats


_CONSTS = None


def kernel(**inputs):
    global _CONSTS
    if _CONSTS is None:
        _CONSTS = host_constants()
    f = lambda a: np.ascontiguousarray(np.asarray(a, dtype=np.float32))
    x_prompt = f(inputs["x_prompt"])
    x_sample = f(inputs["x_sample"])
    shared = {
        "w_ada": f(inputs["w_ada"]), "b_ada": f(inputs["b_ada"]), "w_in": f(inputs["w_in"]),
        "g_q": f(inputs["g_q"]), "g_kv": f(inputs["g_kv"]), "w_uq": f(inputs["w_uq"]), "w_uk": f(inputs["w_uk"]),
        "w_uv": f(inputs["w_uv"]), "gqa_sink": f(inputs["gqa_sink"]), "w_pool": f(inputs["w_pool"]),
        "pool_scale": f(inputs["pool_scale"]), "na_rpb": f(inputs["na_rpb"]), "w_out": f(inputs["w_out"]),
        "ln1_g": f(inputs["ln1_g"]), "ln1_b": f(inputs["ln1_b"]), "peer_wq": f(inputs["peer_wq"]),
        "peer_keys": f(inputs["peer_keys"]).reshape(DEPTH, 16, 128, 128),
        "peer_u": f(inputs["peer_u"]), "peer_v": f(inputs["peer_v"]),
        "ln2_g": f(inputs["ln2_g"]), "ln2_b": f(inputs["ln2_b"]),
    }
    shared.update(_CONSTS)
    in_maps = []
    for c in range(NCORES):
        s = c // 4
        m = dict(shared)
        m["xp"] = np.ascontiguousarray(x_prompt[c * NSEQ:(c + 1) * NSEQ].reshape(NSEQ * SEQ, D))
        m["xs"] = np.ascontiguousarray(x_sample[s])
        m["c_ckv"] = f(inputs["cache_mla_ckv"][s])
        m["c_kr"] = f(inputs["cache_mla_krope"][s])
        m["c_gk"] = f(inputs["cache_gqa_k"][s]).reshape(DEPTH, PAST, 128)
        m["c_gv"] = f(inputs["cache_gqa_v"][s]).reshape(DEPTH, PAST, 128)
        m["c_nk"] = f(inputs["cache_na_k"][s]).reshape(DEPTH, PAST, 512)
        m["c_nv"] = f(inputs["cache_na_v"][s]).reshape(DEPTH, PAST, 512)
        m["cond"] = np.ascontiguousarray(np.stack([f(inputs["c_ctx"]), f(inputs["c"])[s]], 0))
        in_maps.append(m)
    nc, _ = build_program(KOPTS)
    res = run_bass_kernel_spmd(nc, in_maps, core_ids=list(range(NCORES)))
    R = res.results
    B = NCORES * NSEQ
    y_prompt = np.concatenate([R[c]["yp"].reshape(NSEQ, SEQ, D) for c in range(NCORES)], 0)
    y_sample = np.stack([R[0]["ys"], R[4]["ys"]], 0)
    cat = lambda k, shp: np.concatenate([R[c][k] for c in range(NCORES)], 0).reshape((B, DEPTH, SEQ) + shp)
    outs = (y_prompt.astype(np.float32), y_sample.astype(np.float32),
            cat("o_ckv", (256,)), cat("o_kr", (64,)), cat("o_gk", (2, 64)), cat("o_gv", (2, 64)),
            cat("o_nk", (8, 64)), cat("o_nv", (8, 64)))
    kernel.last_results = R
    return outs


KOPTS = {}
```

```python
import contextlib
import math

import numpy as np
import concourse.bass as bass
import concourse.mybir as mybir
from concourse.bass_utils import run_bass_kernel_spmd

F32 = mybir.dt.float32
BF = mybir.dt.bfloat16
I32 = mybir.dt.int32
U32 = mybir.dt.uint32
AF = mybir.ActivationFunctionType
ALU = mybir.AluOpType

D = 2048
DEPTH = 4
NCORES = 8
SEQ = 256
NSEQ = 4
DSEQ = 1024
PAST = 256
GRID_W = 64
ALPHA = (2 * DEPTH) ** 0.25
LN_EPS = 1e-5
RMS_EPS = 1e-6
IN_COLS = 3520
NEG = -1.0e30

SEM_LIMIT = 60000
N_DMA_SEMS = 24


class _Sem:
    def __init__(self, tr, name):
        self.tr = tr
        self.name = name
        self.gen = 0
        self.count = 0
        self.handle = tr._new_sem(f"{name}_0")

    def bump(self, inc):
        if self.count + inc > SEM_LIMIT:
            self.gen += 1
            self.count = 0
            self.handle = self.tr._new_sem(f"{self.name}_{self.gen}")
        self.count += inc
        return (self.handle, self.count)


class TR:
    ENG = ("pe", "dve", "act", "pool", "sp")

    def __init__(self, nc, es):
        self.nc = nc
        self.es = es
        self.nsem = 0
        self.eng = {"pe": nc.tensor, "dve": nc.vector, "act": nc.scalar, "pool": nc.gpsimd, "sp": nc.sync}
        self.esem = {e: _Sem(self, e) for e in self.ENG}
        self.dsem = [_Sem(self, f"d{i}") for i in range(N_DMA_SEMS)]
        self.dsem_sw = [_Sem(self, f"w{i}") for i in range(N_DMA_SEMS)]
        self.dnext = 0
        self.dnext_sw = 0
        self.seen = {e: {} for e in self.ENG}
        self.handles = {}
        self.W = {}
        self.R = {}
        self.excl = set()
        self.part = {}
        self.ninstr = 0

    def _new_sem(self, name):
        self.nsem += 1
        return self.es.enter_context(self.nc.semaphore(name))

    def keys(self, x):
        if isinstance(x, (str, tuple)):
            return [x]
        if not hasattr(x, "tensor"):
            name = x.name
            pt = self.part.get(name)
            if pt is None:
                return [name]
            return [(name, i) for i in range((pt[0] + pt[1] - 1) // pt[1])]
        name = x.tensor.name
        pt = self.part.get(name)
        if pt is None:
            return [name]
        period, block = pt
        off = int(x.offset) % period
        n = int(x.ap[-1][1]) if int(x.ap[-1][0]) == 1 else 1
        return [(name, i) for i in range(off // block, (off + n - 1) // block + 1)]

    def _deps(self, reads, writes):
        deps = {}

        def add(d):
            for hid, (h, v) in d.items():
                if hid not in deps or deps[hid][1] < v:
                    deps[hid] = (h, v)

        for r in reads:
            for k in self.keys(r):
                add(self.W.get(k, {}))
                if k[0] in self.excl if isinstance(k, tuple) else k in self.excl:
                    add(self.R.get(k, {}))
        for w in writes:
            for k in self.keys(w):
                add(self.W.get(k, {}))
                add(self.R.get(k, {}))
        return deps

    def _wait(self, e, deps):
        eng = self.eng[e]
        own = id(self.esem[e].handle)
        for hid, (h, v) in deps.items():
            if e == "pe" and hid == own:
                continue
            if self.seen[e].get(hid, 0) >= v:
                continue
            eng.wait_ge(h, v)
            self.ninstr += 1
            self.seen[e][hid] = v

    def _record(self, reads, writes, h, v):
        hid = id(h)
        for r in reads:
            for k in self.keys(r):
                ex = (k[0] in self.excl) if isinstance(k, tuple) else (k in self.excl)
                if ex:
                    self.W.setdefault(k, {})[hid] = (h, v)
                else:
                    self.R.setdefault(k, {})[hid] = (h, v)
        for w in writes:
            for k in self.keys(w):
                self.W.setdefault(k, {})[hid] = (h, v)

    def op(self, e, fn, reads=(), writes=()):
        self._wait(e, self._deps(reads, writes))
        ins = fn(self.eng[e])
        h, v = self.esem[e].bump(1)
        ins.then_inc(h, 1)
        self.ninstr += 1
        self._record(reads, writes, h, v)
        return ins

    def dma(self, q, fn, reads=(), writes=()):
        deps = self._deps(reads, writes)
        if q == "pool":
            s = self.dsem_sw[self.dnext_sw % N_DMA_SEMS]
            self.dnext_sw += 1
        else:
            s = self.dsem[self.dnext % N_DMA_SEMS]
            self.dnext += 1
        if s.count > 0:
            deps.setdefault(id(s.handle), (s.handle, s.count))
        self._wait(q, deps)
        ins = fn(self.eng[q])
        h, v = s.bump(16)
        ins.then_inc(h, 16)
        self.ninstr += 1
        self._record(reads, writes, h, v)
        return ins

    def barrier(self):
        deps = {}
        for s in list(self.esem.values()) + self.dsem + self.dsem_sw:
            if s.count > 0:
                deps[id(s.handle)] = (s.handle, s.count)
        for e in self.ENG:
            self._wait(e, dict(deps))

    def final_wait(self, e="sp"):
        deps = {}
        for s in list(self.esem.values()) + self.dsem + self.dsem_sw:
            if s.count > 0:
                deps[id(s.handle)] = (s.handle, s.count)
        self._wait(e, deps)


class Rot:
    def __init__(self, items):
        self.items = list(items)
        self.i = 0

    def next(self):
        x = self.items[self.i % len(self.items)]
        self.i += 1
        return x


A_IN = 704
CH = [("A1", 0, 384), ("A2", 384, 320), ("B1", 704, 512), ("B2", 1216, 256),
      ("C", 1472, 512), ("D1", 1984, 512), ("D2", 2496, 512), ("D3", 3008, 512)]


def host_constants():
    c = {}
    n = np.arange(DSEQ)
    inv = (10000.0 ** (-np.arange(16, dtype=np.float32) / 16.0)).astype(np.float32)
    ar = (n // GRID_W).astype(np.float32)[:, None] * inv[None, :]
    ac = (n % GRID_W).astype(np.float32)[:, None] * inv[None, :]
    cosr, sinr = np.cos(ar).astype(np.float32), np.sin(ar).astype(np.float32)
    cosc, sinc = np.cos(ac).astype(np.float32), np.sin(ac).astype(np.float32)
    c["rope_cos"] = np.concatenate([cosr, cosr, cosc, cosc], 1).astype(np.float32)
    c["rope_sin"] = np.concatenate([-sinr, sinr, -sinc, sinc], 1).astype(np.float32)
    def band(S):
        out = np.zeros((4, S, S), np.float32)
        for g, w in enumerate((2, 4, 8, 16)):
            for t in range(S):
                lo = min(max(t - w // 2, 0), S)
                hi = min(max(t + w // 2, 0), S)
                out[g, t, lo:hi] = 1.0 / float(hi - lo)
                out[g, t, t] -= 1.0
        return out
    def blocks(S):
        A = band(S)
        nt = S // 128
        o = np.zeros((4, 5, 128, 128), np.float32)
        for g in range(4):
            At = A[g].T
            o[g, 0] = At[0:128, 128:256]
            o[g, 1] = At[0:128, 0:128]
            mid = 1 if nt > 2 else 0
            o[g, 2] = At[mid * 128:(mid + 1) * 128, mid * 128:(mid + 1) * 128]
            o[g, 3] = At[(nt - 1) * 128:, (nt - 1) * 128:]
            o[g, 4] = At[128:256, 0:128]
        return o
    c["pool_at_ctx"] = np.ascontiguousarray(blocks(SEQ).transpose(2, 0, 1, 3).reshape(128, 20, 128))
    c["pool_at_lat"] = np.ascontiguousarray(blocks(DSEQ).transpose(2, 0, 1, 3).reshape(128, 20, 128))
    kk = np.arange(128)[:, None]
    qq = np.arange(128)[None, :]
    c["win_mask"] = np.stack([(kk >= qq), (kk <= qq)], 1).astype(np.float32)
    cols = np.arange(GRID_W)
    cstart = np.clip(cols - 8, 0, GRID_W - 16)
    ok = (cols[None, :] >= cstart[:, None]) & (cols[None, :] < cstart[:, None] + 16)
    okT = ok.T.astype(np.float32)
    c["na_colok"] = np.concatenate([okT, okT], 0)
    return c


CONST_SHAPES = {"rope_cos": [DSEQ, 64], "rope_sin": [DSEQ, 64], "pool_at_ctx": [128, 20, 128],
                "pool_at_lat": [128, 20, 128], "win_mask": [128, 2, 128], "na_colok": [128, 64]}

IN_SHAPES = {
    "xp": [NSEQ * SEQ, D], "xs": [DSEQ, D],
    "c_ckv": [DEPTH, PAST, 256], "c_kr": [DEPTH, PAST, 64], "c_gk": [DEPTH, PAST, 128],
    "c_gv": [DEPTH, PAST, 128], "c_nk": [DEPTH, PAST, 512], "c_nv": [DEPTH, PAST, 512],
    "cond": [2, D],
    "w_ada": [DEPTH, D, 6 * D], "b_ada": [DEPTH, 6 * D], "w_in": [DEPTH, D, IN_COLS],
    "g_q": [DEPTH, 384], "g_kv": [DEPTH, 256], "w_uq": [DEPTH, 384, 768], "w_uk": [DEPTH, 256, 512],
    "w_uv": [DEPTH, 256, 512], "gqa_sink": [DEPTH, 8], "w_pool": [DEPTH, 4, 128, 128],
    "pool_scale": [DEPTH, 512], "na_rpb": [DEPTH, 8, 15, 31], "w_out": [DEPTH, D, D],
    "ln1_g": [DEPTH, D], "ln1_b": [DEPTH, D], "peer_wq": [DEPTH, D, D],
    "peer_keys": [DEPTH, 16, 128, 128], "peer_u": [DEPTH, 16384, D], "peer_v": [DEPTH, 16384, D],
    "ln2_g": [DEPTH, D], "ln2_b": [DEPTH, D],
}
OUT_SHAPES = {
    "yp": [NSEQ * SEQ, D], "ys": [DSEQ, D],
    "o_ckv": [NSEQ, DEPTH, SEQ, 256], "o_kr": [NSEQ, DEPTH, SEQ, 64], "o_gk": [NSEQ, DEPTH, SEQ, 128],
    "o_gv": [NSEQ, DEPTH, SEQ, 128], "o_nk": [NSEQ, DEPTH, SEQ, 512], "o_nv": [NSEQ, DEPTH, SEQ, 512],
}


def build_program(opts=None):
    opts = opts or {}
    n_layers = opts.get("layers", DEPTH)
    do_latent = opts.get("latent", True)
    do_peer = opts.get("peer", True)
    dbg = opts.get("dbg", False)

    nc = bass.Bass("TRN2", target_bir_lowering=False)
    I = {k: nc.dram_tensor(k, s, F32, kind="ExternalInput").ap() for k, s in IN_SHAPES.items()}
    C = {k: nc.dram_tensor(k, s, F32, kind="ExternalInput").ap() for k, s in CONST_SHAPES.items()}
    O = {k: nc.dram_tensor(k, s, F32, kind="ExternalOutput").ap() for k, s in OUT_SHAPES.items()}
    NTOK = NSEQ * SEQ + DSEQ
    xres = nc.dram_tensor("xres", [NTOK, D], F32, kind="Internal").ap()
    hqres = nc.dram_tensor("hqres", [NTOK, D], F32, kind="Internal").ap()
    modd = nc.dram_tensor("modd", [DEPTH, 2, 6 * D], F32, kind="Internal").ap()
    DBG = {}
    if dbg:
        DBG["mod"] = nc.dram_tensor("dbg_mod", [DEPTH, 2, 6 * D], F32, kind="ExternalOutput").ap()
        DBG["x1"] = nc.dram_tensor("dbg_x1", [NTOK, D], F32, kind="ExternalOutput").ap()
        DBG["o"] = nc.dram_tensor("dbg_o", [NTOK, D], F32, kind="ExternalOutput").ap()

    es = contextlib.ExitStack()
    with es:
        tr = TR(nc, es)

        def sb(name, shape, dt, stack=es):
            return stack.enter_context(nc.sbuf_tensor(name, shape, dt))

        psall = es.enter_context(nc.psum_tensor("psall", [128, 4096], F32))
        tr.excl.add(psall[:].tensor.name)
        tr.part[psall[:].tensor.name] = (4096, 512)
        PS = [psall[:, i * 512:(i + 1) * 512] for i in range(8)]
        psg = Rot(PS[0:3])
        pss = Rot(PS[3:6])
        pso = Rot(PS[6:8])

        def mm(out, lhsT, rhs, start, stop):
            tr.op("pe", lambda e: e.matmul(out, lhsT, rhs, start=start, stop=stop), reads=[lhsT, rhs], writes=[out])

        def tp(out, in_, idn):
            tr.op("pe", lambda e: e.transpose(out, in_, idn), reads=[in_, idn], writes=[out])

        def dma(q, out, in_, **kw):
            tr.dma(q, lambda e: e.dma_start(out=out, in_=in_, **kw), reads=[in_], writes=[out])

        def vcopy(eng, out, in_):
            if eng == "act":
                tr.op("act", lambda e: e.copy(out, in_), reads=[in_], writes=[out])
            else:
                tr.op(eng, lambda e: e.tensor_copy(out, in_), reads=[in_], writes=[out])

        def tt(eng, out, a, b, op):
            tr.op(eng, lambda e: e.tensor_tensor(out, a, b, op=op), reads=[a, b], writes=[out])

        def ts(eng, out, a, s1, s2, op0, op1=None, extra_reads=()):
            rd = [a] + [s for s in (s1, s2) if not isinstance(s, (int, float, type(None)))] + list(extra_reads)
            if op1 is None:
                tr.op(eng, lambda e: e.tensor_scalar(out, a, s1, None, op0=op0), reads=rd, writes=[out])
            else:
                tr.op(eng, lambda e: e.tensor_scalar(out, a, s1, s2, op0=op0, op1=op1), reads=rd, writes=[out])

        def stt(out, a, s, b, op0, op1):
            rd = [a, b] + ([] if isinstance(s, (int, float)) else [s])
            tr.op("dve", lambda e: e.scalar_tensor_tensor(out, a, s, b, op0=op0, op1=op1), reads=rd, writes=[out])

        def actf(out, in_, func, scale=1.0, bias=0.0, accum=None):
            rd = [in_] + [s for s in (scale, bias) if not isinstance(s, (int, float))]
            wr = [out] + ([accum] if accum is not None else [])
            kw = {}
            if accum is not None:
                kw["accum_out"] = accum
            tr.op("act", lambda e: e.activation(out=out, in_=in_, func=func, bias=bias, scale=scale, **kw),
                  reads=rd, writes=wr)

        evq = Rot(["dve", "act"])

        def evac(out, in_):
            vcopy(evq.next(), out, in_)

        ident = sb("ident", [128, 128], F32)
        tr.op("pool", lambda e: e.memset(ident[:], 0.0), writes=[ident])
        tr.op("pool", lambda e: e.affine_select(out=ident[:], in_=ident[:], pattern=[[-1, 128]],
                                                 compare_op=ALU.not_equal, fill=1.0, base=0,
                                                 channel_multiplier=1), reads=[ident], writes=[ident])
        identb = sb("identb", [128, 128], BF)
        tr.op("pool", lambda e: e.tensor_copy(identb[:], ident[:]), reads=[ident], writes=[identb])
        iota_i = sb("iota_i", [128, 256], I32)
        iota256 = sb("iota256", [128, 256], F32)
        tr.op("pool", lambda e: e.iota(iota_i[:], pattern=[[1, 256]], base=0, channel_multiplier=0), writes=[iota_i])
        tr.op("pool", lambda e: e.tensor_copy(iota256[:], iota_i[:]), reads=[iota_i], writes=[iota256])
        epsc = sb("epsc", [128, 2], F32)
        tr.op("pool", lambda e: e.memset(epsc[:, 0:1], RMS_EPS), writes=[epsc])
        tr.op("pool", lambda e: e.memset(epsc[:, 1:2], LN_EPS), writes=[epsc])

        ub16 = nc.dram_tensor("ub16", [DEPTH * 16384, D], BF, kind="Internal").ap()
        vb16 = nc.dram_tensor("vb16", [DEPTH * 16384, D], BF, kind="Internal").ap()

        def convert_table(l_, which):
            src_, dst_ = ((I["peer_u"], ub16), (I["peer_v"], vb16))[which]
            for r0 in range(0, 16384, 2048):
                dma("pool", dst_[l_ * 16384 + r0: l_ * 16384 + r0 + 2048, :], src_[l_, r0:r0 + 2048, :])

        dma("sp", xres[0:NSEQ * SEQ, :], I["xp"])
        dma("sp", xres[NSEQ * SEQ:NTOK, :], I["xs"])

        with contextlib.ExitStack() as ph:
            cond_sb = sb("cond_sb", [2, D], F32, ph)
            sil = sb("sil", [2, D], F32, ph)
            sT = sb("sT", [128, 16, 2], BF, ph)
            wa = [sb(f"wa{i}", [128, 16, 512], BF, ph) for i in range(2)]
            msb = [sb(f"msb{i}", [2, 2048], F32, ph) for i in range(2)]
            bsb = [sb(f"bsb{i}", [2, 2048], F32, ph) for i in range(2)]
            dma("sp", cond_sb[:], I["cond"])
            actf(sil[:], cond_sb[:], AF.Silu)
            ps = psg.next()
            for kc in range(16):
                tp(ps[:, kc * 2:(kc + 1) * 2], sil[0:2, kc * 128:(kc + 1) * 128], ident[0:2, 0:2])
            vcopy("dve", sT[:].rearrange("p a b -> p (a b)"), ps[:, 0:32])
            ci = 0
            for l in range(n_layers):
                for g in range(6):
                    bs, ms = bsb[g % 2], msb[g % 2]
                    dma("sp", bs[:], I["b_ada"][l, g * 2048:(g + 1) * 2048].partition_broadcast(2))
                    for j in range(4):
                        c0 = (g * 4 + j) * 512
                        w = wa[ci % 2]
                        ci += 1
                        dma("pool", w[:], I["w_ada"][l, :, c0:c0 + 512].rearrange("(kc p) c -> p kc c", p=128))
                        ps = psg.next()
                        for kc in range(16):
                            mm(ps[0:2, :], sT[:, kc, :], w[:, kc, :], kc == 0, kc == 15)
                        tt("dve", ms[:, j * 512:(j + 1) * 512], ps[0:2, :], bs[:, j * 512:(j + 1) * 512], ALU.add)
                    dma("sp", modd[l, :, g * 2048:(g + 1) * 2048], ms[:])
                    if dbg:
                        dma("sp", DBG["mod"][l, :, g * 2048:(g + 1) * 2048], ms[:])
        if do_peer:
            convert_table(0, 0)
            convert_table(0, 1)
            if not do_latent:
                for l_ in range(1, n_layers):
                    convert_table(l_, 0)
                    convert_table(l_, 1)
        tr.barrier()

        uid = [0]

        def sbp(name, shape, dt, stack, part=None):
            uid[0] += 1
            t = stack.enter_context(nc.sbuf_tensor(f"{name}_{uid[0]}", shape, dt))
            if part is not None:
                tr.part[t[:].tensor.name] = part
            return t

        modcol = sb("modcol", [128, 2, 2, 16], F32)
        gqcol = sb("gqcol", [128, 3], F32)
        pscol = sb("pscol", [128, 4], F32)
        gkvb = sb("gkvb", [128, 256], F32)
        esink = sb("esink", [128, 8], F32)
        wuqn = sb("wuqn", [128, 3, 512], BF)
        wuqr = sb("wuqr", [128, 3, 256], BF)
        wuk = sb("wuk", [128, 2, 512], BF)
        wuv = sb("wuv", [128, 2, 512], BF)
        wpool = sb("wpool", [128, 4, 128], BF)
        ropec = sb("ropec", [128, 8, 64], F32)
        ropes = sb("ropes", [128, 8, 64], F32)
        winm = sb("winm", [128, 2, 128], BF)
        Ebuf = [sb(f"Ebuf{i}", [128, 512], BF) for i in range(3)]
        Erot = Rot(Ebuf)
        small = sb("small", [128, 64], F32)
        if do_latent:
            dma("sp", ropec[:], C["rope_cos"].rearrange("(t p) c -> p t c", p=128))
            dma("sp", ropes[:], C["rope_sin"].rearrange("(t p) c -> p t c", p=128))
            dma("pool", winm[:], C["win_mask"])

        def load_layer_consts(l):
            with nc.allow_non_contiguous_dma(reason="tiny per-layer vectors in column layout"):
                for path in range(2):
                    for j, k in enumerate((0, 1)):
                        dma("sp", modcol[:, path, j, :], modd[l, path, k * D:(k + 1) * D].rearrange("(c p) -> p c", p=128))
                dma("sp", gqcol[:], I["g_q"][l].rearrange("(c p) -> p c", p=128))
                dma("sp", pscol[:], I["pool_scale"][l].rearrange("(c p) -> p c", p=128))
            ts("dve", modcol[:, :, 1, :], modcol[:, :, 1, :], 1.0, None, ALU.add)
            dma("sp", gkvb[:], I["g_kv"][l].partition_broadcast(128))
            dma("sp", esink[:], I["gqa_sink"][l].partition_broadcast(128))
            actf(esink[:], esink[:], AF.Exp)
            for c3 in range(3):
                srcq = I["w_uq"][l, c3 * 128:(c3 + 1) * 128, :].rearrange("p (h d) -> p h d", h=4)
                dma("pool", wuqn[:, c3, :].rearrange("p (h d) -> p h d", h=4), srcq[:, :, 0:128])
                dma("pool", wuqr[:, c3, :].rearrange("p (h d) -> p h d", h=4), srcq[:, :, 128:192])
            dma("pool", wuk[:], I["w_uk"][l].rearrange("(c p) n -> p c n", p=128))
            dma("pool", wuv[:], I["w_uv"][l].rearrange("(c p) n -> p c n", p=128))
            dma("pool", wpool[:], I["w_pool"][l].rearrange("g c e -> c g e"))

        ropetmp = sb("ropetmp", [128, 8, 2, 16], F32)

        def rope(x, H, ti):
            xv = x.rearrange("p (h a b c) -> p h a b c", h=H, a=2, b=2)
            for a in range(2):
                xa = xv[:, :, a, :, :]
                xsw = xa[:, :, ::-1, :]
                cs = ropec[:, ti, a * 32:(a + 1) * 32].rearrange("p (b c) -> p b c", b=2).unsqueeze(1).to_broadcast([128, H, 2, 16])
                sn = ropes[:, ti, a * 32:(a + 1) * 32].rearrange("p (b c) -> p b c", b=2).unsqueeze(1).to_broadcast([128, H, 2, 16])
                tmp = ropetmp[:, 0:H, :, :]
                tt("dve", tmp, xsw, sn, ALU.mult)
                tt("dve", xa, xa, cs, ALU.mult)
                tt("dve", xa, xa, tmp, ALU.add)

        def attend(q_ops, k_ops, v_op, kblocks, dv, scale, out_ap, sink_ap=None, mask_fn=None):
            Ob = pso.next()
            Oap = Ob[:, 0:dv + 1]
            nkb = len(kblocks)
            done = 0
            for c0 in range(0, nkb, 4):
                chunk = kblocks[c0:c0 + 4]
                Sb = pss.next()
                for si, kb in enumerate(chunk):
                    ko = k_ops(kb)
                    for oi, (kq, qq) in enumerate(zip(ko, q_ops)):
                        mm(Sb[:, si * 128:(si + 1) * 128], kq, qq, oi == 0, oi == len(ko) - 1)
                E = Erot.next()
                n = len(chunk) * 128
                actf(E[:, 0:n], Sb[:, 0:n], AF.Exp, scale=scale)
                for si, kb in enumerate(chunk):
                    Es = E[:, si * 128:(si + 1) * 128]
                    if mask_fn is not None:
                        mask_fn(kb, Es)
                    mm(Oap, Es, v_op(kb), done == 0, done == nkb - 1)
                    done += 1
            den = small[:, 0:1]
            if sink_ap is not None:
                tt("dve", den, Ob[:, dv:dv + 1], sink_ap, ALU.add)
            else:
                vcopy("dve", den, Ob[:, dv:dv + 1])
            tr.op("dve", lambda e: e.reciprocal(small[:, 1:2], den), reads=[small], writes=[small])
            ts("dve", out_ap, Ob[:, 0:dv], small[:, 1:2], None, ALU.mult)

        def mixer(l, path, tok0, NT, seq, oT, ph):
            S = NT * 128
            L = PAST if path == 1 else 0
            LB = L // 128
            NK = L + S
            KB = NK // 128
            hT = sbp("hT", [128, 16, S], BF, ph, (S, 128))
            wb = [sbp(f"wb{i}", [128, 16, 512], BF, ph) for i in range(2)]
            wrot = Rot(wb)
            s1 = contextlib.ExitStack()
            xt = [sbp(f"xt{i}", [128, D], F32, s1) for i in range(2)]
            for ti in range(NT):
                x = xt[ti % 2]
                dma("sp", x[:], xres[tok0 + ti * 128: tok0 + (ti + 1) * 128, :])
                for q4 in range(4):
                    ps = psg.next()
                    for j in range(4):
                        kc = q4 * 4 + j
                        tp(ps[:, j * 128:(j + 1) * 128], x[:, kc * 128:(kc + 1) * 128], ident[:])
                    for j in range(4):
                        kc = q4 * 4 + j
                        dst = hT[:, kc, ti * 128:(ti + 1) * 128]
                        src = ps[:, j * 128:(j + 1) * 128]
                        if (j % 2) == 0:
                            ts("dve", dst, src, modcol[:, path, 1, kc:kc + 1], modcol[:, path, 0, kc:kc + 1], ALU.mult, ALU.add)
                        else:
                            actf(dst, src, AF.Identity, scale=modcol[:, path, 1, kc:kc + 1], bias=modcol[:, path, 0, kc:kc + 1])
            tr.barrier()
            s1.close()

            def in_proj(chunk_idx, handler):
                _, c0, w = CH[chunk_idx]
                wt = wrot.next()
                dma("pool", wt[:, :, 0:w], I["w_in"][l, :, c0:c0 + w].rearrange("(kc p) c -> p kc c", p=128))
                for ti in range(NT):
                    ps = psg.next()
                    for kc in range(16):
                        mm(ps[:, 0:w], hT[:, kc, ti * 128:(ti + 1) * 128], wt[:, kc, 0:w], kc == 0, kc == 15)
                    handler(ti, ps)

            def rms_rstd(ps_ap, n, col):
                junk = Erot.next()
                actf(junk[:, 0:n], ps_ap, AF.Square, accum=small[:, col:col + 1])
                ts("dve", small[:, col:col + 1], small[:, col:col + 1], 1.0 / n, None, ALU.mult)
                actf(small[:, col:col + 1], small[:, col:col + 1], AF.Sqrt, bias=epsc[:, 0:1])
                tr.op("dve", lambda e: e.reciprocal(small[:, col:col + 1], small[:, col:col + 1]), reads=[small], writes=[small])

            def transpose_into(dst_fn, src, nchunks, scale_col=None):
                for c0 in range(0, nchunks, 4):
                    ps = psg.next()
                    n = min(4, nchunks - c0)
                    for j in range(n):
                        tp(ps[:, j * 128:(j + 1) * 128], src[:, (c0 + j) * 128:(c0 + j + 1) * 128], ident[:])
                    for j in range(n):
                        if scale_col is not None:
                            ts("dve", dst_fn(c0 + j), ps[:, j * 128:(j + 1) * 128], scale_col[:, c0 + j:c0 + j + 1], None, ALU.mult)
                        else:
                            evac(dst_fn(c0 + j), ps[:, j * 128:(j + 1) * 128])

            with contextlib.ExitStack() as ga:
                cqT = sbp("cqT", [128, 3, S], BF, ga, (S, 128))
                ckvT = sbp("ckvT", [128, 2, NK], BF, ga, (NK, 128))
                krT = sbp("krT", [128, NK], BF, ga, (NK, 128))
                QnT = sbp("QnT", [128, 4, S], BF, ga, (S, 128))
                QrT = sbp("QrT", [128, 2, S], BF, ga, (S, 128))
                KnT = sbp("KnT", [128, 4, NK], BF, ga, (NK, 128))
                VA = sbp("VA", [128, KB, 4, 130], BF, ga, (KB * 520, 520))
                og = sbp("ogA", [128, NT, 512], F32, ga, (NT * 512, 512))
                tmpA = [sbp(f"tmpA{i}", [128, 384], F32, ga) for i in range(2)]
                ckvs = [sbp(f"ckvs{i}", [128, 256], F32, ga) for i in range(2)]
                kr2 = [sbp(f"kr2{i}", [128, 128], F32, ga) for i in range(2)]
                qrs = [sbp(f"qrs{i}", [128, 256], F32, ga) for i in range(2)]
                tr.op("pool", lambda e: e.memset(VA[:, :, :, 128:129], 1.0), writes=["VAones"])

                def hA1(ti, ps):
                    rms_rstd(ps[:, 0:384], 384, 2)
                    t = tmpA[ti % 2]
                    ts("dve", t[:], ps[:, 0:384], small[:, 2:3], None, ALU.mult)
                    transpose_into(lambda c: cqT[:, c, ti * 128:(ti + 1) * 128], t, 3, scale_col=gqcol)

                def ckv_finish(cs, k2, col0, ti_rope):
                    if ti_rope is not None:
                        rope(k2[:, 0:64], 1, ti_rope)
                    vcopy("pool", k2[:, 64:128], k2[:, 0:64])
                    transpose_into(lambda c: ckvT[:, c, col0:col0 + 128], cs, 2)
                    transpose_into(lambda c: krT[:, col0:col0 + 128], k2, 1)

                def hA2(ti, ps):
                    rms_rstd(ps[:, 0:256], 256, 3)
                    cs, k2 = ckvs[ti % 2], kr2[ti % 2]
                    stt(cs[:], ps[:, 0:256], small[:, 3:4], gkvb[:], ALU.mult, ALU.mult)
                    vcopy("act", k2[:, 0:64], ps[:, 256:320])
                    if path == 0:
                        dma("sp", O["o_ckv"][seq, l, ti * 128:(ti + 1) * 128, :], cs[:])
                        dma("sp", O["o_kr"][seq, l, ti * 128:(ti + 1) * 128, :], k2[:, 0:64])
                    ckv_finish(cs, k2, L + ti * 128, ti if path == 1 else None)

                in_proj(0, hA1)
                in_proj(1, hA2)
                if do_peer and path == 1 and l + 1 < n_layers:
                    convert_table(l + 1, 0)
                for b in range(LB):
                    cs, k2 = ckvs[b % 2], kr2[b % 2]
                    dma("sp", cs[:], I["c_ckv"][l, b * 128:(b + 1) * 128, :])
                    dma("sp", k2[:, 0:64], I["c_kr"][l, b * 128:(b + 1) * 128, :])
                    ckv_finish(cs, k2, b * 128, None)
                for h in range(4):
                    for k0 in range(0, NK, 512):
                        n = min(512, NK - k0)
                        ps = psg.next()
                        for c in range(2):
                            mm(ps[:, 0:n], wuk[:, c, h * 128:(h + 1) * 128], ckvT[:, c, k0:k0 + n], c == 0, c == 1)
                        evac(KnT[:, h, k0:k0 + n], ps[:, 0:n])
                for kb in range(KB):
                    ps = psg.next()
                    for c in range(2):
                        mm(ps[:, 0:512], ckvT[:, c, kb * 128:(kb + 1) * 128], wuv[:, c, :], c == 0, c == 1)
                    tr.op(evq.next(), (lambda kb, ps: (lambda e: (e.copy if e is nc.scalar else e.tensor_copy)(
                        VA[:, kb, :, 0:128], ps[:, 0:512].rearrange("p (h d) -> p h d", h=4))))(kb, ps),
                        reads=[ps, "VAones"], writes=[VA[:, kb, 0, :]])
                for h in range(4):
                    for t0 in range(0, S, 512):
                        n = min(512, S - t0)
                        ps = psg.next()
                        for c in range(3):
                            mm(ps[:, 0:n], wuqn[:, c, h * 128:(h + 1) * 128], cqT[:, c, t0:t0 + n], c == 0, c == 2)
                        evac(QnT[:, h, t0:t0 + n], ps[:, 0:n])
                for ti in range(NT):
                    ps = psg.next()
                    for c in range(3):
                        mm(ps[:, 0:256], cqT[:, c, ti * 128:(ti + 1) * 128], wuqr[:, c, :], c == 0, c == 2)
                    qr = qrs[ti % 2]
                    evac(qr[:], ps[:, 0:256])
                    if path == 1:
                        rope(qr[:], 4, ti)
                    transpose_into(lambda c: QrT[:, c, ti * 128:(ti + 1) * 128], qr, 2)
                scl = 192.0 ** -0.5
                for qt in range(NT):
                    for h in range(4):
                        j, e2 = h // 2, h % 2
                        qs = slice(qt * 128, (qt + 1) * 128)
                        attend([QnT[:, h, qs], QrT[e2 * 64:(e2 + 1) * 64, j, qs]],
                               lambda kb, h=h, e2=e2: [KnT[:, h, kb * 128:(kb + 1) * 128], krT[e2 * 64:(e2 + 1) * 64, kb * 128:(kb + 1) * 128]],
                               lambda kb, h=h: VA[:, kb, h, 0:129],
                               list(range(KB)), 128, scl, og[:, qt, h * 128:(h + 1) * 128])
                for qt in range(NT):
                    transpose_into(lambda c: oT[:, c, qt * 128:(qt + 1) * 128], og[:, qt, :], 4)
                    if dbg:
                        dma("sp", DBG["o"][tok0 + qt * 128: tok0 + (qt + 1) * 128, 0:512], og[:, qt, :])
                tr.barrier()

            with contextlib.ExitStack() as gb:
                QTB = sbp("QTB", [128, 4, S], BF, gb, (S, 128))
                KTB = sbp("KTB", [128, 2, NK], BF, gb, (NK, 128))
                VB = sbp("VB", [128, KB, 2, 66], BF, gb, (KB * 132, 132))
                og = sbp("ogB", [128, NT, 512], F32, gb, (NT * 512, 512))
                gqs = [sbp(f"gqs{i}", [128, 512], F32, gb) for i in range(2)]
                gks = [sbp(f"gks{i}", [128, 256], F32, gb) for i in range(2)]
                gvs = [sbp(f"gvs{i}", [128, 128], F32, gb) for i in range(2)]
                tr.op("pool", lambda e: e.memset(VB[:, :, :, 64:65], 1.0), writes=["VBones"])

                def hB1(ti, ps):
                    gq = gqs[ti % 2]
                    evac(gq[:], ps[:, 0:512])
                    if path == 1:
                        rope(gq[:], 8, ti)
                    transpose_into(lambda c: QTB[:, c, ti * 128:(ti + 1) * 128], gq, 4)

                def kvB(gk, gv, col0, kb):
                    vcopy("pool", gk[:, 128:192], gk[:, 64:128])
                    vcopy("pool", gk[:, 192:256], gk[:, 0:64])
                    transpose_into(lambda c: KTB[:, c, col0:col0 + 128], gk, 2)
                    tr.op("dve", lambda e: e.tensor_copy(VB[:, kb, :, 0:64], gv[:].rearrange("p (h d) -> p h d", h=2)),
                          reads=[gv, "VBones"], writes=[VB[:, kb, 0, :]])

                def hB2(ti, ps):
                    gk, gv = gks[ti % 2], gvs[ti % 2]
                    evac(gk[:, 0:128], ps[:, 0:128])
                    evac(gv[:], ps[:, 128:256])
                    if path == 0:
                        dma("sp", O["o_gk"][seq, l, ti * 128:(ti + 1) * 128, :], gk[:, 0:128])
                        dma("sp", O["o_gv"][seq, l, ti * 128:(ti + 1) * 128, :], gv[:])
                    else:
                        rope(gk[:, 0:128], 2, ti)
                    kvB(gk, gv, L + ti * 128, LB + ti)

                in_proj(2, hB1)
                in_proj(3, hB2)
                if do_peer and path == 1 and l + 1 < n_layers:
                    convert_table(l + 1, 1)
                for b in range(LB):
                    gk, gv = gks[b % 2], gvs[b % 2]
                    dma("sp", gk[:, 0:128], I["c_gk"][l, b * 128:(b + 1) * 128, :])
                    dma("sp", gv[:], I["c_gv"][l, b * 128:(b + 1) * 128, :])
                    kvB(gk, gv, b * 128, b)
                for qt in range(NT):
                    if path == 0:
                        kbl = list(range(KB))
                    else:
                        kbl = list(range(LB)) + [LB + t for t in (qt - 1, qt, qt + 1) if 0 <= t < NT]

                    def mfn(kb, Es, qt=qt):
                        if path == 0 or kb < LB or kb == LB + qt:
                            return
                        i = 0 if kb == LB + qt - 1 else 1
                        tt("pool", Es, Es, winm[:, i, :], ALU.mult)

                    for h in range(8):
                        j, e2, kv = h // 2, h % 2, h // 4
                        a = 0 if e2 == kv else 1
                        qs = slice(qt * 128, (qt + 1) * 128)
                        attend([QTB[e2 * 64:(e2 + 1) * 64, j, qs]],
                               lambda kb, a=a, e2=e2: [KTB[e2 * 64:(e2 + 1) * 64, a, kb * 128:(kb + 1) * 128]],
                               lambda kb, kv=kv: VB[:, kb, kv, 0:65],
                               kbl, 64, 0.125, og[:, qt, h * 64:(h + 1) * 64],
                               sink_ap=esink[:, h:h + 1], mask_fn=mfn)
                for qt in range(NT):
                    transpose_into(lambda c: oT[:, 4 + c, qt * 128:(qt + 1) * 128], og[:, qt, :], 4)
                    if dbg:
                        dma("sp", DBG["o"][tok0 + qt * 128: tok0 + (qt + 1) * 128, 512:1024], og[:, qt, :])
                tr.barrier()

            with contextlib.ExitStack() as gc:
                zc = sbp("zc", [128, NT, 512], F32, gc, (NT * 512, 512))
                AT = sbp("AT", [128, 20, 128], F32, gc)
                dlt = [sbp(f"dlt{i}", [128, 128], BF, gc) for i in range(2)]
                dma("sp", AT[:], C["pool_at_lat" if path == 1 else "pool_at_ctx"])
                in_proj(4, lambda ti, ps: evac(zc[:, ti, :], ps[:, 0:512]))
                di = 0
                for ti in range(NT):
                    for g in range(4):
                        nb = []
                        if ti > 0:
                            nb.append((ti - 1, 0))
                        nb.append((ti, 1 if ti == 0 else (3 if ti == NT - 1 else 2)))
                        if ti < NT - 1:
                            nb.append((ti + 1, 4))
                        ps = psg.next()
                        for i, (st, kind) in enumerate(nb):
                            mm(ps[:, 0:128], zc[:, st, g * 128:(g + 1) * 128], AT[:, g * 5 + kind, :], i == 0, i == len(nb) - 1)
                        d = dlt[di % 2]
                        di += 1
                        evac(d[:], ps[:, 0:128])
                        mm(ps[:, 128:256], wpool[:, g, :], d[:], True, True)
                        ts("dve", oT[:, 8 + g, ti * 128:(ti + 1) * 128], ps[:, 128:256], pscol[:, g:g + 1], None, ALU.mult)
                tr.barrier()

            with contextlib.ExitStack() as gd:
                Tm = build_tm(l, gd) if path == 1 else None
                QTD = sbp("QTD", [128, 4, S], BF, gd, (S, 128))
                KTD = sbp("KTD", [128, 4, NK], BF, gd, (NK, 128))
                VD = sbp("VD", [128, KB, 8, 66], BF, gd, (KB * 528, 528))
                og = sbp("ogD", [128, NT, 512], F32, gd, (NT * 512, 512))
                nqs = [sbp(f"nqs{i}", [128, 512], F32, gd) for i in range(2)]
                nks = [sbp(f"nks{i}", [128, 512], F32, gd) for i in range(2)]
                nvs = [sbp(f"nvs{i}", [128, 512], F32, gd) for i in range(2)]
                tr.op("pool", lambda e: e.memset(VD[:, :, :, 64:65], 1.0), writes=["VDones"])

                def hD1(ti, ps):
                    nq = nqs[ti % 2]
                    evac(nq[:], ps[:, 0:512])
                    transpose_into(lambda c: QTD[:, c, ti * 128:(ti + 1) * 128], nq, 4)

                def kD(nk, col0):
                    transpose_into(lambda c: KTD[:, c, col0:col0 + 128], nk, 4)

                def vD(nv, kb):
                    tr.op("dve", lambda e: e.tensor_copy(VD[:, kb, :, 0:64], nv[:].rearrange("p (h d) -> p h d", h=8)),
                          reads=[nv, "VDones"], writes=[VD[:, kb, 0, :]])

                def hD2(ti, ps):
                    nk = nks[ti % 2]
                    evac(nk[:], ps[:, 0:512])
                    if path == 0:
                        dma("sp", O["o_nk"][seq, l, ti * 128:(ti + 1) * 128, :], nk[:])
                    kD(nk, L + ti * 128)

                def hD3(ti, ps):
                    nv = nvs[ti % 2]
                    evac(nv[:], ps[:, 0:512])
                    if path == 0:
                        dma("sp", O["o_nv"][seq, l, ti * 128:(ti + 1) * 128, :], nv[:])
                    vD(nv, LB + ti)

                in_proj(5, hD1)
                in_proj(6, hD2)
                in_proj(7, hD3)
                for b in range(LB):
                    nk, nv = nks[b % 2], nvs[b % 2]
                    dma("sp", nk[:], I["c_nk"][l, b * 128:(b + 1) * 128, :])
                    dma("sp", nv[:], I["c_nv"][l, b * 128:(b + 1) * 128, :])
                    kD(nk, b * 128)
                    vD(nv, b)
                rst = lambda r: min(max(r - 4, 0), 8)
                for qt in range(NT):
                    if path == 0:
                        kbl = list(range(KB))
                    else:
                        k_lo = rst(2 * qt) // 2
                        k_hi = (rst(2 * qt + 1) + 7) // 2
                        kbl = list(range(LB)) + [LB + t for t in range(k_lo, k_hi + 1)]
                    for h in range(8):
                        j, e2 = h // 2, h % 2
                        qs = slice(qt * 128, (qt + 1) * 128)

                        def mfn(kb, Es, qt=qt, h=h):
                            if path == 0 or kb < LB:
                                return
                            kt = kb - LB
                            for kr in range(2):
                                for qr in range(2):
                                    rk, rq = 2 * kt + kr, 2 * qt + qr
                                    quad = Es[kr * 64:(kr + 1) * 64, qr * 64:(qr + 1) * 64]
                                    if rst(rq) <= rk <= rst(rq) + 7:
                                        dr = rk - rq + 7
                                        tt("pool" if kr == 0 else "dve", quad, quad, Tm[kr * 64:(kr + 1) * 64, h * 15 + dr, :], ALU.mult)
                                    else:
                                        tr.op("pool", lambda e, quad=quad: e.memset(quad, 0.0), writes=[quad])

                        attend([QTD[e2 * 64:(e2 + 1) * 64, j, qs]],
                               lambda kb, j=j, e2=e2: [KTD[e2 * 64:(e2 + 1) * 64, j, kb * 128:(kb + 1) * 128]],
                               lambda kb, h=h: VD[:, kb, h, 0:65],
                               kbl, 64, 0.125, og[:, qt, h * 64:(h + 1) * 64], mask_fn=mfn)
                for qt in range(NT):
                    transpose_into(lambda c: oT[:, 12 + c, qt * 128:(qt + 1) * 128], og[:, qt, :], 4)
                    if dbg:
                        dma("sp", DBG["o"][tok0 + qt * 128: tok0 + (qt + 1) * 128, 1536:2048], og[:, qt, :])
                tr.barrier()

        colok = sb("colok", [128, 64], F32)
        YW = 64 * 97
        ysc = nc.dram_tensor("ysc", [120, YW], F32, kind="Internal").ap()
        if do_latent:
            dma("sp", colok[:], C["na_colok"])

        def build_tm(l, stack):
            Tm = sbp("Tm", [128, 120, 64], BF, stack)
            with contextlib.ExitStack() as ph:
                rows = sbp("rpbrows", [120, 31], F32, ph)
                rr = sbp("rpbrev", [120, 31], F32, ph)
                zt = sbp("zt", [120, YW], F32, ph)
                traw = sbp("traw", [128, 120, 64], F32, ph)
                dma("sp", rows[:], I["na_rpb"][l].rearrange("h a b -> (h a) b"))
                vcopy("dve", rr[:], rows[:, ::-1])
                tr.op("pool", lambda e: e.memset(zt[:], 0.0), writes=[zt])
                dma("sp", ysc, zt[:])
                with nc.allow_non_contiguous_dma(reason="toeplitz skew"):
                    dma("sp", ysc.rearrange("a (k c) -> a k c", c=97)[:, :, 0:31],
                        rr[:].unsqueeze(1).to_broadcast([120, 64, 31]))
                    src = ysc[:, 0:64 * 96].rearrange("a (k c) -> k a c", c=96)[:, :, 15:79]
                    dma("sp", traw[0:64, :, :], src)
                    dma("sp", traw[64:128, :, :], src)
                actf(traw[:], traw[:], AF.Exp)
                tt("dve", Tm[:], traw[:], colok[:].unsqueeze(1).to_broadcast([128, 120, 64]), ALU.mult)
                tr.barrier()
            return Tm

        def layer_norm_tile(x, out, lng, lnb, scratch):
            st = scratch["st"]
            for c in range(4):
                tr.op("dve", lambda e, c=c: e.bn_stats(st[:, c, :], x[:, c * 512:(c + 1) * 512]), reads=[x], writes=[st])
            mv = scratch["mv"]
            tr.op("dve", lambda e: e.bn_aggr(mv[:, 0:2], st[:].rearrange("p a b -> p (a b)")), reads=[st], writes=[mv])
            actf(mv[:, 2:3], mv[:, 1:2], AF.Sqrt, bias=epsc[:, 1:2])
            tr.op("dve", lambda e: e.reciprocal(mv[:, 3:4], mv[:, 2:3]), reads=[mv], writes=[mv])
            ts("dve", out[:], x[:], mv[:, 0:1], mv[:, 3:4], ALU.subtract, ALU.mult)
            tt("pool", out[:], out[:], lng[:], ALU.mult)
            tt("dve", out[:], out[:], lnb[:], ALU.add)

        def wpass1(l, path, tok0, NT, oT):
            with contextlib.ExitStack() as ph:
                wo = sbp("wo", [128, 16, D], BF, ph, (16 * D, D * 4))
                for c in range(4):
                    dma("pool", wo[:, 4 * c:4 * c + 4, :], I["w_out"][l, c * 512:(c + 1) * 512, :].rearrange("(kc p) n -> p kc n", p=128))
                g1b = sbp("g1b", [128, D], F32, ph)
                dma("sp", g1b[:], modd[l, path, 2 * D:3 * D].partition_broadcast(128))
                xt = [sbp(f"w1x{i}", [128, D], F32, ph) for i in range(2)]
                x1p = [sbp(f"w1p{i}", [128, D], F32, ph) for i in range(2)]
                tmp = sbp("w1t", [128, 512], F32, ph)
                for ti in range(NT):
                    x, xp_ = xt[ti % 2], x1p[ti % 2]
                    rows = slice(tok0 + ti * 128, tok0 + (ti + 1) * 128)
                    dma("sp", x[:], xres[rows, :])
                    for cc in range(4):
                        cs = slice(cc * 512, (cc + 1) * 512)
                        ps = psg.next()
                        for mc in range(16):
                            mm(ps[:], oT[:, mc, ti * 128:(ti + 1) * 128], wo[:, mc, cs], mc == 0, mc == 15)
                        tt("dve", tmp[:], ps[:], g1b[:, cs], ALU.mult)
                        stt(xp_[:, cs], x[:, cs], ALPHA, tmp[:], ALU.mult, ALU.add)
                    dma("sp", xres[rows, :], xp_[:])

        def wpass2(l, path, tok0, NT):
            with contextlib.ExitStack() as ph:
                lng = sbp("lng", [128, D], F32, ph)
                lnb = sbp("lnb", [128, D], F32, ph)
                sc2b = sbp("sc2b", [128, D], F32, ph)
                sh2b = sbp("sh2b", [128, D], F32, ph)
                dma("sp", lng[:], I["ln1_g"][l].partition_broadcast(128))
                dma("sp", lnb[:], I["ln1_b"][l].partition_broadcast(128))
                dma("sp", sh2b[:], modd[l, path, 3 * D:4 * D].partition_broadcast(128))
                dma("sp", sc2b[:], modd[l, path, 4 * D:5 * D].partition_broadcast(128))
                ts("pool", sc2b[:], sc2b[:], 1.0, None, ALU.add)
                scr = {"st": sbp("lnst", [128, 4, 6], F32, ph), "mv": sbp("lnmv", [128, 4], F32, ph)}
                xt = [sbp(f"w2x{i}", [128, D], F32, ph) for i in range(2)]
                hq = [sbp(f"w2h{i}", [128, D], F32, ph) for i in range(2)]
                for ti in range(NT):
                    x, h = xt[ti % 2], hq[ti % 2]
                    rows = slice(tok0 + ti * 128, tok0 + (ti + 1) * 128)
                    dma("sp", x[:], xres[rows, :])
                    layer_norm_tile(x, x, lng, lnb, scr)
                    dma("sp", xres[rows, :], x[:])
                    if dbg:
                        dma("sp", DBG["x1"][rows, :], x[:])
                    tt("pool", h[:], x[:], sc2b[:], ALU.mult)
                    tt("dve", h[:], h[:], sh2b[:], ALU.add)
                    dma("sp", hqres[rows, :], h[:])

        def peer(l, path, tok0, G, last):
            GT = G * 128
            with contextlib.ExitStack() as ph:
                eidx = sbp("eidx", [128, G, 128], I32, ph)
                gwt = sbp("gwt", [128, G, 128], F32, ph)
                with contextlib.ExitStack() as p1:
                    hqT = sbp("hqT", [128, 16, GT], F32, p1, (GT, 128))
                    qT = sbp("qT", [128, 16, GT], F32, p1)
                    wq = [sbp(f"wq{i}", [128, 16, 128], F32, p1) for i in range(2)]
                    hq = [sbp(f"phq{i}", [128, D], F32, p1) for i in range(2)]
                    ssb = sbp("ssb", [128, 16, 128], F32, p1, (2048, 128))
                    v16 = sbp("v16", [128, 16, 16], F32, p1)
                    i16 = sbp("i16", [128, 16, 16], U32, p1)
                    i16f = sbp("i16f", [128, 16, 16], F32, p1)
                    cand = sbp("cand", [128, 8, 256], F32, p1)
                    eix = sbp("eix", [128, 8, 256], F32, p1)
                    scr = sbp("pscr", [128, 256], F32, p1)
                    sc16 = sbp("sc16", [128, 8, 16], F32, p1)
                    ci16 = sbp("ci16", [128, 8, 16], U32, p1)
                    cif = sbp("cif", [128, 8, 16], F32, p1)
                    ef = sbp("ef", [128, 128], F32, p1)
                    gs = sbp("gs", [128, 24], F32, p1)
                    keysT = sbp("keysT", [128, 16, 128], F32, p1)
                    dma("sp", ssb[:], I["peer_keys"][l].rearrange("c n d -> n c d"))
                    for q4 in range(4):
                        ps = psg.next()
                        for j in range(4):
                            tp(ps[:, j * 128:(j + 1) * 128], ssb[:, q4 * 4 + j, :], ident[:])
                        evac(keysT[:, q4 * 4:(q4 + 1) * 4, :].rearrange("p a b -> p (a b)"), ps[:])
                    for t in range(G):
                        h = hq[t % 2]
                        dma("sp", h[:], hqres[tok0 + t * 128: tok0 + (t + 1) * 128, :])
                        for q4 in range(4):
                            ps = psg.next()
                            for j in range(4):
                                tp(ps[:, j * 128:(j + 1) * 128], h[:, (q4 * 4 + j) * 128:(q4 * 4 + j + 1) * 128], ident[:])
                            for j in range(4):
                                evac(hqT[:, q4 * 4 + j, t * 128:(t + 1) * 128], ps[:, j * 128:(j + 1) * 128])
                    for c in range(16):
                        w = wq[c % 2]
                        dma("sp", w[:], I["peer_wq"][l, :, c * 128:(c + 1) * 128].rearrange("(kc p) n -> p kc n", p=128))
                        ps = psg.next()
                        for kc in range(16):
                            mm(ps[:, 0:GT], w[:, kc, :], hqT[:, kc, :], kc == 0, kc == 15)
                        evac(qT[:, c, :], ps[:, 0:GT])
                    v16v = v16[:].rearrange("p (h two) k -> p h two k", two=2)
                    i16fv = i16f[:].rearrange("p (h two) k -> p h two k", two=2)
                    for t in range(G):
                        for q4 in range(4):
                            ps = psg.next()
                            for j in range(4):
                                c = q4 * 4 + j
                                mm(ps[:, j * 128:(j + 1) * 128], qT[:, c, t * 128:(t + 1) * 128], keysT[:, c, :], True, True)
                            evac(ssb[:, q4 * 4:(q4 + 1) * 4, :].rearrange("p a b -> p (a b)"), ps[:])
                        for c in range(16):
                            sv = ssb[:, c, :]
                            tr.op("dve", lambda e, c=c, sv=sv: e.max(out=v16[:, c, 0:8], in_=sv), reads=[sv], writes=[v16])
                            tr.op("dve", lambda e, c=c, sv=sv: e.max_index(out=i16[:, c, 0:8], in_max=v16[:, c, 0:8], in_values=sv), reads=[sv, v16], writes=[i16])
                            tr.op("dve", lambda e, c=c, sv=sv: e.match_replace(out=scr[:, 0:128], in_to_replace=v16[:, c, 0:8], in_values=sv, imm_value=NEG), reads=[sv, v16], writes=[scr])
                            tr.op("dve", lambda e, c=c: e.max(out=v16[:, c, 8:16], in_=scr[:, 0:128]), reads=[scr], writes=[v16])
                            tr.op("dve", lambda e, c=c: e.max_index(out=i16[:, c, 8:16], in_max=v16[:, c, 8:16], in_values=scr[:, 0:128]), reads=[scr, v16], writes=[i16])
                        vcopy("dve", i16f[:], i16[:])
                        ts("dve", i16fv[:, :, 0, :], i16fv[:, :, 0, :], 128.0, None, ALU.mult)
                        c4 = cand[:].rearrange("p h (a b) -> p h a b", a=16)
                        e4 = eix[:].rearrange("p h (a b) -> p h a b", a=16)
                        tt("dve", c4, v16v[:, :, 0, :].unsqueeze(3).to_broadcast([128, 8, 16, 16]),
                           v16v[:, :, 1, :].unsqueeze(2).to_broadcast([128, 8, 16, 16]), ALU.add)
                        tt("pool", e4, i16fv[:, :, 0, :].unsqueeze(3).to_broadcast([128, 8, 16, 16]),
                           i16fv[:, :, 1, :].unsqueeze(2).to_broadcast([128, 8, 16, 16]), ALU.add)
                        for hh in range(8):
                            cv = cand[:, hh, :]
                            tr.op("dve", lambda e, hh=hh, cv=cv: e.max(out=sc16[:, hh, 0:8], in_=cv), reads=[cv], writes=[sc16])
                            tr.op("dve", lambda e, hh=hh, cv=cv: e.max_index(out=ci16[:, hh, 0:8], in_max=sc16[:, hh, 0:8], in_values=cv), reads=[cv, sc16], writes=[ci16])
                            tr.op("dve", lambda e, hh=hh, cv=cv: e.match_replace(out=scr[:], in_to_replace=sc16[:, hh, 0:8], in_values=cv, imm_value=NEG), reads=[cv, sc16], writes=[scr])
                            tr.op("dve", lambda e, hh=hh: e.max(out=sc16[:, hh, 8:16], in_=scr[:]), reads=[scr], writes=[sc16])
                            tr.op("dve", lambda e, hh=hh: e.max_index(out=ci16[:, hh, 8:16], in_max=sc16[:, hh, 8:16], in_values=scr[:]), reads=[scr, sc16], writes=[ci16])
                        vcopy("dve", cif[:], ci16[:])
                        for hh in range(8):
                            for k in range(16):
                                tr.op("dve", lambda e, hh=hh, k=k: e.scalar_tensor_tensor(
                                    scr[:], iota256[:], cif[:, hh, k:k + 1], eix[:, hh, :],
                                    op0=ALU.is_equal, op1=ALU.mult, accum_out=ef[:, hh * 16 + k:hh * 16 + k + 1]),
                                    reads=[iota256, cif, eix], writes=[scr, ef])
                        ts("dve", ef[:], ef[:], float(l * 16384), None, ALU.add)
                        vcopy("dve", eidx[:, t, :], ef[:])
                        ts("dve", gs[:, 0:8], sc16[:, :, 0], -1.0, None, ALU.mult)
                        for hh in range(8):
                            actf(gwt[:, t, hh * 16:(hh + 1) * 16], sc16[:, hh, :], AF.Exp, bias=gs[:, hh:hh + 1], accum=gs[:, 8 + hh:9 + hh])
                        tr.op("dve", lambda e: e.reciprocal(gs[:, 16:24], gs[:, 8:16]), reads=[gs], writes=[gs])
                        gv_ = gwt[:, t, :].rearrange("p (h k) -> p h k", h=8)
                        tt("dve", gv_, gv_, gs[:, 16:24].unsqueeze(2).to_broadcast([128, 8, 16]), ALU.mult)
                tr.barrier()
                with contextlib.ExitStack() as p2:
                    NB = 8
                    ub = [sbp(f"ub{i}", [128, D], BF, p2) for i in range(NB)]
                    vb = [sbp(f"vb{i}", [128, D], BF, p2) for i in range(NB)]
                    hqt = sbp("hqt", [128, D], F32, p2)
                    yt = sbp("yt", [128, D], F32, p2)
                    x1 = sbp("x1t", [128, D], F32, p2)
                    junk = sbp("junk", [128, D], BF, p2)
                    dgs = [sbp(f"dg{i}", [128, 128], BF, p2) for i in range(4)]
                    av = sbp("av", [128, 128], F32, p2)
                    wg = sbp("wg", [128, 128], F32, p2)
                    g2b = sbp("g2b", [128, D], F32, p2)
                    lng = sbp("lng2", [128, D], F32, p2)
                    lnb = sbp("lnb2", [128, D], F32, p2)
                    scr2 = {"st": sbp("lnst2", [128, 4, 6], F32, p2), "mv": sbp("lnmv2", [128, 4], F32, p2)}
                    hqP = psall[:, 0:2048]
                    yP = psall[:, 2048:4096]
                    dma("sp", g2b[:], modd[l, path, 5 * D:6 * D].partition_broadcast(128))
                    dma("sp", lng[:], I["ln2_g"][l].partition_broadcast(128))
                    dma("sp", lnb[:], I["ln2_b"][l].partition_broadcast(128))
                    for t in range(G):
                        rows = slice(tok0 + t * 128, tok0 + (t + 1) * 128)
                        dma("sp", hqt[:], hqres[rows, :])
                        dma("sp", x1[:], xres[rows, :])
                        vcopy("act", hqP, hqt[:])
                        for p in range(128):
                            u = ub[p % NB]
                            tr.dma("pool", lambda e, u=u, p=p: e.indirect_dma_start(
                                out=u[:], out_offset=None, in_=ub16,
                                in_offset=bass.IndirectOffsetOnAxis(ap=eidx[:, t, p:p + 1], axis=0)),
                                reads=[eidx, ub16], writes=[u])
                            tr.op("dve", lambda e, u=u, p=p: e.scalar_tensor_tensor(
                                junk[:], u[:], 1.0, hqP, op0=ALU.mult, op1=ALU.mult, accum_out=av[:, p:p + 1]),
                                reads=[u, hqP], writes=[junk, av])
                        actf(wg[:], av[:], AF.Gelu_apprx_tanh)
                        tt("dve", wg[:], wg[:], gwt[:, t, :], ALU.mult)
                        for p in range(128):
                            v = vb[p % NB]
                            tr.dma("pool", lambda e, v=v, p=p: e.indirect_dma_start(
                                out=v[:], out_offset=None, in_=vb16,
                                in_offset=bass.IndirectOffsetOnAxis(ap=eidx[:, t, p:p + 1], axis=0)),
                                reads=[eidx, vb16], writes=[v])
                            dg = dgs[p % 4]
                            actf(dg[:], identb[:], AF.Copy, scale=wg[:, p:p + 1])
                            for cc in range(4):
                                mm(yP[:, cc * 512:(cc + 1) * 512], dg[:], v[:, cc * 512:(cc + 1) * 512], p == 0, p == 127)
                        tt("dve", yt[:], yP, g2b[:], ALU.mult)
                        stt(yt[:], x1[:], ALPHA, yt[:], ALU.mult, ALU.add)
                        layer_norm_tile(yt, yt, lng, lnb, scr2)
                        dma("sp", xres[rows, :], yt[:])
            tr.barrier()

        units = [(0, s * SEQ, 2, s) for s in range(NSEQ)]
        if do_latent:
            units.append((1, NSEQ * SEQ, 8, None))
        for l in range(n_layers):
            load_layer_consts(l)
            for (path, tok0, NT, seq) in units:
                with contextlib.ExitStack() as pu:
                    oT = sbp("oT", [128, 16, NT * 128], BF, pu, (NT * 128, 128))
                    with contextlib.ExitStack() as pm:
                        mixer(l, path, tok0, NT, seq, oT, pm)
                    tr.barrier()
                    wpass1(l, path, tok0, NT, oT)
                tr.barrier()
                wpass2(l, path, tok0, NT)
                tr.barrier()
                if do_peer:
                    for g0 in range(0, NT, 4):
                        peer(l, path, tok0 + g0 * 128, min(4, NT - g0), l == n_layers - 1)
        tr.barrier()
        dma("sp", O["yp"], xres[0:NSEQ * SEQ, :])
        dma("sp", O["ys"], xres[NSEQ * SEQ:NTOK, :])
        tr.final_wait("sp")
        stats = {"ninstr": tr.ninstr, "nsem": tr.nsem}
    return nc, stats


_CONSTS = None


def kernel(**inputs):
    global _CONSTS
    if _CONSTS is None:
        _CONSTS = host_constants()
    f = lambda a: np.ascontiguousarray(np.asarray(a, dtype=np.float32))
    x_prompt = f(inputs["x_prompt"])
    x_sample = f(inputs["x_sample"])
    shared = {
        "w_ada": f(inputs["w_ada"]), "b_ada": f(inputs["b_ada"]), "w_in": f(inputs["w_in"]),
        "g_q": f(inputs["g_q"]), "g_kv": f(inputs["g_kv"]), "w_uq": f(inputs["w_uq"]), "w_uk": f(inputs["w_uk"]),
        "w_uv": f(inputs["w_uv"]), "gqa_sink": f(inputs["gqa_sink"]), "w_pool": f(inputs["w_pool"]),
        "pool_scale": f(inputs["pool_scale"]), "na_rpb": f(inputs["na_rpb"]), "w_out": f(inputs["w_out"]),
        "ln1_g": f(inputs["ln1_g"]), "ln1_b": f(inputs["ln1_b"]), "peer_wq": f(inputs["peer_wq"]),
        "peer_keys": f(inputs["peer_keys"]).reshape(DEPTH, 16, 128, 128),
        "peer_u": f(inputs["peer_u"]), "peer_v": f(inputs["peer_v"]),
        "ln2_g": f(inputs["ln2_g"]), "ln2_b": f(inputs["ln2_b"]),
    }
    shared.update(_CONSTS)
    in_maps = []
    for c in range(NCORES):
        s = c // 4
        m = dict(shared)
        m["xp"] = np.ascontiguousarray(x_prompt[c * NSEQ:(c + 1) * NSEQ].reshape(NSEQ * SEQ, D))
        m["xs"] = np.ascontiguousarray(x_sample[s])
        m["c_ckv"] = f(inputs["cache_mla_ckv"][s])
        m["c_kr"] = f(inputs["cache_mla_krope"][s])
        m["c_gk"] = f(inputs["cache_gqa_k"][s]).reshape(DEPTH, PAST, 128)
        m["c_gv"] = f(inputs["cache_gqa_v"][s]).reshape(DEPTH, PAST, 128)
        m["c_nk"] = f(inputs["cache_na_k"][s]).reshape(DEPTH, PAST, 512)
        m["c_nv"] = f(inputs["cache_na_v"][s]).reshape(DEPTH, PAST, 512)
        m["cond"] = np.ascontiguousarray(np.stack([f(inputs["c_ctx"]), f(inputs["c"])[s]], 0))
        in_maps.append(m)
    nc, _ = build_program(KOPTS)
    res = run_bass_kernel_spmd(nc, in_maps, core_ids=list(range(NCORES)))
    R = res.results
    B = NCORES * NSEQ
    y_prompt = np.concatenate([R[c]["yp"].reshape(NSEQ, SEQ, D) for c in range(NCORES)], 0)
    y_sample = np.stack([R[0]["ys"], R[4]["ys"]], 0)
    cat = lambda k, shp: np.concatenate([R[c][k] for c in range(NCORES)], 0).reshape((B, DEPTH, SEQ) + shp)
    outs = (y_prompt.astype(np.float32), y_sample.astype(np.float32),
            cat("o_ckv", (256,)), cat("o_kr", (64,)), cat("o_gk", (2, 64)), cat("o_gv", (2, 64)),
            cat("o_nk", (8, 64)), cat("o_nv", (8, 64)))
    kernel.last_results = R
    return outs


KOPTS = {}
```
